# Optimizing a Trainium2 kernel written in Bass

```python
import math
import jax, jax.numpy as jnp
from jax import lax
import numpy as np

D_MODEL = 1024
BATCH = 2
SEQ = 8192
DEPTH = 2

N_A_LAYERS = DEPTH // 2
N_B_LAYERS = DEPTH - N_A_LAYERS

GDN_HEADS = 8
GDN_HEAD_DIM = 128
GDN_WIDTH = GDN_HEADS * GDN_HEAD_DIM
CONV_WIDTH = 4
CHUNK = 64
GDN_IN_COLS = 4 * GDN_WIDTH + 2 * GDN_HEADS

SB_HEADS = 8
SB_HEAD_DIM = 128
SB_WIDTH = SB_HEADS * SB_HEAD_DIM
Q_BLOCK = 128

D_FF = 4 * D_MODEL

EPS = 1e-6

kernel_name = "yoco_gdn_stickbreaking_hybrid"


def rms_norm(x, gain):
    xf = x.astype(jnp.float32)
    y = xf * lax.rsqrt(jnp.mean(xf * xf, axis=-1, keepdims=True) + EPS)
    return (y * gain.astype(jnp.float32)).astype(x.dtype)


def l2_norm(x):
    xf = x.astype(jnp.float32)
    return xf * lax.rsqrt(jnp.sum(xf * xf, axis=-1, keepdims=True) + EPS)


def causal_dwconv(x, w):
    k_w = w.shape[0]
    t_len = x.shape[1]
    xp = jnp.pad(x, ((0, 0), (k_w - 1, 0), (0, 0)))
    return sum(xp[:, i:i + t_len, :] * w[i] for i in range(k_w))


def gated_delta_rule_chunked(q, k, v, beta, g):
    b, t_len, h, dk = q.shape
    dv = v.shape[-1]
    n = t_len // CHUNK

    def chunks(t):
        t = jnp.moveaxis(t, 2, 1)
        return t.reshape((b, h, n, CHUNK) + t.shape[3:])

    q, k, v, beta, g = (chunks(t) for t in (q, k, v, beta, g))
    gc = jnp.cumsum(g, axis=-1)
    idx = jnp.arange(CHUNK)
    incl = idx[:, None] >= idx[None, :]
    strict = idx[:, None] > idx[None, :]
    diff = gc[..., :, None] - gc[..., None, :]
    decay = jnp.where(incl, jnp.exp(jnp.where(incl, diff, 0.0)), 0.0)

    kb = k * beta[..., None]
    lower = jnp.where(strict, jnp.einsum('bhnid,bhnjd->bhnij', kb, k) * decay, 0.0)
    eye = jnp.eye(CHUNK, dtype=jnp.float32)
    t_mat = lax.linalg.triangular_solve(eye + lower, jnp.broadcast_to(eye, lower.shape),
                                        left_side=True, lower=True)
    w = t_mat @ (kb * jnp.exp(gc)[..., None])
    u = t_mat @ (v * beta[..., None])
    attn = jnp.einsum('bhnid,bhnjd->bhnij', q, k) * decay
    qg = q * jnp.exp(gc)[..., None]
    kg = k * jnp.exp(gc[..., -1:] - gc)[..., None]
    g_last = jnp.exp(gc[..., -1])

    xs = tuple(jnp.moveaxis(t, 2, 0) for t in (qg, kg, w, u, attn, g_last))

    def step(state, inp):
        qg_c, kg_c, w_c, u_c, attn_c, gl_c = inp
        v_new = u_c - w_c @ state
        o_c = qg_c @ state + attn_c @ v_new
        state = state * gl_c[..., None, None] + jnp.einsum('bhck,bhcv->bhkv', kg_c, v_new)
        return state, o_c

    s0 = jnp.zeros((b, h, dk, dv), jnp.float32)
    _, o = lax.scan(step, s0, xs)
    o = jnp.moveaxis(o, 0, 2).reshape(b, h, t_len, dv)
    return jnp.moveaxis(o, 1, 2)


def gated_deltanet(h, w_in, conv_w, a_log, dt_bias, out_gain, w_out):
    b, t_len, _ = h.shape
    proj = h @ w_in
    qkv, gate, b_raw, a_raw = jnp.split(
        proj, [3 * GDN_WIDTH, 4 * GDN_WIDTH, 4 * GDN_WIDTH + GDN_HEADS], axis=-1)
    qkv = jax.nn.silu(causal_dwconv(qkv, conv_w))
    q, k, v = jnp.split(qkv, 3, axis=-1)

    def heads(t):
        return t.reshape(b, t_len, GDN_HEADS, GDN_HEAD_DIM).astype(jnp.float32)

    q = l2_norm(heads(q)) * (GDN_HEAD_DIM ** -0.5)
    k = l2_norm(heads(k))
    v = heads(v)
    beta = jax.nn.sigmoid(b_raw.astype(jnp.float32))
    g = -jnp.exp(a_log.astype(jnp.float32)) * jax.nn.softplus(
        a_raw.astype(jnp.float32) + dt_bias.astype(jnp.float32))
    o = gated_delta_rule_chunked(q, k, v, beta, g)
    o = rms_norm(o, out_gain) * jax.nn.silu(heads(gate))
    return o.reshape(b, t_len, GDN_WIDTH).astype(h.dtype) @ w_out


def stick_breaking_attention(q, k, v):
    b, h, t_len, d = q.shape
    nb = t_len // Q_BLOCK
    qb = jnp.moveaxis(q.reshape(b, h, nb, Q_BLOCK, d), 2, 0)
    key_pos = jnp.arange(t_len)
    scale = d ** -0.5

    def block(args):
        q_blk, i = args
        z = jnp.einsum('bhqd,bhkd->bhqk', q_blk, k).astype(jnp.float32) * scale
        q_pos = i * Q_BLOCK + jnp.arange(Q_BLOCK)
        before = key_pos[None, :] < q_pos[:, None]
        log_beta = jax.nn.log_sigmoid(z)
        log_1m = jnp.where(before, jax.nn.log_sigmoid(-z), 0.0)
        tail = lax.cumsum(log_1m, axis=3, reverse=True) - log_1m
        a = jnp.where(before, jnp.exp(log_beta + tail), 0.0)
        return jnp.einsum('bhqk,bhkd->bhqd', a.astype(v.dtype), v)

    o = lax.map(block, (qb, jnp.arange(nb)))
    return jnp.moveaxis(o, 0, 2).reshape(b, h, t_len, d)


def shared_kv(x, kv_gain, w_kv):
    b, t_len, _ = x.shape
    kv = rms_norm(x, kv_gain) @ w_kv
    k, v = jnp.split(kv, 2, axis=-1)
    k = k.reshape(b, t_len, SB_HEADS, SB_HEAD_DIM).transpose(0, 2, 1, 3)
    v = v.reshape(b, t_len, SB_HEADS, SB_HEAD_DIM).transpose(0, 2, 1, 3)
    return k, v


def stick_breaking_mixer(h, w_q, w_o, k_sh, v_sh):
    b, t_len, _ = h.shape
    q = (h @ w_q).reshape(b, t_len, SB_HEADS, SB_HEAD_DIM).transpose(0, 2, 1, 3)
    o = stick_breaking_attention(q, k_sh, v_sh)
    return o.transpose(0, 2, 1, 3).reshape(b, t_len, SB_WIDTH) @ w_o


def squared_relu_mlp(h, w_up, w_down):
    return jnp.square(jax.nn.relu(h @ w_up)) @ w_down


def setup_inputs(seed: int = 0) -> dict:
    key = jax.random.key(seed)
    ks = jax.random.split(key, 20)
    f32 = jnp.float32

    def nrm(k, shape, fan_in):
        return jax.random.normal(k, shape, f32) * (fan_in ** -0.5)

    def gain(k, shape):
        return 1.0 + 0.05 * jax.random.normal(k, shape, f32)

    x = jax.random.normal(ks[0], (BATCH, SEQ, D_MODEL), f32)
    dt = jnp.exp(jax.random.uniform(ks[10], (N_A_LAYERS, GDN_HEADS), f32,
                                    minval=math.log(1e-3), maxval=math.log(1e-1)))
    dt_bias = dt + jnp.log(-jnp.expm1(-dt))
    a_log = jnp.log(jax.random.uniform(ks[11], (N_A_LAYERS, GDN_HEADS), f32,
                                       minval=1.0, maxval=16.0))
    return {
        "x": x,
        "mix_pre_gain": gain(ks[1], (DEPTH, D_MODEL)),
        "mix_post_gain": gain(ks[2], (DEPTH, D_MODEL)),
        "mlp_pre_gain": gain(ks[3], (DEPTH, D_MODEL)),
        "mlp_post_gain": gain(ks[4], (DEPTH, D_MODEL)),
        "mlp_w_up": nrm(ks[5], (DEPTH, D_MODEL, D_FF), D_MODEL),
        "mlp_w_down": nrm(ks[6], (DEPTH, D_FF, D_MODEL), D_FF),
        "gdn_w_in": nrm(ks[7], (N_A_LAYERS, D_MODEL, GDN_IN_COLS), D_MODEL),
        "gdn_conv_w": nrm(ks[8], (N_A_LAYERS, CONV_WIDTH, 3 * GDN_WIDTH), CONV_WIDTH),
        "gdn_a_log": a_log,
        "gdn_dt_bias": dt_bias,
        "gdn_out_gain": gain(ks[12], (N_A_LAYERS, GDN_HEAD_DIM)),
        "gdn_w_out": nrm(ks[13], (N_A_LAYERS, GDN_WIDTH, D_MODEL), GDN_WIDTH),
        "kv_gain": gain(ks[14], (D_MODEL,)),
        "w_kv": nrm(ks[15], (D_MODEL, 2 * SB_WIDTH), D_MODEL),
        "sb_w_q": nrm(ks[16], (N_B_LAYERS, D_MODEL, SB_WIDTH), D_MODEL),
        "sb_w_o": nrm(ks[17], (N_B_LAYERS, SB_WIDTH, D_MODEL), SB_WIDTH),
    }


def reference(x, mix_pre_gain, mix_post_gain, mlp_pre_gain, mlp_post_gain, mlp_w_up, mlp_w_down,
              gdn_w_in, gdn_conv_w, gdn_a_log, gdn_dt_bias, gdn_out_gain, gdn_w_out,
              kv_gain, w_kv, sb_w_q, sb_w_o):
    k_sh = None
    v_sh = None
    for layer in range(DEPTH):
        h = rms_norm(x, mix_pre_gain[layer])
        if layer < N_A_LAYERS:
            a = layer
            mix = gated_deltanet(h, gdn_w_in[a], gdn_conv_w[a], gdn_a_log[a], gdn_dt_bias[a],
                                 gdn_out_gain[a], gdn_w_out[a])
        else:
            if layer == N_A_LAYERS:
                k_sh, v_sh = shared_kv(x, kv_gain, w_kv)
            bl = layer - N_A_LAYERS
            mix = stick_breaking_mixer(h, sb_w_q[bl], sb_w_o[bl], k_sh, v_sh)
        x = x + rms_norm(mix, mix_post_gain[layer])
        h = rms_norm(x, mlp_pre_gain[layer])
        x = x + rms_norm(squared_relu_mlp(h, mlp_w_up[layer], mlp_w_down[layer]), mlp_post_gain[layer])
    return x
```

```python
import numpy as np
from contextlib import ExitStack
import concourse.bass as bass
import concourse.mybir as mybir
from concourse.bass_utils import run_bass_kernel_spmd

F32 = mybir.dt.float32
BF16 = mybir.dt.bfloat16
I32 = mybir.dt.int32
U32 = mybir.dt.uint32
AF = mybir.ActivationFunctionType
ALU = mybir.AluOpType

D = 1024
H = 8
HD = 128
FF = 4096
EPS = 1e-6
GIN = 4 * D + 2 * H


class Res:
    __slots__ = ("name", "writer", "readers", "excl")

    def __init__(self, name="", excl=False):
        self.name = name
        self.writer = None
        self.readers = []
        self.excl = excl


class Tok:
    __slots__ = ("sem", "val", "knows", "eng")

    def __init__(self, sem, val, knows, eng):
        self.sem = sem
        self.val = val
        self.knows = knows
        self.eng = eng


class _Rec:
    def __init__(self):
        self.call = None

    def __getattr__(self, name):
        def f(*a, **k):
            self.call = (name, a, k)
            return self
        return f


def _record(fn):
    if fn is None:
        return None
    r = _Rec()
    fn(r)
    assert r.call is not None
    return r.call


class Eng:
    def __init__(self, name, sem):
        self.name = name
        self.sem = sem
        self.count = 0
        self.known = {}
        self.ops = []


class Sched:
    DMA_NS = 8

    def __init__(self, nc, stack):
        self.nc = nc
        self.engs = {}
        for n in ("pe", "act", "dve", "pool", "sp"):
            sem = stack.enter_context(nc.semaphore("s_" + n))
            self.engs[n] = Eng(n, sem)
        self.dsem = {}
        for q in ("sp", "pool", "act"):
            sems = [stack.enter_context(nc.semaphore("d_%s%d" % (q, i))) for i in range(self.DMA_NS)]
            self.dsem[q] = {"sems": sems, "uses": [0] * self.DMA_NS, "last": [None] * self.DMA_NS, "i": 0}
        self.nwaits = 0

    def _merge(self, known, tok):
        for s, v in tok.knows.items():
            if known.get(s, 0) < v:
                known[s] = v
        if known.get(tok.sem, 0) < tok.val:
            known[tok.sem] = tok.val

    def _deps(self, eng, reads, writes, extra=()):
        e = self.engs[eng]
        deps = list(extra)
        for r in reads:
            if r.writer is not None:
                deps.append((r.writer, True))
        for w in writes:
            if w.writer is not None:
                deps.append((w.writer, False))
            for t in w.readers:
                deps.append((t, False))
        best = {}
        for t, raw in deps:
            if t is None:
                continue
            if t.eng == eng and not raw:
                continue
            if e.known.get(t.sem, 0) >= t.val:
                continue
            k = id(t.sem)
            if k not in best or best[k][1] < t.val:
                best[k] = (t.sem, t.val)
            self._merge(e.known, t)
        waits = list(best.values())
        self.nwaits += len(waits)
        return waits

    def _finish(self, tok, reads, writes):
        for r in reads:
            r.readers.append(tok)
        for w in writes:
            w.writer = tok
            w.readers = []

    @staticmethod
    def _flat(xs):
        out = []
        for x in xs:
            if isinstance(x, (list, tuple)):
                out.extend(Sched._flat(x))
            else:
                out.append(x)
        return out

    def op(self, eng, fn, reads=(), writes=()):
        reads = self._flat(reads)
        writes = self._flat(writes)
        if any(r.excl for r in reads):
            writes = list(writes) + [r for r in reads if r.excl]
            reads = [r for r in reads if not r.excl]
        e = self.engs[eng]
        waits = self._deps(eng, reads, writes)
        e.count += 1
        tok = Tok(e.sem, e.count, dict(e.known), eng)
        e.ops.append((waits, _record(fn), (e.sem, 1)))
        self._finish(tok, reads, writes)
        return tok

    def dma(self, q, fn, reads=(), writes=()):
        reads = self._flat(reads)
        writes = self._flat(writes)
        e = self.engs[q]
        d = self.dsem[q]
        k = d["i"] % self.DMA_NS
        d["i"] += 1
        extra = [(d["last"][k], True)] if d["last"][k] is not None else []
        waits = self._deps(q, reads, writes, extra)
        d["uses"][k] += 1
        tok = Tok(d["sems"][k], 16 * d["uses"][k], dict(e.known), "dma_" + q)
        d["last"][k] = tok
        e.ops.append((waits, _record(fn), (d["sems"][k], 16)))
        self._finish(tok, reads, writes)
        return tok

    def barrier(self):
        toks = []
        for n, e in self.engs.items():
            if e.count:
                toks.append(Tok(e.sem, e.count, {}, n))
        for q, d in self.dsem.items():
            for t in d["last"]:
                if t is not None:
                    toks.append(t)
        for n, e in self.engs.items():
            waits = []
            for t in toks:
                if t.eng == n:
                    continue
                if e.known.get(t.sem, 0) >= t.val:
                    continue
                waits.append((t.sem, t.val))
                e.known[t.sem] = t.val
            if waits:
                e.ops.append((waits, None, None))

    def emit(self):
        nc = self.nc
        handles = {"pe": "tensor", "act": "scalar", "dve": "vector", "pool": "gpsimd", "sp": "sync"}
        with nc.Block() as block:
            for n, attr in handles.items():
                e = self.engs[n]

                def body(h, e=e):
                    for waits, fn, inc in e.ops:
                        for s, v in waits:
                            h.wait_ge(s, v)
                        if fn is not None:
                            getattr(h, fn[0])(*fn[1], **fn[2]).then_inc(inc[0], inc[1])
                getattr(block, attr)(body)


class WRes(list):
    def __init__(self):
        super().__init__()
        self.map = {}
        self.step = 1

    def at(self, kc, col):
        return self.map[(kc, (col // self.step) * self.step)]


class Pool_:
    def __init__(self, nc, stack, name, shape, dtype, n):
        self.tiles = [stack.enter_context(nc.sbuf_tensor("%s%d" % (name, i), shape, dtype)) for i in range(n)]
        self.res = [Res("%s%d" % (name, i)) for i in range(n)]
        self.i = 0

    def get(self):
        k = self.i % len(self.tiles)
        self.i += 1
        return self.tiles[k], self.res[k]


class KB:
    def __init__(self, T, NQB, phases="ABCDE", feed=(), dump=()):
        self.T = T
        self.NQB = NQB
        self.NT = T // 128
        self.phases = phases
        self.feed = set(feed)
        self.dump = set(dump)
        self.nc = bass.Bass("TRN2", target_bir_lowering=False)
        self.chain_dt = F32
        self.a_stage = 9
        self.d_stage = 9

    def din(self, name, shape, dt=F32):
        return self.nc.dram_tensor(name, list(shape), dt, kind="ExternalInput")

    def dscratch(self, name, shape, dt=F32):
        kind = "Internal"
        if name in self.feed:
            kind = "ExternalInput"
        elif name in self.dump:
            kind = "ExternalOutput"
        return self.nc.dram_tensor(name, list(shape), dt, kind=kind)

    def build(self):
        nc = self.nc
        T, NQB, NT = self.T, self.NQB, self.NT
        self.x = self.din("x", [T, D])
        self.gains = self.din("gains", [9, D])
        self.w_up = self.din("mlp_w_up", [2, D, FF])
        self.w_down = self.din("mlp_w_down", [2, FF, D])
        self.w_in = self.din("gdn_w_in", [D, GIN])
        self.w_out = self.din("gdn_w_out", [D, D])
        self.w_kv = self.din("w_kv", [D, 2 * D])
        self.w_q = self.din("sb_w_q", [D, D])
        self.w_o = self.din("sb_w_o", [D, D])
        self.consts = self.din("consts", [128, 8, 128])
        self.convw_l = self.din("convw_l", [128, 24, 4])
        self.small_bc = self.din("small_bc", [128, 2, 8])
        self.ogain_bc = self.din("ogain_bc", [128, 128])
        self.selall_d = self.din("selall", [8, 1024])
        if "dbg_bg" in self.dump:
            self.dbg_bg = self.nc.dram_tensor("dbg_bg", [T, 2, 8], F32, kind="ExternalOutput")
        if "dbg_qkv" in self.dump:
            self.dbg_qkv = self.nc.dram_tensor("dbg_qkv", [4, 128, 8, T], BF16, kind="ExternalOutput")
        if "dbg_o" in self.dump:
            self.dbg_o = self.nc.dram_tensor("dbg_o", [T, D], F32, kind="ExternalOutput")
        self.qsel = self.din("qsel", [128, 4])
        if "dbg_ao" in self.dump:
            self.dbg_ao = self.nc.dram_tensor("dbg_ao", [NQB * 128, D], F32, kind="ExternalOutput")
        self.amask = self.din("amask", [128, 4, 128])
        self.out = nc.dram_tensor("out", [NQB * 128, D], F32, kind="ExternalOutput")
        self.xm = self.dscratch("xm", [T, D])
        self.x1 = self.dscratch("x1", [T, D])
        self.xm2 = self.dscratch("xm2", [NQB * 128, D])
        self.qkvg = self.dscratch("qkvg", [4, 128, H, T], BF16)
        self.scd = self.dscratch("scd", [NT, 128, 12, 8])
        self.gctd = self.dscratch("gctd", [NT, 8, 128])
        self.KT = self.dscratch("KT", [NT, 128, H, 128], BF16)
        self.Vd = self.dscratch("Vd", [NT, 128, H, 128], BF16)
        with ExitStack() as st:
            self.S = S = Sched(nc, st)
            self.psb = [st.enter_context(nc.psum_tensor("psb%d" % i, [128, 512], F32)) for i in range(8)]
            self.psr = [Res("ps%d" % i, excl=True) for i in range(8)]
            self.ident_f = st.enter_context(nc.sbuf_tensor("ident_f", [128, 128], F32))
            self.ident_b = st.enter_context(nc.sbuf_tensor("ident_b", [128, 128], BF16))
            self.r_const = Res("const")
            self.eps_col = st.enter_context(nc.sbuf_tensor("eps_col", [128, 2], F32))
            S.op("pool", lambda h: h.memset(self.eps_col[:], EPS), writes=[self.r_const])
            self.one_col = st.enter_context(nc.sbuf_tensor("one_col", [128, 2], F32))
            S.op("pool", lambda h: h.memset(self.one_col[:], 1.0), writes=[self.r_const])
            S.dma("sp", lambda h: h.dma_start(out=self.ident_f[:], in_=self.consts[:, 0, :]), writes=[self.r_const])
            S.dma("pool", lambda h: h.dma_start(out=self.ident_b[:], in_=self.consts[:, 0, :]), writes=[self.r_const])
            self.final_toks = []
            if "A" in self.phases or "1" in self.phases:
                self.phase_A1()
                S.barrier()
            if "A" in self.phases or "2" in self.phases:
                self.phase_A2()
                S.barrier()
            if "B" in self.phases:
                with ExitStack() as ps:
                    self.phase_mlp(ps, 0, self.xm, self.x1, T)
                S.barrier()
            if "C" in self.phases:
                self.phase_C()
                S.barrier()
            if "D" in self.phases:
                self.phase_D()
                S.barrier()
            if "E" in self.phases:
                with ExitStack() as ps:
                    self.phase_mlp(ps, 1, self.xm2, self.out, NQB * 128)
                S.barrier()
            S.barrier()
            S.emit()
        return nc

    def dbgdump(self, name, ap, res, shape):
        if "dbgx" not in self.dump:
            return
        if not hasattr(self, "_dbgdone"):
            self._dbgdone = {}
        if name in self._dbgdone:
            return
        t = self.nc.dram_tensor("dbg_" + name, list(shape), F32, kind="ExternalOutput")
        self._dbgdone[name] = t
        self.S.dma("pool", lambda h: h.dma_start(out=t.ap(), in_=ap), reads=list(res))

    def load_gain_bc(self, st, name, row):
        t = st.enter_context(self.nc.sbuf_tensor(name, [128, D], F32))
        r = Res(name)
        self.S.dma("sp", lambda h: h.dma_start(out=t[:], in_=self.gains[row, :].partition_broadcast(128)), writes=[r])
        return t, r

    def load_w_kc(self, st, name, w_ap, ncols, q="pool"):
        t = st.enter_context(self.nc.sbuf_tensor(name, [128, 8, ncols], BF16))
        r = WRes()
        wv = w_ap.rearrange("(kc p) n -> p kc n", p=128)
        step = 1024
        r.step = step
        for c0 in range(0, ncols, step):
            for kc in range(8):
                c1 = min(ncols, c0 + step)
                ri = Res(name)
                r.append(ri)
                r.map[(kc, c0)] = ri
                self.S.dma(q, lambda h, kc=kc, c0=c0, c1=c1: h.dma_start(out=t[:, kc, c0:c1], in_=wv[:, kc, c0:c1]), writes=[ri])
        return t, r

    def rms_rstd(self, ss_ap, rstd_ap, r_ss, r_rstd, n):
        S = self.S
        S.op("act", lambda h: h.activation(out=rstd_ap, in_=ss_ap, func=AF.Ln, bias=self.eps_col[:, 0:1], scale=1.0 / n),
             reads=[r_ss, self.r_const], writes=[r_rstd])
        S.op("act", lambda h: h.activation(out=rstd_ap, in_=rstd_ap, func=AF.Exp, scale=-0.5),
             reads=[r_rstd], writes=[r_rstd])

    def norm_to_hT(self, xt, r_xt, gain, r_gain, hT, r_hT, col0, pools):
        S = self.S
        junk, r_junk = pools["junk"].get()
        st_, r_st = pools["stat"].get()
        hb, r_hb = pools["hb"].get()
        S.op("act", lambda h: h.activation(out=junk[:], in_=xt[:], func=AF.Square, accum_out=st_[:, 0:1]),
             reads=[r_xt], writes=[r_junk, r_st])
        self.rms_rstd(st_[:, 0:1], st_[:, 1:2], r_st, r_st, D)
        S.op("dve", lambda h: h.scalar_tensor_tensor(out=hb[:], in0=xt[:], scalar=st_[:, 1:2], in1=gain[:], op0=ALU.mult, op1=ALU.mult),
             reads=[r_xt, r_st, r_gain], writes=[r_hb])
        for half in range(2):
            pi = pools["tp_banks"][self._tpi % len(pools["tp_banks"])]
            self._tpi += 1
            pst = self.psb[pi].bitcast(BF16)
            for j in range(4):
                kc = half * 4 + j
                S.op("pe", lambda h, j=j, kc=kc, pst=pst: h.transpose(pst[:, j * 128:(j + 1) * 128], hb[:, kc * 128:(kc + 1) * 128], self.ident_b[:]),
                     reads=[r_hb, self.r_const], writes=[self.psr[pi]])
            eng = "act" if half == 0 else "dve"
            src = pst[:, 0:512].rearrange("p (a b) -> p a b", a=4)
            dst = hT[:, half * 4:half * 4 + 4, col0:col0 + 128]
            if eng == "act":
                S.op("act", lambda h, src=src, dst=dst: h.copy(dst, src), reads=[self.psr[pi]], writes=[r_hT])
            else:
                S.op("dve", lambda h, src=src, dst=dst: h.tensor_copy(dst, src), reads=[self.psr[pi]], writes=[r_hT])

    def post_norm_residual(self, y, r_y, ss_ap, r_ss, gain, r_gain, xres, r_xres, pools):
        S = self.S
        st_, r_st = pools["stat"].get()
        S.op("dve", lambda h: h.tensor_tensor(out=st_[:, 0:1], in0=ss_ap[:, 0:1], in1=ss_ap[:, 1:2], op=ALU.add), reads=[r_ss], writes=[r_st])
        self.rms_rstd(st_[:, 0:1], st_[:, 1:2], r_st, r_st, D)
        S.op("dve", lambda h: h.scalar_tensor_tensor(out=y[:], in0=y[:], scalar=st_[:, 1:2], in1=gain[:], op0=ALU.mult, op1=ALU.mult),
             reads=[r_y, r_st, r_gain], writes=[r_y])
        S.op("dve", lambda h: h.tensor_tensor(out=y[:], in0=y[:], in1=xres[:], op=ALU.add), reads=[r_y, r_xres], writes=[r_y])

    def phase_mlp(self, st, layer, src, dst, ntok):
        nc, S = self.nc, self.S
        TB = 512 if ntok % 512 == 0 else 256
        nblk = ntok // TB
        self._tpi = 0
        wup, r_wup = self.load_w_kc(st, "wupL%d" % layer, self.w_up[layer], FF)
        wdn = st.enter_context(nc.sbuf_tensor("wdnL%d" % layer, [128, 32, D], BF16))
        r_wdn = []
        wdv = self.w_down[layer].rearrange("(fc p) n -> p fc n", p=128)
        for fc in range(0, 32, 2):
            ri = Res("wdnL%d" % layer)
            r_wdn.append(ri)
            S.dma("pool", lambda h, fc=fc: h.dma_start(out=wdn[:, fc:fc + 2, :], in_=wdv[:, fc:fc + 2, :]), writes=[ri])
        gpre, r_gpre = self.load_gain_bc(st, "gpreL%d" % layer, 4 + layer)
        gpost, r_gpost = self.load_gain_bc(st, "gpostL%d" % layer, 6 + layer)
        pools = {
            "junk": Pool_(nc, st, "junkL%d" % layer, [128, D], BF16, 1),
            "stat": Pool_(nc, st, "statL%d" % layer, [128, 4], F32, 8),
            "hb": Pool_(nc, st, "hbL%d" % layer, [128, D], BF16, 1),
            "tp_banks": [0, 1],
        }
        xt_pool = Pool_(nc, st, "xtL%d" % layer, [128, D], F32, 2)
        y_pool = Pool_(nc, st, "yL%d" % layer, [128, D], F32, 2)
        hT_pool = Pool_(nc, st, "hTL%d" % layer, [128, 8, TB], BF16, 2)
        actT_pool = Pool_(nc, st, "actTL%d" % layer, [128, 32, TB], BF16, 1)
        relu_pool = Pool_(nc, st, "reluL%d" % layer, [128, TB], F32, 1)
        ss_pool = Pool_(nc, st, "ssL%d" % layer, [128, 2], F32, 4)
        up_banks = [2, 3]
        dn_banks = [4, 5, 6, 7]
        upi = 0
        stage = getattr(self, "dbg_stage", 9)
        if stage < 1:
            return
        def do_norm(b):
            hT, r_hT = hT_pool.get()
            for tt in range(TB // 128):
                r0 = b * TB + tt * 128
                xt, r_xt = xt_pool.get()
                S.dma("sp", lambda h, xt=xt, r0=r0: h.dma_start(out=xt[:], in_=src[r0:r0 + 128, :]), writes=[r_xt])
                self.norm_to_hT(xt, r_xt, gpre, r_gpre, hT, r_hT, tt * 128, pools)
            return hT, r_hT

        nxt = do_norm(0)
        for b in range(nblk):
            hT, r_hT = nxt
            if stage < 2:
                continue
            actT, r_actT = actT_pool.get()
            for fc in range(32):
                pi = up_banks[upi % 2]
                upi += 1
                pu = self.psb[pi][:, 0:TB]
                for kc in range(8):
                    S.op("pe", lambda h, pu=pu, kc=kc, fc=fc, hT=hT: h.matmul(pu, wup[:, kc, fc * 128:(fc + 1) * 128], hT[:, kc, :], start=(kc == 0), stop=(kc == 7)),
                         reads=[r_wup.at(kc, fc * 128), r_hT], writes=[self.psr[pi]])
                rl, r_rl = relu_pool.get()
                S.op("act", lambda h, pu=pu, rl=rl: h.activation(out=rl[:], in_=pu, func=AF.Relu), reads=[self.psr[pi]], writes=[r_rl])
                S.op("dve", lambda h, rl=rl, fc=fc, actT=actT: h.tensor_tensor(out=actT[:, fc, :], in0=rl[:], in1=rl[:], op=ALU.mult), reads=[r_rl], writes=[r_actT])
            if b + 1 < nblk:
                nxt = do_norm(b + 1)
            if stage < 3:
                continue
            for tt in range(TB // 128):
                r0 = b * TB + tt * 128
                y, r_y = y_pool.get()
                ss, r_ss = ss_pool.get()
                for half in range(2):
                    pi = dn_banks[(tt * 2 + half) % 4]
                    pd = self.psb[pi]
                    for fc in range(32):
                        S.op("pe", lambda h, pd=pd, fc=fc, tt=tt, half=half, actT=actT: h.matmul(pd[:], actT[:, fc, tt * 128:(tt + 1) * 128], wdn[:, fc, half * 512:(half + 1) * 512], start=(fc == 0), stop=(fc == 31)),
                             reads=[r_actT, r_wdn[fc // 2]], writes=[self.psr[pi]])
                    import os
                    jk, r_jk = pools["junk"].get()
                    if not os.environ.get("NOSQ"):
                        S.op("act", lambda h, pd=pd, jk=jk, ss=ss, half=half: h.activation(out=jk[:, 0:512], in_=pd[:], func=AF.Square, accum_out=ss[:, half:half + 1]),
                             reads=[self.psr[pi]], writes=[r_jk, r_ss])
                    if not os.environ.get("NOCP"):
                        S.op("dve", lambda h, pd=pd, y=y, half=half: h.tensor_copy(y[:, half * 512:(half + 1) * 512], pd[:]), reads=[self.psr[pi]], writes=[r_y])
                if stage < 4:
                    continue
                xr, r_xr = xt_pool.get()
                S.dma("sp", lambda h, xr=xr, r0=r0: h.dma_start(out=xr[:], in_=src[r0:r0 + 128, :]), writes=[r_xr])
                self.post_norm_residual(y, r_y, ss, r_ss, gpost, r_gpost, xr, r_xr, pools)
                if stage < 5:
                    continue
                t = S.dma("sp", lambda h, y=y, r0=r0: h.dma_start(out=dst[r0:r0 + 128, :], in_=y[:]), reads=[r_y])
                self.final_toks.append(t)

    def phase_A(self):
        nc, S = self.nc, self.S
        T = self.T
        TB = 256
        NTT = TB // 128
        nblk = T // TB
        TD = self.chain_dt
        self._tpi = 0
        st = ExitStack()
        with st:
            def sb(name, shape, dt):
                return st.enter_context(nc.sbuf_tensor(name, shape, dt))
            win, r_win = self.load_w_kc(st, "win", self.w_in, GIN)
            wout, r_wout = self.load_w_kc(st, "wout", self.w_out, D)
            gpre, r_gpre = self.load_gain_bc(st, "gpreA", 0)
            gpost, r_gpost = self.load_gain_bc(st, "gpostA", 2)
            r_c = Res("constA")
            cst = sb("cstA", [128, 6, 128], F32)
            S.dma("sp", lambda h: h.dma_start(out=cst[:], in_=self.consts[:, 0:6, :]), writes=[r_c])
            ident_f = cst[:, 0, :]
            UTbd = cst[:, 1, :]
            BDones = cst[:, 2, :]
            negmaskU = cst[:, 3, :]
            posmaskL = cst[:, 4, :]
            ones_b = sb("ones_b", [128, 128], BF16)
            S.op("pool", lambda h: h.memset(ones_b[:], 1.0), writes=[r_c])
            ident4 = sb("ident4", [128, 4, 128], F32)
            for hh in range(4):
                S.dma("sp", lambda h, hh=hh: h.dma_start(out=ident4[:, hh, :], in_=self.consts[:, 0, :]), writes=[r_c])
            ogain = sb("ogain", [128, 128], F32)
            S.dma("sp", lambda h: h.dma_start(out=ogain[:], in_=self.ogain_bc[:, :]), writes=[r_c])
            cw = sb("cw", [128, 24, 4], F32)
            S.dma("sp", lambda h: h.dma_start(out=cw[:], in_=self.convw_l[:, :, :]), writes=[r_c])
            small = sb("small", [128, 2, 8], F32)
            S.dma("sp", lambda h: h.dma_start(out=small[:], in_=self.small_bc[:, :, :]), writes=[r_c])
            selall = sb("selall_sb", [8, 1024], F32)
            S.dma("sp", lambda h: h.dma_start(out=selall[:], in_=self.selall_d[:, :]), writes=[r_c])
            negA = sb("negA", [128, 8], F32)
            S.op("act", lambda h: h.activation(out=negA[:], in_=small[:, 0, :], func=AF.Exp), reads=[r_c], writes=[r_c])
            S.op("dve", lambda h: h.tensor_scalar(out=negA[:], in0=negA[:], scalar1=-1.0, scalar2=None, op0=ALU.mult), reads=[r_c], writes=[r_c])
            dtb = small[:, 1, :]
            cm = sb("cm", [128, 2], F32)
            S.op("pool", lambda h: h.memset(cm[:], 0.0), writes=[r_c])
            S.op("pool", lambda h: h.memset(cm[0:64, 0:1], 1.0), writes=[r_c])
            S.op("pool", lambda h: h.memset(cm[64:128, 1:2], 1.0), writes=[r_c])
            halo = sb("halo", [128, 24, 3], F32)
            r_halo = Res("halo")
            S.op("pool", lambda h: h.memset(halo[:], 0.0), writes=[r_halo])
            Sf = sb("Sf", [128, 8, 128], F32)
            r_Sf = [Res("Sf0"), Res("Sf1")]
            S.op("pool", lambda h: h.memset(Sf[:], 0.0), writes=r_Sf)
            Sb = [sb("Sb%d" % i, [128, 8, 128], BF16) for i in range(2)]
            r_Sb = [[Res("Sb%d_%d" % (i, g)) for g in range(2)] for i in range(2)]
            for i in range(2):
                S.op("pool", lambda h, i=i: h.memset(Sb[i][:], 0.0), writes=r_Sb[i])
            pools = {
                "junk": Pool_(nc, st, "junkA", [128, D], BF16, 1),
                "stat": Pool_(nc, st, "statA", [128, 4], F32, 8),
                "hb": Pool_(nc, st, "hbA", [128, D], BF16, 2),
                "tp_banks": [0],
            }
            xt_pool = Pool_(nc, st, "xtA", [128, D], F32, 2)
            y_pool = Pool_(nc, st, "yA", [128, D], F32, 1)
            ss_pool = Pool_(nc, st, "ssA", [128, 2], F32, 4)
            hT_pool = Pool_(nc, st, "hTA", [128, 8, TB], BF16, 1)
            pre_pool = Pool_(nc, st, "pre", [128, TB + 3], F32, 2)
            acc_pool = Pool_(nc, st, "acc", [128, TB], F32, 2)
            e_pool = Pool_(nc, st, "eA", [128, TB], F32, 2)
            sil_pool = Pool_(nc, st, "sil", [128, TB], F32, 2)
            sqb_pool = Pool_(nc, st, "sqb", [128, TB], BF16, 2)
            rinv_pool = Pool_(nc, st, "rinv", [128, TB], F32, 2)
            qT = sb("qT", [128, 8, TB], BF16)
            kT = sb("kT", [128, 8, TB], BF16)
            vT = sb("vT", [128, 8, TB], BF16)
            gateT = sb("gateT", [128, 8, TB], BF16)
            r_qkv = Res("qkvT")
            r_gate = Res("gateT")
            ogT = sb("ogT", [128, 8, TB], BF16)
            r_ogT = Res("ogT")
            sc_pool = Pool_(nc, st, "sc", [128, 12, 8], F32, 2)
            gcT_pool = Pool_(nc, st, "gcT", [8, 128], F32, 2)
            kbg_pool = Pool_(nc, st, "kbg", [128, 4, 128], BF16, 1)
            kg0_pool = Pool_(nc, st, "kg0", [128, 4, 128], BF16, 1)
            kg1_pool = Pool_(nc, st, "kg1", [128, 4, 128], BF16, 1)
            vb_pool = Pool_(nc, st, "vb", [128, 4, 128], BF16, 2)
            d_pool = Pool_(nc, st, "dd", [128, 4, 128], F32, 1)
            eT_pool = Pool_(nc, st, "eT", [128, 4, 128], F32, 2)
            attnT_pool = Pool_(nc, st, "attnT", [128, 4, 128], BF16, 2)
            egrow_pool = Pool_(nc, st, "egrow", [128, 4, 128], F32, 1)
            qg0 = sb("qg0", [128, 4, 128], BF16)
            qg1 = sb("qg1", [128, 4, 128], BF16)
            wT0 = sb("wT0", [128, 4, 128], BF16)
            wT1 = sb("wT1", [128, 4, 128], BF16)
            r_qg, r_wT = Res("qg"), Res("wT")
            for t_ in (qg0, qg1, wT0, wT1):
                S.op("pool", lambda h, t_=t_: h.memset(t_[:], 0.0), writes=[r_qg, r_wT])
            L_pool = Pool_(nc, st, "Lp", [128, 4, 128], TD, 2)
            N_pool = Pool_(nc, st, "Np", [128, 4, 128], TD, 2)
            P_pool = Pool_(nc, st, "Pp", [128, 4, 128], TD, 2)
            TT_pool = Pool_(nc, st, "TT", [128, 4, 128], BF16, 2)
            u_pool = Pool_(nc, st, "u", [128, 4, 128], F32, 1)
            vnew_pool = Pool_(nc, st, "vnew", [128, 4, 128], BF16, 2)
            for t_ in vnew_pool.tiles:
                S.op("pool", lambda h, t_=t_: h.memset(t_[:], 0.0), writes=vnew_pool.res)
            tmpS_pool = Pool_(nc, st, "tmpS", [128, 4, 128], F32, 1)
            osq_pool = Pool_(nc, st, "osq", [128, 4, 128], F32, 1)
            on_pool = Pool_(nc, st, "on", [128, 4, 128], BF16, 1)
            ident_td = self.ident_b if TD == BF16 else self.ident_f
            print("phase A sbuf remaining", nc.sbuf_bytes_remaining)

            def v4(ap):
                return ap.rearrange("p (a b) -> p a b", a=4)

            PS = self.psb
            PR = self.psr
            sbpar = 0
            for b in range(nblk):
                hT, r_hT = hT_pool.get()
                for tt in range(NTT):
                    r0 = b * TB + tt * 128
                    xt, r_xt = xt_pool.get()
                    S.dma("sp", lambda h, xt=xt, r0=r0: h.dma_start(out=xt[:], in_=self.x[r0:r0 + 128, :]), writes=[r_xt])
                    self.norm_to_hT(xt, r_xt, gpre, r_gpre, hT, r_hT, tt * 128, pools)
                scs = []
                for tt in range(NTT):
                    sc, r_sc = sc_pool.get()
                    gcT, r_gcT = gcT_pool.get()
                    scs.append((sc, r_sc, gcT, r_gcT))
                    pb = PS[1][:, 0:16]
                    for kc in range(8):
                        S.op("pe", lambda h, kc=kc, tt=tt, hT=hT, pb=pb: h.matmul(pb, hT[:, kc, tt * 128:(tt + 1) * 128], win[:, kc, 4 * D:4 * D + 16], start=(kc == 0), stop=(kc == 7)),
                             reads=[r_hT, r_win], writes=[PR[1]])
                    S.op("act", lambda h, sc=sc, pb=pb: h.activation(out=sc[:, 9, :], in_=pb[:, 0:8], func=AF.Exp, scale=-1.0), reads=[PR[1]], writes=[r_sc])
                    S.op("dve", lambda h, sc=sc, pb=pb: h.tensor_tensor(out=sc[:, 1, :], in0=pb[:, 8:16], in1=dtb, op=ALU.add), reads=[PR[1], r_c], writes=[r_sc])
                    S.op("dve", lambda h, sc=sc: h.tensor_scalar(out=sc[:, 9, :], in0=sc[:, 9, :], scalar1=1.0, scalar2=None, op0=ALU.add), reads=[r_sc], writes=[r_sc])
                    S.op("dve", lambda h, sc=sc: h.reciprocal(out=sc[:, 0, :], in_=sc[:, 9, :]), reads=[r_sc], writes=[r_sc])
                    S.op("act", lambda h, sc=sc: h.activation(out=sc[:, 1, :], in_=sc[:, 1, :], func=AF.Exp), reads=[r_sc], writes=[r_sc])
                    S.op("act", lambda h, sc=sc: h.activation(out=sc[:, 1, :], in_=sc[:, 1, :], func=AF.Ln, bias=self.one_col[:, 0:1], scale=1.0), reads=[r_sc, self.r_const], writes=[r_sc])
                    S.op("dve", lambda h, sc=sc: h.tensor_tensor(out=sc[:, 1, :], in0=sc[:, 1, :], in1=negA[:], op=ALU.mult), reads=[r_sc, r_c], writes=[r_sc])
                    pg = PS[2][:, 0:16]
                    S.op("pe", lambda h, sc=sc, pg=pg: h.matmul(pg[:, 0:8], UTbd, sc[:, 1, :], start=True, stop=True), reads=[r_sc, r_c], writes=[PR[2]])
                    S.op("pe", lambda h, sc=sc, pg=pg: h.matmul(pg[:, 8:16], BDones, sc[:, 1, :], start=True, stop=True), reads=[r_sc, r_c], writes=[PR[2]])
                    S.op("dve", lambda h, sc=sc, pg=pg: h.tensor_copy(sc[:, 2:4, :], pg.rearrange("p (a b) -> p a b", a=2)), reads=[PR[2]], writes=[r_sc])
                    S.op("act", lambda h, sc=sc: h.activation(out=sc[:, 4, :], in_=sc[:, 2, :], func=AF.Exp), reads=[r_sc], writes=[r_sc])
                    S.op("dve", lambda h, sc=sc: h.tensor_tensor(out=sc[:, 5, :], in0=sc[:, 4, :], in1=sc[:, 0, :], op=ALU.mult), reads=[r_sc], writes=[r_sc])
                    S.op("dve", lambda h, sc=sc: h.tensor_tensor(out=sc[:, 6, :], in0=sc[:, 3, :], in1=sc[:, 2, :], op=ALU.subtract), reads=[r_sc], writes=[r_sc])
                    S.op("act", lambda h, sc=sc: h.activation(out=sc[:, 6, :], in_=sc[:, 6, :], func=AF.Exp), reads=[r_sc], writes=[r_sc])
                    S.op("dve", lambda h, sc=sc: h.tensor_scalar(out=sc[:, 7, :], in0=sc[:, 6, :], scalar1=cm[:, 0:1], scalar2=None, op0=ALU.mult), reads=[r_sc, r_c], writes=[r_sc])
                    S.op("dve", lambda h, sc=sc: h.tensor_scalar(out=sc[:, 8, :], in0=sc[:, 6, :], scalar1=cm[:, 1:2], scalar2=None, op0=ALU.mult), reads=[r_sc, r_c], writes=[r_sc])
                    S.op("pe", lambda h, sc=sc: h.transpose(PS[0][0:8, 0:128], sc[:, 2, :], ident_f), reads=[r_sc, r_c], writes=[PR[0]])
                    S.op("dve", lambda h, gcT=gcT: h.tensor_copy(gcT[:], PS[0][0:8, 0:128]), reads=[PR[0]], writes=[r_gcT])
                    if "dbg_bg" in self.dump:
                        r0 = b * TB + tt * 128
                        S.dma("sp", lambda h, sc=sc, r0=r0: h.dma_start(out=self.dbg_bg[r0:r0 + 128, :, :], in_=sc[:, 0:2, :]), reads=[r_sc])
                pbi = 0
                for cc in range(32):
                    pi = 1 + (pbi % 2)
                    pbi += 1
                    pp = PS[pi][:, 0:TB]
                    for kc in range(8):
                        S.op("pe", lambda h, kc=kc, cc=cc, hT=hT, pp=pp: h.matmul(pp, win[:, kc, cc * 128:(cc + 1) * 128], hT[:, kc, :], start=(kc == 0), stop=(kc == 7)),
                             reads=[r_hT, r_win], writes=[PR[pi]])
                    if cc >= 24:
                        hh = cc - 24
                        e_, r_e = e_pool.get()
                        S.op("act", lambda h, e_=e_, pp=pp: h.activation(out=e_[:], in_=pp, func=AF.Exp, scale=-1.0), reads=[PR[pi]], writes=[r_e])
                        S.op("pool", lambda h, e_=e_: h.tensor_scalar(out=e_[:], in0=e_[:], scalar1=1.0, scalar2=None, op0=ALU.add), reads=[r_e], writes=[r_e])
                        S.op("dve", lambda h, e_=e_: h.reciprocal(out=e_[:], in_=e_[:]), reads=[r_e], writes=[r_e])
                        S.op("dve", lambda h, e_=e_, pp=pp, hh=hh: h.tensor_tensor(out=gateT[:, hh, :], in0=pp, in1=e_[:], op=ALU.mult), reads=[PR[pi], r_e], writes=[r_gate])
                        continue
                    pre, r_pre = pre_pool.get()
                    acc, r_acc = acc_pool.get()
                    S.op("pool", lambda h, pre=pre, cc=cc: h.tensor_copy(pre[:, 0:3], halo[:, cc, :]), reads=[r_halo], writes=[r_pre])
                    S.op("act", lambda h, pre=pre, pp=pp: h.copy(pre[:, 3:3 + TB], pp), reads=[PR[pi]], writes=[r_pre])
                    S.op("pool", lambda h, pre=pre, cc=cc: h.tensor_copy(halo[:, cc, :], pre[:, TB:TB + 3]), reads=[r_pre], writes=[r_halo])
                    S.op("dve", lambda h, pre=pre, acc=acc, cc=cc: h.tensor_scalar(out=acc[:], in0=pre[:, 3:3 + TB], scalar1=cw[:, cc, 3:4], scalar2=None, op0=ALU.mult), reads=[r_pre, r_c], writes=[r_acc])
                    S.op("dve", lambda h, pre=pre, acc=acc, cc=cc: h.scalar_tensor_tensor(out=acc[:], in0=pre[:, 2:2 + TB], scalar=cw[:, cc, 2:3], in1=acc[:], op0=ALU.mult, op1=ALU.add), reads=[r_pre, r_c, r_acc], writes=[r_acc])
                    S.op("dve", lambda h, pre=pre, acc=acc, cc=cc: h.scalar_tensor_tensor(out=acc[:], in0=pre[:, 1:1 + TB], scalar=cw[:, cc, 1:2], in1=acc[:], op0=ALU.mult, op1=ALU.add), reads=[r_pre, r_c, r_acc], writes=[r_acc])
                    S.op("dve", lambda h, pre=pre, acc=acc, cc=cc: h.scalar_tensor_tensor(out=acc[:], in0=pre[:, 0:TB], scalar=cw[:, cc, 0:1], in1=acc[:], op0=ALU.mult, op1=ALU.add), reads=[r_pre, r_c, r_acc], writes=[r_acc])
                    e_, r_e = e_pool.get()
                    S.op("act", lambda h, e_=e_, acc=acc: h.activation(out=e_[:], in_=acc[:], func=AF.Exp, scale=-1.0), reads=[r_acc], writes=[r_e])
                    S.op("pool", lambda h, e_=e_: h.tensor_scalar(out=e_[:], in0=e_[:], scalar1=1.0, scalar2=None, op0=ALU.add), reads=[r_e], writes=[r_e])
                    S.op("dve", lambda h, e_=e_: h.reciprocal(out=e_[:], in_=e_[:]), reads=[r_e], writes=[r_e])
                    which, hh = cc // 8, cc % 8
                    if which == 2:
                        S.op("pool", lambda h, e_=e_, acc=acc, hh=hh: h.tensor_tensor(out=vT[:, hh, :], in0=acc[:], in1=e_[:], op=ALU.mult), reads=[r_acc, r_e], writes=[r_qkv])
                        continue
                    sil, r_sil = sil_pool.get()
                    sqb, r_sqb = sqb_pool.get()
                    rinv, r_rinv = rinv_pool.get()
                    S.op("pool", lambda h, e_=e_, acc=acc, sil=sil: h.tensor_tensor(out=sil[:], in0=acc[:], in1=e_[:], op=ALU.mult), reads=[r_acc, r_e], writes=[r_sil])
                    S.op("act", lambda h, sil=sil, sqb=sqb: h.activation(out=sqb[:], in_=sil[:], func=AF.Square), reads=[r_sil], writes=[r_sqb])
                    pn = PS[3][:, 0:TB]
                    S.op("pe", lambda h, sqb=sqb, pn=pn: h.matmul(pn, ones_b[:], sqb[:], start=True, stop=True), reads=[r_sqb, r_c], writes=[PR[3]])
                    S.op("act", lambda h, rinv=rinv, pn=pn: h.activation(out=rinv[:], in_=pn, func=AF.Ln, bias=self.eps_col[:, 0:1], scale=1.0), reads=[PR[3], self.r_const], writes=[r_rinv])
                    S.op("act", lambda h, rinv=rinv: h.activation(out=rinv[:], in_=rinv[:], func=AF.Exp, scale=-0.5), reads=[r_rinv], writes=[r_rinv])
                    dst = (qT if which == 0 else kT)
                    scl = HD ** -0.5 if which == 0 else 1.0
                    S.op("dve", lambda h, sil=sil, rinv=rinv, dst=dst, hh=hh, scl=scl: h.scalar_tensor_tensor(out=dst[:, hh, :], in0=sil[:], scalar=scl, in1=rinv[:], op0=ALU.mult, op1=ALU.mult),
                         reads=[r_sil, r_rinv], writes=[r_qkv])
                if "dbg_qkv" in self.dump:
                    for i_, t_ in enumerate((qT, kT, vT, gateT)):
                        S.dma("sp", lambda h, i_=i_, t_=t_: h.dma_start(out=self.dbg_qkv[i_, :, :, b * TB:(b + 1) * TB], in_=t_[:]), reads=[r_qkv, r_gate])
                if self.a_stage < 2:
                    continue
                for tt in range(NTT):
                    sc, r_sc, gcT, r_gcT = scs[tt]
                    tc0 = tt * 128
                    for g in range(2):
                        h0 = g * 4
                        kbg, r_kbg = kbg_pool.get()
                        kg0, r_kg0 = kg0_pool.get()
                        kg1, r_kg1 = kg1_pool.get()
                        vb, r_vb = vb_pool.get()
                        pt = PS[0].bitcast(BF16)
                        for hh in range(4):
                            S.op("pe", lambda h, hh=hh, pt=pt: h.transpose(pt[:, hh * 128:(hh + 1) * 128], kT[:, h0 + hh, tc0:tc0 + 128], self.ident_b[:]), reads=[r_qkv, self.r_const], writes=[PR[0]])
                        pt4 = v4(pt[:, 0:512])

                        def bc(slot, sc=sc):
                            return sc[:, slot, h0:h0 + 4].unsqueeze(2).to_broadcast([128, 4, 128])
                        S.op("dve", lambda h, kbg=kbg, pt4=pt4, bc=bc: h.tensor_tensor(out=kbg[:], in0=pt4, in1=bc(5), op=ALU.mult), reads=[PR[0], r_sc], writes=[r_kbg])
                        S.op("dve", lambda h, kg0=kg0, pt4=pt4, bc=bc: h.tensor_tensor(out=kg0[:], in0=pt4, in1=bc(7), op=ALU.mult), reads=[PR[0], r_sc], writes=[r_kg0])
                        S.op("dve", lambda h, kg1=kg1, pt4=pt4, bc=bc: h.tensor_tensor(out=kg1[:], in0=pt4, in1=bc(8), op=ALU.mult), reads=[PR[0], r_sc], writes=[r_kg1])
                        for hh in range(4):
                            S.op("pe", lambda h, hh=hh, pt=pt: h.transpose(pt[:, hh * 128:(hh + 1) * 128], vT[:, h0 + hh, tc0:tc0 + 128], self.ident_b[:]), reads=[r_qkv, self.r_const], writes=[PR[0]])
                        S.op("dve", lambda h, vb=vb, pt4=pt4, bc=bc: h.tensor_tensor(out=vb[:], in0=pt4, in1=bc(0), op=ALU.mult), reads=[PR[0], r_sc], writes=[r_vb])
                        for hh in range(4):
                            S.op("pe", lambda h, hh=hh: h.matmul(PS[3][:, hh * 128:(hh + 1) * 128], kT[:, h0 + hh, tc0:tc0 + 128], qT[:, h0 + hh, tc0:tc0 + 128], start=True, stop=True), reads=[r_qkv], writes=[PR[3]])
                        for hh in range(4):
                            S.op("pe", lambda h, hh=hh: h.matmul(PS[4][:, hh * 128:(hh + 1) * 128], kT[:, h0 + hh, tc0:tc0 + 128], kT[:, h0 + hh, tc0:tc0 + 128], start=True, stop=True), reads=[r_qkv], writes=[PR[4]])
                        for hh in range(4):
                            S.op("pe", lambda h, hh=hh, gcT=gcT: h.matmul(PS[5][:, hh * 128:(hh + 1) * 128], selall[0:8, (h0 + hh) * 128:(h0 + hh + 1) * 128], gcT[:], start=True, stop=True), reads=[r_gcT, r_c], writes=[PR[5]])
                        gcb = v4(PS[5][:])
                        dd, r_dd = d_pool.get()
                        eT, r_eT = eT_pool.get()
                        attnT, r_attnT = attnT_pool.get()
                        egrow, r_egrow = egrow_pool.get()
                        Lp, r_L = L_pool.get()
                        negU4 = negmaskU.unsqueeze(1).to_broadcast([128, 4, 128])
                        posL4 = posmaskL.unsqueeze(1).to_broadcast([128, 4, 128])
                        S.op("dve", lambda h, dd=dd, gcb=gcb, bc=bc: h.tensor_tensor(out=dd[:], in0=gcb, in1=bc(2), op=ALU.subtract), reads=[PR[5], r_sc], writes=[r_dd])
                        S.op("pool", lambda h, eT=eT, dd=dd, negU4=negU4: h.tensor_tensor(out=eT[:], in0=dd[:], in1=negU4, op=ALU.add), reads=[r_dd, r_c], writes=[r_eT])
                        S.op("act", lambda h, eT=eT: h.activation(out=eT[:], in_=eT[:], func=AF.Exp), reads=[r_eT], writes=[r_eT])
                        S.op("dve", lambda h, attnT=attnT, eT=eT: h.tensor_tensor(out=attnT[:], in0=v4(PS[3][:]), in1=eT[:], op=ALU.mult), reads=[PR[3], r_eT], writes=[r_attnT])
                        S.op("act", lambda h, egrow=egrow, gcb=gcb: h.activation(out=egrow[:], in_=gcb, func=AF.Exp), reads=[PR[5]], writes=[r_egrow])
                        S.op("pool", lambda h, egrow=egrow: h.tensor_tensor(out=qg0[:, :, 0:64], in0=qT[:, h0:h0 + 4, tc0:tc0 + 64], in1=egrow[:, :, 0:64], op=ALU.mult), reads=[r_qkv, r_egrow], writes=[r_qg])
                        S.op("pool", lambda h, egrow=egrow: h.tensor_tensor(out=qg1[:, :, 64:128], in0=qT[:, h0:h0 + 4, tc0 + 64:tc0 + 128], in1=egrow[:, :, 64:128], op=ALU.mult), reads=[r_qkv, r_egrow], writes=[r_qg])
                        eL, r_eL = eT_pool.get()
                        S.op("pool", lambda h, eL=eL, dd=dd, posL4=posL4: h.tensor_tensor(out=eL[:], in0=dd[:], in1=posL4, op=ALU.add), reads=[r_dd, r_c], writes=[r_eL])
                        S.op("act", lambda h, eL=eL: h.activation(out=eL[:], in_=eL[:], func=AF.Exp, scale=-1.0), reads=[r_eL], writes=[r_eL])
                        S.op("dve", lambda h, eL=eL, bc=bc: h.tensor_tensor(out=eL[:], in0=eL[:], in1=bc(0), op=ALU.mult), reads=[r_eL, r_sc], writes=[r_eL])
                        S.op("dve", lambda h, Lp=Lp, eL=eL: h.tensor_tensor(out=Lp[:], in0=v4(PS[4][:]), in1=eL[:], op=ALU.mult), reads=[PR[4], r_eL], writes=[r_L])
                        self.dbgdump("attnT", attnT[:], [r_attnT], [128, 4, 128])
                        self.dbgdump("L", Lp[:], [r_L], [128, 4, 128])
                        self.dbgdump("dd", dd[:], [r_dd], [128, 4, 128])
                        self.dbgdump("egrow", egrow[:], [r_egrow], [128, 4, 128])
                        self.dbgdump("kbg", kbg[:], [r_kbg], [128, 4, 128])
                        self.dbgdump("vb", vb[:], [r_vb], [128, 4, 128])
                        self.dbgdump("kg0", kg0[:], [r_kg0], [128, 4, 128])
                        self.dbgdump("sc", sc[:], [r_sc], [128, 12, 8])
                        Np, r_N = N_pool.get()
                        Pp, r_P = P_pool.get()
                        pa, pb_ = 6, 7
                        pta = PS[pa].bitcast(TD) if TD == BF16 else PS[pa]
                        for hh in range(4):
                            S.op("pe", lambda h, hh=hh, Lp=Lp, pta=pta: h.transpose(pta[:, hh * 128:(hh + 1) * 128], Lp[:, hh, :], ident_td[:] if TD == BF16 else ident_f), reads=[r_L, self.r_const, r_c], writes=[PR[pa]])
                        S.op("act", lambda h, Np=Np, pta=pta: h.copy(Np[:], v4(pta[:, 0:512])), reads=[PR[pa]], writes=[r_N])
                        S.op("dve", lambda h, Pp=Pp, pta=pta: h.tensor_tensor(out=Pp[:], in0=ident4[:], in1=v4(pta[:, 0:512]), op=ALU.subtract), reads=[PR[pa], r_c], writes=[r_P])
                        for lvl in range(1, 6):
                            L2, r_L2 = L_pool.get()
                            for hh in range(4):
                                S.op("pe", lambda h, hh=hh, Np=Np, Lp=Lp: h.matmul(PS[pa][:, hh * 128:(hh + 1) * 128], Np[:, hh, :], Lp[:, hh, :], start=True, stop=True), reads=[r_N, r_L], writes=[PR[pa]])
                            S.op("act", lambda h, L2=L2: h.copy(L2[:], v4(PS[pa][:])), reads=[PR[pa]], writes=[r_L2])
                            if lvl < 5:
                                N2, r_N2 = N_pool.get()
                                for hh in range(4):
                                    S.op("pe", lambda h, hh=hh, Np=Np, Lp=Lp: h.matmul(PS[pb_][:, hh * 128:(hh + 1) * 128], Lp[:, hh, :], Np[:, hh, :], start=True, stop=True), reads=[r_N, r_L], writes=[PR[pb_]])
                                S.op("dve", lambda h, N2=N2: h.tensor_copy(N2[:], v4(PS[pb_][:])), reads=[PR[pb_]], writes=[r_N2])
                            P2, r_P2 = P_pool.get()
                            pc = 5
                            for hh in range(4):
                                S.op("pe", lambda h, hh=hh, L2=L2, Pp=Pp, pc=pc: h.matmul(PS[pc][:, hh * 128:(hh + 1) * 128], L2[:, hh, :], Pp[:, hh, :], start=True, stop=True), reads=[r_L2, r_P], writes=[PR[pc]])
                            if lvl < 5:
                                S.op("dve", lambda h, P2=P2, Pp=Pp, pc=pc: h.tensor_tensor(out=P2[:], in0=v4(PS[pc][:]), in1=Pp[:], op=ALU.add), reads=[PR[pc], r_P], writes=[r_P2])
                                Np, r_N = N2, r_N2
                                Pp, r_P = P2, r_P2
                            else:
                                TT, r_TT = TT_pool.get()
                                S.op("dve", lambda h, TT=TT, Pp=Pp, pc=pc: h.tensor_tensor(out=TT[:], in0=v4(PS[pc][:]), in1=Pp[:], op=ALU.add), reads=[PR[pc], r_P], writes=[r_TT])
                            Lp, r_L = L2, r_L2
                        u, r_u = u_pool.get()
                        for hh in range(4):
                            S.op("pe", lambda h, hh=hh, kbg=kbg, TT=TT: h.matmul(PS[6][:, hh * 128:(hh + 1) * 128], kbg[:, hh, :], TT[:, hh, :], start=True, stop=True), reads=[r_kbg, r_TT], writes=[PR[6]])
                        S.op("act", lambda h: h.copy(wT0[:, :, 0:64], v4(PS[6][:])[:, :, 0:64]), reads=[PR[6]], writes=[r_wT])
                        S.op("act", lambda h: h.copy(wT1[:, :, 64:128], v4(PS[6][:])[:, :, 64:128]), reads=[PR[6]], writes=[r_wT])
                        for hh in range(4):
                            S.op("pe", lambda h, hh=hh, vb=vb, TT=TT: h.matmul(PS[7][:, hh * 128:(hh + 1) * 128], TT[:, hh, :], vb[:, hh, :], start=True, stop=True), reads=[r_vb, r_TT], writes=[PR[7]])
                        S.op("dve", lambda h, u=u: h.tensor_copy(u[:], v4(PS[7][:])), reads=[PR[7]], writes=[r_u])
                        self.dbgdump("TT", TT[:], [r_TT], [128, 4, 128])
                        self.dbgdump("u", u[:], [r_u], [128, 4, 128])
                        self.dbgdump("wT0", wT0[:], [r_wT], [128, 4, 128])
                        self.dbgdump("wT1", wT1[:], [r_wT], [128, 4, 128])
                        vnew, r_vnew = vnew_pool.get()
                        for c in range(2):
                            cs = slice(c * 64, (c + 1) * 64)
                            Sb_in, r_Sb_in = Sb[sbpar][:, h0:h0 + 4, :], r_Sb[sbpar][g]
                            Sb_out, r_Sb_out = Sb[1 - sbpar][:, h0:h0 + 4, :], r_Sb[1 - sbpar][g]
                            wTc = wT0 if c == 0 else wT1
                            kgc, r_kgc = (kg0, r_kg0) if c == 0 else (kg1, r_kg1)
                            for hh in range(4):
                                S.op("pe", lambda h, hh=hh, wTc=wTc, Sb_in=Sb_in: h.matmul(PS[6][:, hh * 128:(hh + 1) * 128], wTc[:, hh, :], Sb_in[:, hh, :], start=True, stop=True), reads=[r_wT, r_Sb_in], writes=[PR[6]])
                            S.op("dve", lambda h, cs=cs, vnew=vnew, u=u: h.tensor_tensor(out=vnew[cs, :, :], in0=u[cs, :, :], in1=v4(PS[6][:])[cs, :, :], op=ALU.subtract), reads=[PR[6], r_u], writes=[r_vnew])
                            if c == 1:
                                Sb0 = Sb[1 - sbpar][:, h0:h0 + 4, :]
                                for hh in range(4):
                                    oc = PS[4][:, hh * 128:(hh + 1) * 128]
                                    S.op("pe", lambda h, hh=hh, oc=oc, Sb0=Sb0: h.matmul(oc, qg0[:, hh, :], Sb0[:, hh, :], start=True, stop=False), reads=[r_qg, r_Sb[1 - sbpar][g]], writes=[PR[4]])
                                    S.op("pe", lambda h, hh=hh, oc=oc, Sb_in=Sb_in: h.matmul(oc, qg1[:, hh, :], Sb_in[:, hh, :], start=False, stop=False), reads=[r_qg, r_Sb_in], writes=[PR[4]])
                                    S.op("pe", lambda h, hh=hh, oc=oc, attnT=attnT, vnew=vnew: h.matmul(oc, attnT[:, hh, :], vnew[:, hh, :], start=False, stop=True), reads=[r_attnT, r_vnew], writes=[PR[4]])
                            for hh in range(4):
                                S.op("pe", lambda h, hh=hh, kgc=kgc, vnew=vnew: h.matmul(PS[7][:, hh * 128:(hh + 1) * 128], kgc[:, hh, :], vnew[:, hh, :], start=True, stop=True), reads=[r_kgc, r_vnew], writes=[PR[7]])
                            tmpS, r_tmpS = tmpS_pool.get()
                            col = 63 if c == 0 else 127
                            eglb = egrow[:, :, col:col + 1].to_broadcast([128, 4, 128])
                            S.op("pool", lambda h, tmpS=tmpS, eglb=eglb: h.tensor_tensor(out=tmpS[:], in0=Sf[:, h0:h0 + 4, :], in1=eglb, op=ALU.mult), reads=[r_Sf[g], r_egrow], writes=[r_tmpS])
                            S.op("dve", lambda h, tmpS=tmpS: h.tensor_tensor(out=Sf[:, h0:h0 + 4, :], in0=tmpS[:], in1=v4(PS[7][:]), op=ALU.add), reads=[PR[7], r_tmpS], writes=[r_Sf[g]])
                            S.op("act", lambda h, Sb_out=Sb_out: h.copy(Sb_out, Sf[:, h0:h0 + 4, :]), reads=[r_Sf[g]], writes=[r_Sb_out])
                            if g == 1:
                                pass
                            sbpar_next = 1 - sbpar
                            sbpar = sbpar_next
                        self.dbgdump("vnew", vnew[:], [r_vnew], [128, 4, 128])
                        self.dbgdump("Sf", Sf[:, 0:4, :], [r_Sf[0]], [128, 4, 128])
                        self.dbgdump("qg0", qg0[:], [r_qg], [128, 4, 128])
                        self.dbgdump("qg1", qg1[:], [r_qg], [128, 4, 128])
                        osq, r_osq = osq_pool.get()
                        on, r_on = on_pool.get()
                        stt_, r_stt = pools["stat"].get()
                        o4 = v4(PS[4][:])
                        if "dbg_o" in self.dump:
                            S.op("act", lambda h, osq=osq, o4=o4: h.copy(osq[:], o4), reads=[PR[4]], writes=[r_osq])
                            r0 = b * TB + tt * 128
                            S.dma("sp", lambda h, osq=osq, r0=r0: h.dma_start(out=self.dbg_o[r0:r0 + 128, h0 * 128:(h0 + 4) * 128], in_=osq[:].rearrange("p a b -> p (a b)")), reads=[r_osq])
                        S.op("act", lambda h, osq=osq, o4=o4: h.activation(out=osq[:], in_=o4, func=AF.Square), reads=[PR[4]], writes=[r_osq])
                        S.op("dve", lambda h, osq=osq, stt_=stt_: h.tensor_reduce(out=stt_[:, 0:4], in_=osq[:], axis=mybir.AxisListType.X, op=ALU.add), reads=[r_osq], writes=[r_stt])
                        self.rms_rstd(stt_[:, 0:4], stt_[:, 0:4], r_stt, r_stt, HD)
                        rsb = stt_[:, 0:4].unsqueeze(2).to_broadcast([128, 4, 128])
                        ogb = ogain[:].unsqueeze(1).to_broadcast([128, 4, 128])
                        S.op("dve", lambda h, osq=osq, o4=o4, rsb=rsb: h.tensor_tensor(out=osq[:], in0=o4, in1=rsb, op=ALU.mult), reads=[PR[4], r_stt], writes=[r_osq])
                        S.op("pool", lambda h, osq=osq, on=on, ogb=ogb: h.tensor_tensor(out=on[:], in0=osq[:], in1=ogb, op=ALU.mult), reads=[r_osq, r_c], writes=[r_on])
                        pt = PS[0].bitcast(BF16)
                        for hh in range(4):
                            S.op("pe", lambda h, hh=hh, pt=pt, on=on: h.transpose(pt[:, hh * 128:(hh + 1) * 128], on[:, hh, :], self.ident_b[:]), reads=[r_on, self.r_const], writes=[PR[0]])
                        S.op("dve", lambda h, pt=pt: h.tensor_tensor(out=ogT[:, h0:h0 + 4, tc0:tc0 + 128], in0=v4(pt[:, 0:512]), in1=gateT[:, h0:h0 + 4, tc0:tc0 + 128], op=ALU.mult), reads=[PR[0], r_gate], writes=[r_ogT])
                if self.a_stage < 3:
                    continue
                for tt in range(NTT):
                    r0 = b * TB + tt * 128
                    y, r_y = y_pool.get()
                    ss, r_ss = ss_pool.get()
                    for half in range(2):
                        pi = 1 + half
                        pd = PS[pi]
                        for hh in range(8):
                            S.op("pe", lambda h, pd=pd, hh=hh, tt=tt, half=half: h.matmul(pd[:], ogT[:, hh, tt * 128:(tt + 1) * 128], wout[:, hh, half * 512:(half + 1) * 512], start=(hh == 0), stop=(hh == 7)),
                                 reads=[r_ogT, r_wout], writes=[PR[pi]])
                        jk, r_jk = pools["junk"].get()
                        S.op("act", lambda h, pd=pd, jk=jk, ss=ss, half=half: h.activation(out=jk[:, 0:512], in_=pd[:], func=AF.Square, accum_out=ss[:, half:half + 1]),
                             reads=[PR[pi]], writes=[r_jk, r_ss])
                        S.op("dve", lambda h, pd=pd, y=y, half=half: h.tensor_copy(y[:, half * 512:(half + 1) * 512], pd[:]), reads=[PR[pi]], writes=[r_y])
                    xr, r_xr = xt_pool.get()
                    S.dma("sp", lambda h, xr=xr, r0=r0: h.dma_start(out=xr[:], in_=self.x[r0:r0 + 128, :]), writes=[r_xr])
                    self.post_norm_residual(y, r_y, ss, r_ss, gpost, r_gpost, xr, r_xr, pools)
                    S.dma("sp", lambda h, y=y, r0=r0: h.dma_start(out=self.xm[r0:r0 + 128, :], in_=y[:]), reads=[r_y])


    def phase_A1(self):
        nc, S = self.nc, self.S
        T = self.T
        TB = 512 if T % 512 == 0 else 256
        NTT = TB // 128
        nblk = T // TB
        self._tpi = 0
        st = ExitStack()
        with st:
            def sb(name, shape, dt):
                return st.enter_context(nc.sbuf_tensor(name, shape, dt))
            win, r_win = self.load_w_kc(st, "win", self.w_in, GIN)
            gpre, r_gpre = self.load_gain_bc(st, "gpreA", 0)
            r_c = Res("constA1")
            cst = sb("cstA1", [128, 3, 128], F32)
            S.dma("sp", lambda h: h.dma_start(out=cst[:], in_=self.consts[:, 0:3, :]), writes=[r_c])
            ident_f = cst[:, 0, :]
            UTbd = cst[:, 1, :]
            BDones = cst[:, 2, :]
            ones_b = sb("ones_b", [128, 128], BF16)
            S.op("pool", lambda h: h.memset(ones_b[:], 1.0), writes=[r_c])
            cw = sb("cw", [128, 24, 4], F32)
            S.dma("sp", lambda h: h.dma_start(out=cw[:], in_=self.convw_l[:, :, :]), writes=[r_c])
            small = sb("small", [128, 2, 8], F32)
            S.dma("sp", lambda h: h.dma_start(out=small[:], in_=self.small_bc[:, :, :]), writes=[r_c])
            negA = sb("negA", [128, 8], F32)
            S.op("act", lambda h: h.activation(out=negA[:], in_=small[:, 0, :], func=AF.Exp), reads=[r_c], writes=[r_c])
            S.op("dve", lambda h: h.tensor_scalar(out=negA[:], in0=negA[:], scalar1=-1.0, scalar2=None, op0=ALU.mult), reads=[r_c], writes=[r_c])
            dtb = small[:, 1, :]
            cm = sb("cm", [128, 2], F32)
            S.op("pool", lambda h: h.memset(cm[:], 0.0), writes=[r_c])
            S.op("pool", lambda h: h.memset(cm[0:64, 0:1], 1.0), writes=[r_c])
            S.op("pool", lambda h: h.memset(cm[64:128, 1:2], 1.0), writes=[r_c])
            halo = sb("halo", [128, 24, 3], BF16)
            r_halo = Res("halo")
            S.op("pool", lambda h: h.memset(halo[:], 0.0), writes=[r_halo])
            dg = sb("dg", [128, 24, 4, 128], BF16)
            for cc_ in range(24):
                for k_ in range(4):
                    S.op("dve", lambda h: h.tensor_scalar(out=dg[:, cc_, k_, :], in0=ident_f, scalar1=cw[:, cc_, k_:k_ + 1], scalar2=None, op0=ALU.mult), reads=[r_c], writes=[r_c])
            slot_banks = [3, 4, 5, 6, 7]
            pools = {
                "junk": Pool_(nc, st, "junkA", [128, D], BF16, 1),
                "stat": Pool_(nc, st, "statA", [128, 4], F32, 8),
                "hb": Pool_(nc, st, "hbA", [128, D], BF16, 2),
                "tp_banks": [0],
            }
            xt_pool = Pool_(nc, st, "xtA", [128, D], F32, 2)
            hT_pool = Pool_(nc, st, "hTA", [128, 8, TB], BF16, 2)
            pre_pool = Pool_(nc, st, "pre", [128, TB + 4], BF16, 8)
            e_pool = Pool_(nc, st, "eA", [128, TB], F32, 8)
            sil_pool = Pool_(nc, st, "sil", [128, TB], F32, 8)
            sqb_pool = Pool_(nc, st, "sqb", [128, TB], BF16, 8)
            rinv_pool = Pool_(nc, st, "rinv", [128, TB], F32, 8)
            oc_pool = Pool_(nc, st, "ocA", [128, TB], BF16, 8)
            sc_pool = Pool_(nc, st, "sc", [128, 12, 8], F32, 4)
            gcT_pool = Pool_(nc, st, "gcT", [8, 128], F32, 4)
            PS, PR = self.psb, self.psr
            pbi = 0
            hTs = {}

            def gen_norm(b):
                hT, r_hT = hT_pool.get()
                hTs[b] = (hT, r_hT)
                for tt in range(NTT):
                    r0 = b * TB + tt * 128
                    xt, r_xt = xt_pool.get()
                    S.dma("sp", lambda h: h.dma_start(out=xt[:], in_=self.x[r0:r0 + 128, :]), writes=[r_xt])
                    self.norm_to_hT(xt, r_xt, gpre, r_gpre, hT, r_hT, tt * 128, pools)
                    yield

            for _ in gen_norm(0):
                pass
            for b in range(nblk):
                hT, r_hT = hTs[b]
                def gen_sc(b=b, hT=hT, r_hT=r_hT):
                    for tt in range(NTT):
                        sc, r_sc = sc_pool.get()
                        gcT, r_gcT = gcT_pool.get()
                        pb = PS[1][:, 0:16]
                        for kc in range(8):
                            S.op("pe", lambda h: h.matmul(pb, hT[:, kc, tt * 128:(tt + 1) * 128], win[:, kc, 4 * D:4 * D + 16], start=(kc == 0), stop=(kc == 7)),
                                 reads=[r_hT, r_win], writes=[PR[1]])
                        S.op("act", lambda h: h.activation(out=sc[:, 9, :], in_=pb[:, 0:8], func=AF.Exp, scale=-1.0), reads=[PR[1]], writes=[r_sc])
                        S.op("dve", lambda h: h.tensor_tensor(out=sc[:, 1, :], in0=pb[:, 8:16], in1=dtb, op=ALU.add), reads=[PR[1], r_c], writes=[r_sc])
                        S.op("dve", lambda h: h.tensor_scalar(out=sc[:, 9, :], in0=sc[:, 9, :], scalar1=1.0, scalar2=None, op0=ALU.add), reads=[r_sc], writes=[r_sc])
                        S.op("dve", lambda h: h.reciprocal(out=sc[:, 0, :], in_=sc[:, 9, :]), reads=[r_sc], writes=[r_sc])
                        yield
                        S.op("act", lambda h: h.activation(out=sc[:, 1, :], in_=sc[:, 1, :], func=AF.Exp), reads=[r_sc], writes=[r_sc])
                        S.op("act", lambda h: h.activation(out=sc[:, 1, :], in_=sc[:, 1, :], func=AF.Ln, bias=self.one_col[:, 0:1], scale=1.0), reads=[r_sc, self.r_const], writes=[r_sc])
                        yield
                        S.op("dve", lambda h: h.tensor_tensor(out=sc[:, 1, :], in0=sc[:, 1, :], in1=negA[:], op=ALU.mult), reads=[r_sc, r_c], writes=[r_sc])
                        pg = PS[2][:, 0:16]
                        S.op("pe", lambda h: h.matmul(pg[:, 0:8], UTbd, sc[:, 1, :], start=True, stop=True), reads=[r_sc, r_c], writes=[PR[2]])
                        S.op("pe", lambda h: h.matmul(pg[:, 8:16], BDones, sc[:, 1, :], start=True, stop=True), reads=[r_sc, r_c], writes=[PR[2]])
                        S.op("dve", lambda h: h.tensor_copy(sc[:, 2:4, :], pg.rearrange("p (a b) -> p a b", a=2)), reads=[PR[2]], writes=[r_sc])
                        yield
                        S.op("act", lambda h: h.activation(out=sc[:, 4, :], in_=sc[:, 2, :], func=AF.Exp), reads=[r_sc], writes=[r_sc])
                        S.op("dve", lambda h: h.tensor_tensor(out=sc[:, 5, :], in0=sc[:, 4, :], in1=sc[:, 0, :], op=ALU.mult), reads=[r_sc], writes=[r_sc])
                        S.op("dve", lambda h: h.tensor_tensor(out=sc[:, 6, :], in0=sc[:, 3, :], in1=sc[:, 2, :], op=ALU.subtract), reads=[r_sc], writes=[r_sc])
                        S.op("act", lambda h: h.activation(out=sc[:, 6, :], in_=sc[:, 6, :], func=AF.Exp), reads=[r_sc], writes=[r_sc])
                        yield
                        S.op("dve", lambda h: h.tensor_scalar(out=sc[:, 7, :], in0=sc[:, 6, :], scalar1=cm[:, 0:1], scalar2=None, op0=ALU.mult), reads=[r_sc, r_c], writes=[r_sc])
                        S.op("dve", lambda h: h.tensor_scalar(out=sc[:, 8, :], in0=sc[:, 6, :], scalar1=cm[:, 1:2], scalar2=None, op0=ALU.mult), reads=[r_sc, r_c], writes=[r_sc])
                        S.op("pe", lambda h: h.transpose(PS[0][0:8, 0:128], sc[:, 2, :], ident_f), reads=[r_sc, r_c], writes=[PR[0]])
                        S.op("dve", lambda h: h.tensor_copy(gcT[:], PS[0][0:8, 0:128]), reads=[PR[0]], writes=[r_gcT])
                        ti = b * NTT + tt
                        S.dma("sp", lambda h: h.dma_start(out=self.scd[ti], in_=sc[:]), reads=[r_sc])
                        S.dma("sp", lambda h: h.dma_start(out=self.gctd[ti], in_=gcT[:]), reads=[r_gcT])
                        yield
                        if "dbg_bg" in self.dump:
                            r0 = b * TB + tt * 128
                            S.dma("sp", lambda h: h.dma_start(out=self.dbg_bg[r0:r0 + 128, :, :], in_=sc[:, 0:2, :]), reads=[r_sc])

                def gen_chunk(cc, hT=hT, r_hT=r_hT, b=b):
                    oc, r_oc = oc_pool.get()
                    which_, hh_ = cc // 8, cc % 8

                    def store():
                        S.dma("sp", lambda h: h.dma_start(out=self.qkvg[which_, :, hh_, b * TB:(b + 1) * TB], in_=oc[:]), reads=[r_oc])
                    pi = free_banks.pop(0)
                    pp = PS[pi][:, 0:TB]
                    for kc in range(8):
                        S.op("pe", lambda h: h.matmul(pp, win[:, kc, cc * 128:(cc + 1) * 128], hT[:, kc, :], start=(kc == 0), stop=(kc == 7)),
                             reads=[r_hT, r_win.at(kc, cc * 128)], writes=[PR[pi]])
                    yield
                    e_, r_e = e_pool.get()
                    if cc >= 24:
                        hh = cc - 24
                        S.op("act", lambda h: h.activation(out=e_[:], in_=pp, func=AF.Exp, scale=-1.0), reads=[PR[pi]], writes=[r_e])
                        yield
                        S.op("act", lambda h: h.activation(out=e_[:], in_=e_[:], func=AF.Ln, bias=self.one_col[:, 0:1], scale=1.0), reads=[r_e, self.r_const], writes=[r_e])
                        yield
                        S.op("act", lambda h: h.activation(out=e_[:], in_=e_[:], func=AF.Exp, scale=-1.0), reads=[r_e], writes=[r_e])
                        yield
                        S.op("dve", lambda h: h.tensor_tensor(out=oc[:], in0=pp, in1=e_[:], op=ALU.mult), reads=[PR[pi], r_e], writes=[r_oc])
                        store()
                        free_banks.append(pi)
                        return
                    pre, r_pre = pre_pool.get()
                    S.op("pool", lambda h: h.tensor_copy(pre[:, 0:3], halo[:, cc, :]), reads=[r_halo], writes=[r_pre])
                    S.op("dve", lambda h: h.tensor_copy(pre[:, 3:3 + TB], pp), reads=[PR[pi]], writes=[r_pre])
                    yield
                    S.op("pool", lambda h: h.tensor_copy(halo[:, cc, :], pre[:, TB:TB + 3]), reads=[r_pre], writes=[r_halo])
                    for k_ in range(4):
                        S.op("pe", lambda h: h.matmul(pp, dg[:, cc, k_, :], pre[:, k_:k_ + TB], start=(k_ == 0), stop=(k_ == 3)), reads=[r_pre, r_c], writes=[PR[pi]])
                    yield
                    S.op("act", lambda h: h.activation(out=e_[:], in_=pp, func=AF.Exp, scale=-1.0), reads=[PR[pi]], writes=[r_e])
                    yield
                    S.op("act", lambda h: h.activation(out=e_[:], in_=e_[:], func=AF.Ln, bias=self.one_col[:, 0:1], scale=1.0), reads=[r_e, self.r_const], writes=[r_e])
                    yield
                    S.op("act", lambda h: h.activation(out=e_[:], in_=e_[:], func=AF.Exp, scale=-1.0), reads=[r_e], writes=[r_e])
                    yield
                    which, hh = cc // 8, cc % 8
                    if which == 2:
                        S.op("dve", lambda h: h.tensor_tensor(out=oc[:], in0=pp, in1=e_[:], op=ALU.mult), reads=[PR[pi], r_e], writes=[r_oc])
                        store()
                        free_banks.append(pi)
                        return
                    sil, r_sil = sil_pool.get()
                    sqb, r_sqb = sqb_pool.get()
                    rinv, r_rinv = rinv_pool.get()
                    S.op("dve", lambda h: h.tensor_tensor(out=sil[:], in0=pp, in1=e_[:], op=ALU.mult), reads=[PR[pi], r_e], writes=[r_sil])
                    yield
                    S.op("dve", lambda h: h.tensor_tensor(out=sqb[:], in0=sil[:], in1=sil[:], op=ALU.mult), reads=[r_sil], writes=[r_sqb])
                    yield
                    S.op("pe", lambda h: h.matmul(pp, ones_b[:], sqb[:], start=True, stop=True), reads=[r_sqb, r_c], writes=[PR[pi]])
                    yield
                    S.op("act", lambda h: h.activation(out=rinv[:], in_=pp, func=AF.Ln, bias=self.eps_col[:, 0:1], scale=1.0), reads=[PR[pi], self.r_const], writes=[r_rinv])
                    yield
                    S.op("act", lambda h: h.activation(out=rinv[:], in_=rinv[:], func=AF.Exp, scale=-0.5), reads=[r_rinv], writes=[r_rinv])
                    yield
                    scl = HD ** -0.5 if which == 0 else 1.0
                    S.op("dve", lambda h: h.scalar_tensor_tensor(out=oc[:], in0=sil[:], scalar=scl, in1=rinv[:], op0=ALU.mult, op1=ALU.mult),
                         reads=[r_sil, r_rinv], writes=[r_oc])
                    store()
                    free_banks.append(pi)

                NFL = len(slot_banks)
                free_banks = list(slot_banks)
                pending = list(range(32))
                active = [gen_sc()]
                if b + 1 < nblk:
                    active.append(gen_norm(b + 1))
                NFL += len(active)
                rnd = 0
                while pending or active:
                    if pending and free_banks and rnd % 2 == 0:
                        active.append(gen_chunk(pending.pop(0)))
                    rnd += 1
                    for gn in list(active):
                        try:
                            next(gn)
                        except StopIteration:
                            active.remove(gn)

    def phase_A2(self):
        nc, S = self.nc, self.S
        T = self.T
        TB = 256
        NTT = TB // 128
        nblk = T // TB
        TD = self.chain_dt
        st = ExitStack()
        with st:
            def sb(name, shape, dt):
                return st.enter_context(nc.sbuf_tensor(name, shape, dt))
            wout, r_wout = self.load_w_kc(st, "wout", self.w_out, D)
            gpost, r_gpost = self.load_gain_bc(st, "gpostA", 2)
            r_c = Res("constA2")
            cst = sb("cstA2", [128, 2, 128], F32)
            S.dma("sp", lambda h: h.dma_start(out=cst[:], in_=self.consts[:, 3:5, :]), writes=[r_c])
            negmaskU = cst[:, 0, :]
            posmaskL = cst[:, 1, :]
            ident4 = sb("ident4", [128, 4, 128], F32)
            for hh in range(4):
                S.dma("sp", lambda h: h.dma_start(out=ident4[:, hh, :], in_=self.consts[:, 0, :]), writes=[r_c])
            ogain = sb("ogain", [128, 128], F32)
            S.dma("sp", lambda h: h.dma_start(out=ogain[:], in_=self.ogain_bc[:, :]), writes=[r_c])
            selall = sb("selall_sb", [8, 1024], F32)
            S.dma("sp", lambda h: h.dma_start(out=selall[:], in_=self.selall_d[:, :]), writes=[r_c])
            Sf = sb("Sf", [128, 8, 128], F32)
            r_Sf = [Res("Sf0"), Res("Sf1")]
            S.op("pool", lambda h: h.memset(Sf[:], 0.0), writes=r_Sf)
            Sb = [sb("Sb%d" % i, [128, 8, 128], BF16) for i in range(2)]
            r_Sb = [[Res("Sb%d_%d" % (i, g)) for g in range(2)] for i in range(2)]
            for i in range(2):
                S.op("pool", lambda h: h.memset(Sb[i][:], 0.0), writes=r_Sb[i])
            pools = {
                "junk": Pool_(nc, st, "junkA2", [128, D], BF16, 1),
                "stat": Pool_(nc, st, "statA2", [128, 4], F32, 8),
            }
            xt_pool = Pool_(nc, st, "xtA2", [128, D], F32, 2)
            y_pool = Pool_(nc, st, "yA2", [128, D], F32, 1)
            ss_pool = Pool_(nc, st, "ssA2", [128, 2], F32, 4)
            in_pool = Pool_(nc, st, "qkvgin", [128, 4, 8, TB], BF16, 2)
            ogT_pool = Pool_(nc, st, "ogT", [128, 8, TB], BF16, 2)
            sc_pool = Pool_(nc, st, "sc2", [128, 12, 8], F32, 4)
            gcT_pool = Pool_(nc, st, "gcT2", [8, 128], F32, 4)
            NCH = NTT * 2

            def chain_bufs(ci):
                d = {}
                def mk(nm, dt, n=1):
                    ts = [sb("%s_%d_%d" % (nm, ci, i), [128, 4, 128], dt) for i in range(n)]
                    rs = [Res("%s_%d_%d" % (nm, ci, i)) for i in range(n)]
                    d[nm] = (ts, rs)
                for nm, dt, n in (("kbg", BF16, 1), ("kg0", BF16, 1), ("kg1", BF16, 1), ("vb", BF16, 1), ("dd", F32, 1), ("eT", F32, 1),
                                  ("attnT", BF16, 1), ("egrow", F32, 1), ("qg0", BF16, 1), ("qg1", BF16, 1), ("wT0", BF16, 1), ("wT1", BF16, 1),
                                  ("Lp", TD, 2), ("Np", TD, 2), ("Pp", TD, 2), ("TT", BF16, 1), ("u", F32, 1), ("vnew", BF16, 1)):
                    mk(nm, dt, n)
                d["eL"] = d["dd"]
                for nm in ("qg0", "qg1", "wT0", "wT1", "vnew"):
                    t_ = d[nm][0][0]
                    S.op("pool", lambda h: h.memset(t_[:], 0.0), writes=[d[nm][1][0]])
                return d
            CB = [chain_bufs(ci) for ci in range(NCH)]
            tmpS_pool = Pool_(nc, st, "tmpS", [128, 4, 128], F32, 2)
            osq_pool = Pool_(nc, st, "osq", [128, 4, 128], F32, 1)
            on_pool = Pool_(nc, st, "on", [128, 4, 128], BF16, 2)
            ident_td = self.ident_b if TD == BF16 else self.ident_f
            print("phase A2 sbuf remaining", nc.sbuf_bytes_remaining)
            PS, PR = self.psb, self.psr

            def v4(ap):
                return ap.rearrange("p (a b) -> p a b", a=4)

            def bview(pi):
                return PS[pi].bitcast(BF16)

            def gen_pre(ci, tt, g, it, r_it, sc, r_sc, gcT, r_gcT):
                qT, kT, vT = it[:, 0], it[:, 1], it[:, 2]
                B = CB[ci]
                b0, b1 = 2 * ci, 2 * ci + 1
                h0 = g * 4
                tc0 = tt * 128
                kbg, r_kbg = B["kbg"][0][0], B["kbg"][1][0]
                kg0, r_kg0 = B["kg0"][0][0], B["kg0"][1][0]
                kg1, r_kg1 = B["kg1"][0][0], B["kg1"][1][0]
                vb, r_vb = B["vb"][0][0], B["vb"][1][0]
                dd, r_dd = B["dd"][0][0], B["dd"][1][0]
                eT, r_eT = B["eT"][0][0], B["eT"][1][0]
                eL, r_eL = B["eL"][0][0], B["eL"][1][0]
                attnT, r_attnT = B["attnT"][0][0], B["attnT"][1][0]
                egrow, r_egrow = B["egrow"][0][0], B["egrow"][1][0]
                qg0, qg1 = B["qg0"][0][0], B["qg1"][0][0]
                r_qg0, r_qg1 = B["qg0"][1][0], B["qg1"][1][0]
                wT0, wT1 = B["wT0"][0][0], B["wT1"][0][0]
                r_wT0, r_wT1 = B["wT0"][1][0], B["wT1"][1][0]
                TT, r_TT = B["TT"][0][0], B["TT"][1][0]
                u, r_u = B["u"][0][0], B["u"][1][0]

                def bc(slot):
                    return sc[:, slot, h0:h0 + 4].unsqueeze(2).to_broadcast([128, 4, 128])
                pt = bview(b0)
                for hh in range(4):
                    S.op("pe", lambda h: h.transpose(pt[:, hh * 128:(hh + 1) * 128], kT[:, h0 + hh, tc0:tc0 + 128], self.ident_b[:]), reads=[r_it, self.r_const], writes=[PR[b0]])
                pt4 = v4(pt[:, 0:512])
                yield
                S.op("dve", lambda h: h.tensor_tensor(out=kbg[:], in0=pt4, in1=bc(5), op=ALU.mult), reads=[PR[b0], r_sc], writes=[r_kbg])
                S.op("dve", lambda h: h.tensor_tensor(out=kg0[:], in0=pt4, in1=bc(7), op=ALU.mult), reads=[PR[b0], r_sc], writes=[r_kg0])
                S.op("dve", lambda h: h.tensor_tensor(out=kg1[:], in0=pt4, in1=bc(8), op=ALU.mult), reads=[PR[b0], r_sc], writes=[r_kg1])
                pt2 = bview(b1)
                for hh in range(4):
                    S.op("pe", lambda h: h.transpose(pt2[:, hh * 128:(hh + 1) * 128], vT[:, h0 + hh, tc0:tc0 + 128], self.ident_b[:]), reads=[r_it, self.r_const], writes=[PR[b1]])
                yield
                S.op("dve", lambda h: h.tensor_tensor(out=vb[:], in0=v4(pt2[:, 0:512]), in1=bc(0), op=ALU.mult), reads=[PR[b1], r_sc], writes=[r_vb])
                for hh in range(4):
                    S.op("pe", lambda h: h.matmul(PS[b0][:, hh * 128:(hh + 1) * 128], selall[0:8, (h0 + hh) * 128:(h0 + hh + 1) * 128], gcT[:], start=True, stop=True), reads=[r_gcT, r_c], writes=[PR[b0]])
                yield
                gcb = v4(PS[b0][:])
                negU4 = negmaskU.unsqueeze(1).to_broadcast([128, 4, 128])
                posL4 = posmaskL.unsqueeze(1).to_broadcast([128, 4, 128])
                S.op("dve", lambda h: h.tensor_tensor(out=dd[:], in0=gcb, in1=bc(2), op=ALU.subtract), reads=[PR[b0], r_sc], writes=[r_dd])
                S.op("act", lambda h: h.activation(out=egrow[:], in_=gcb, func=AF.Exp), reads=[PR[b0]], writes=[r_egrow])
                for hh in range(4):
                    S.op("pe", lambda h: h.matmul(PS[b1][:, hh * 128:(hh + 1) * 128], kT[:, h0 + hh, tc0:tc0 + 128], qT[:, h0 + hh, tc0:tc0 + 128], start=True, stop=True), reads=[r_it], writes=[PR[b1]])
                yield
                S.op("dve", lambda h: h.tensor_tensor(out=eT[:], in0=dd[:], in1=negU4, op=ALU.add), reads=[r_dd, r_c], writes=[r_eT])
                S.op("dve", lambda h: h.tensor_tensor(out=eL[:], in0=dd[:], in1=posL4, op=ALU.add), reads=[r_dd, r_c], writes=[r_eL])
                S.op("dve", lambda h: h.tensor_tensor(out=qg0[:, :, 0:64], in0=qT[:, h0:h0 + 4, tc0:tc0 + 64], in1=egrow[:, :, 0:64], op=ALU.mult), reads=[r_it, r_egrow], writes=[r_qg0])
                S.op("dve", lambda h: h.tensor_tensor(out=qg1[:, :, 64:128], in0=qT[:, h0:h0 + 4, tc0 + 64:tc0 + 128], in1=egrow[:, :, 64:128], op=ALU.mult), reads=[r_it, r_egrow], writes=[r_qg1])
                yield
                S.op("act", lambda h: h.activation(out=eT[:], in_=eT[:], func=AF.Exp), reads=[r_eT], writes=[r_eT])
                S.op("act", lambda h: h.activation(out=eL[:], in_=eL[:], func=AF.Exp, scale=-1.0), reads=[r_eL], writes=[r_eL])
                for hh in range(4):
                    S.op("pe", lambda h: h.matmul(PS[b0][:, hh * 128:(hh + 1) * 128], kT[:, h0 + hh, tc0:tc0 + 128], kT[:, h0 + hh, tc0:tc0 + 128], start=True, stop=True), reads=[r_it], writes=[PR[b0]])
                yield
                S.op("dve", lambda h: h.tensor_tensor(out=attnT[:], in0=v4(PS[b1][:]), in1=eT[:], op=ALU.mult), reads=[PR[b1], r_eT], writes=[r_attnT])
                S.op("dve", lambda h: h.tensor_tensor(out=eL[:], in0=eL[:], in1=bc(0), op=ALU.mult), reads=[r_eL, r_sc], writes=[r_eL])
                yield
                Lts, Lrs = B["Lp"]
                Nts, Nrs = B["Np"]
                Pts, Prs = B["Pp"]
                li = ni = pi_ = 0
                Lp, r_L = Lts[0], Lrs[0]
                S.op("dve", lambda h: h.tensor_tensor(out=Lp[:], in0=v4(PS[b0][:]), in1=eL[:], op=ALU.mult), reads=[PR[b0], r_eL], writes=[r_L])
                yield
                Np, r_N = Nts[0], Nrs[0]
                Pp, r_P = Pts[0], Prs[0]
                pta = bview(b1) if TD == BF16 else PS[b1]
                for hh in range(4):
                    S.op("pe", lambda h: h.transpose(pta[:, hh * 128:(hh + 1) * 128], Lp[:, hh, :], ident_td[:]), reads=[r_L, self.r_const], writes=[PR[b1]])
                yield
                S.op("act", lambda h: h.copy(Np[:], v4(pta[:, 0:512])), reads=[PR[b1]], writes=[r_N])
                S.op("dve", lambda h: h.tensor_tensor(out=Pp[:], in0=ident4[:], in1=v4(pta[:, 0:512]), op=ALU.subtract), reads=[PR[b1], r_c], writes=[r_P])
                yield
                for lvl in range(1, 6):
                    li ^= 1
                    L2, r_L2 = Lts[li], Lrs[li]
                    for hh in range(4):
                        S.op("pe", lambda h: h.matmul(PS[b0][:, hh * 128:(hh + 1) * 128], Np[:, hh, :], Lp[:, hh, :], start=True, stop=True), reads=[r_N, r_L], writes=[PR[b0]])
                    if lvl < 5:
                        ni ^= 1
                        N2, r_N2 = Nts[ni], Nrs[ni]
                        for hh in range(4):
                            S.op("pe", lambda h: h.matmul(PS[b1][:, hh * 128:(hh + 1) * 128], Lp[:, hh, :], Np[:, hh, :], start=True, stop=True), reads=[r_N, r_L], writes=[PR[b1]])
                    yield
                    S.op("act", lambda h: h.copy(L2[:], v4(PS[b0][:])), reads=[PR[b0]], writes=[r_L2])
                    if lvl < 5:
                        S.op("dve", lambda h: h.tensor_copy(N2[:], v4(PS[b1][:])), reads=[PR[b1]], writes=[r_N2])
                    yield
                    for hh in range(4):
                        S.op("pe", lambda h: h.matmul(PS[b0][:, hh * 128:(hh + 1) * 128], ident_td[:], Pp[:, hh, :], start=True, stop=False), reads=[self.r_const, r_P], writes=[PR[b0]])
                        S.op("pe", lambda h: h.matmul(PS[b0][:, hh * 128:(hh + 1) * 128], L2[:, hh, :], Pp[:, hh, :], start=False, stop=True), reads=[r_L2, r_P], writes=[PR[b0]])
                    yield
                    if lvl < 5:
                        pi_ ^= 1
                        P2, r_P2 = Pts[pi_], Prs[pi_]
                        S.op("dve" if lvl % 2 else "act", (lambda h: h.tensor_copy(P2[:], v4(PS[b0][:]))) if lvl % 2 else (lambda h: h.copy(P2[:], v4(PS[b0][:]))), reads=[PR[b0]], writes=[r_P2])
                        Np, r_N = N2, r_N2
                        Pp, r_P = P2, r_P2
                    else:
                        S.op("act", lambda h: h.copy(TT[:], v4(PS[b0][:])), reads=[PR[b0]], writes=[r_TT])
                    Lp, r_L = L2, r_L2
                    yield
                for hh in range(4):
                    S.op("pe", lambda h: h.matmul(PS[b0][:, hh * 128:(hh + 1) * 128], kbg[:, hh, :], TT[:, hh, :], start=True, stop=True), reads=[r_kbg, r_TT], writes=[PR[b0]])
                for hh in range(4):
                    S.op("pe", lambda h: h.matmul(PS[b1][:, hh * 128:(hh + 1) * 128], TT[:, hh, :], vb[:, hh, :], start=True, stop=True), reads=[r_vb, r_TT], writes=[PR[b1]])
                yield
                S.op("act", lambda h: h.copy(wT0[:, :, 0:64], v4(PS[b0][:])[:, :, 0:64]), reads=[PR[b0]], writes=[r_wT0])
                S.op("act", lambda h: h.copy(wT1[:, :, 64:128], v4(PS[b0][:])[:, :, 64:128]), reads=[PR[b0]], writes=[r_wT1])
                S.op("dve", lambda h: h.tensor_copy(u[:], v4(PS[b1][:])), reads=[PR[b1]], writes=[r_u])
                yield

            def gen_rec(g, it, r_it, ogT, r_ogT):
                gateT = it[:, 3]
                h0 = g * 4
                bw, bd, bo = 3 * g, 3 * g + 1, 3 * g + 2
                for tt in range(NTT):
                    ci = tt * 2 + g
                    B = CB[ci]
                    tc0 = tt * 128
                    u, r_u = B["u"][0][0], B["u"][1][0]
                    vnew, r_vnew = B["vnew"][0][0], B["vnew"][1][0]
                    egrow, r_egrow = B["egrow"][0][0], B["egrow"][1][0]
                    attnT, r_attnT = B["attnT"][0][0], B["attnT"][1][0]
                    for c in range(2):
                        cs = slice(c * 64, (c + 1) * 64)
                        Sb_in, r_Sb_in = Sb[c][:, h0:h0 + 4, :], r_Sb[c][g]
                        Sb_out, r_Sb_out = Sb[1 - c][:, h0:h0 + 4, :], r_Sb[1 - c][g]
                        wTc, r_wTc = (B["wT0"][0][0], B["wT0"][1][0]) if c == 0 else (B["wT1"][0][0], B["wT1"][1][0])
                        kgc, r_kgc = (B["kg0"][0][0], B["kg0"][1][0]) if c == 0 else (B["kg1"][0][0], B["kg1"][1][0])
                        for hh in range(4):
                            S.op("pe", lambda h: h.matmul(PS[bw][:, hh * 128:(hh + 1) * 128], wTc[:, hh, :], Sb_in[:, hh, :], start=True, stop=True), reads=[r_wTc, r_Sb_in], writes=[PR[bw]])
                        yield
                        S.op("dve", lambda h: h.tensor_tensor(out=vnew[cs, :, :], in0=u[cs, :, :], in1=v4(PS[bw][:])[cs, :, :], op=ALU.subtract), reads=[PR[bw], r_u], writes=[r_vnew])
                        yield
                        if c == 1:
                            Sb0 = Sb[0][:, h0:h0 + 4, :]
                            qg0, qg1 = B["qg0"][0][0], B["qg1"][0][0]
                            for hh in range(4):
                                oc = PS[bo][:, hh * 128:(hh + 1) * 128]
                                S.op("pe", lambda h: h.matmul(oc, qg0[:, hh, :], Sb0[:, hh, :], start=True, stop=False), reads=[B["qg0"][1][0], r_Sb[0][g]], writes=[PR[bo]])
                                S.op("pe", lambda h: h.matmul(oc, qg1[:, hh, :], Sb_in[:, hh, :], start=False, stop=False), reads=[B["qg1"][1][0], r_Sb_in], writes=[PR[bo]])
                                S.op("pe", lambda h: h.matmul(oc, attnT[:, hh, :], vnew[:, hh, :], start=False, stop=True), reads=[r_attnT, r_vnew], writes=[PR[bo]])
                        for hh in range(4):
                            S.op("pe", lambda h: h.matmul(PS[bd][:, hh * 128:(hh + 1) * 128], kgc[:, hh, :], vnew[:, hh, :], start=True, stop=True), reads=[r_kgc, r_vnew], writes=[PR[bd]])
                        tmpS, r_tmpS = tmpS_pool.get()
                        col = 63 if c == 0 else 127
                        eglb = egrow[:, :, col:col + 1].to_broadcast([128, 4, 128])
                        S.op("dve", lambda h: h.tensor_tensor(out=tmpS[:], in0=Sf[:, h0:h0 + 4, :], in1=eglb, op=ALU.mult), reads=[r_Sf[g], r_egrow], writes=[r_tmpS])
                        yield
                        S.op("dve", lambda h: h.tensor_tensor(out=Sf[:, h0:h0 + 4, :], in0=tmpS[:], in1=v4(PS[bd][:]), op=ALU.add), reads=[PR[bd], r_tmpS], writes=[r_Sf[g]])
                        yield
                        S.op("act", lambda h: h.copy(Sb_out, Sf[:, h0:h0 + 4, :]), reads=[r_Sf[g]], writes=[r_Sb_out])
                        yield
                    osq, r_osq = osq_pool.get()
                    on, r_on = on_pool.get()
                    stt_, r_stt = pools["stat"].get()
                    o4 = v4(PS[bo][:])
                    if "dbg_o" in self.dump:
                        S.op("act", lambda h: h.copy(osq[:], o4), reads=[PR[bo]], writes=[r_osq])
                        r0 = self._cur_b * TB + tt * 128
                        S.dma("sp", lambda h: h.dma_start(out=self.dbg_o[r0:r0 + 128, h0 * 128:(h0 + 4) * 128], in_=osq[:].rearrange("p a b -> p (a b)")), reads=[r_osq])
                    S.op("act", lambda h: h.activation(out=osq[:], in_=o4, func=AF.Square), reads=[PR[bo]], writes=[r_osq])
                    S.op("dve", lambda h: h.tensor_reduce(out=stt_[:, 0:4], in_=osq[:], axis=mybir.AxisListType.X, op=ALU.add), reads=[r_osq], writes=[r_stt])
                    yield
                    self.rms_rstd(stt_[:, 0:4], stt_[:, 0:4], r_stt, r_stt, HD)
                    rsb = stt_[:, 0:4].unsqueeze(2).to_broadcast([128, 4, 128])
                    ogb = ogain[:].unsqueeze(1).to_broadcast([128, 4, 128])
                    yield
                    S.op("dve", lambda h: h.tensor_tensor(out=osq[:], in0=o4, in1=rsb, op=ALU.mult), reads=[PR[bo], r_stt], writes=[r_osq])
                    S.op("dve", lambda h: h.tensor_tensor(out=on[:], in0=osq[:], in1=ogb, op=ALU.mult), reads=[r_osq, r_c], writes=[r_on])
                    yield
                    pt = bview(bw)
                    for hh in range(4):
                        S.op("pe", lambda h: h.transpose(pt[:, hh * 128:(hh + 1) * 128], on[:, hh, :], self.ident_b[:]), reads=[r_on, self.r_const], writes=[PR[bw]])
                    yield
                    S.op("dve", lambda h: h.tensor_tensor(out=ogT[:, h0:h0 + 4, tc0:tc0 + 128], in0=v4(pt[:, 0:512]), in1=gateT[:, h0:h0 + 4, tc0:tc0 + 128], op=ALU.mult), reads=[PR[bw], r_it], writes=[r_ogT])
                    yield

            def run_gens(gens, stagger=3):
                pend = list(gens)
                act_ = []
                rnd = 0
                while pend or act_:
                    if pend and rnd % stagger == 0:
                        act_.append(pend.pop(0))
                    rnd += 1
                    for gn in list(act_):
                        try:
                            next(gn)
                        except StopIteration:
                            act_.remove(gn)

            def gen_wout(b, ogT, r_ogT):
                for tt in range(NTT):
                    r0 = b * TB + tt * 128
                    y, r_y = y_pool.get()
                    ss, r_ss = ss_pool.get()
                    for half in range(2):
                        pi = 6 + half
                        pd = PS[pi]
                        for hh in range(8):
                            S.op("pe", lambda h: h.matmul(pd[:], ogT[:, hh, tt * 128:(tt + 1) * 128], wout[:, hh, half * 512:(half + 1) * 512], start=(hh == 0), stop=(hh == 7)),
                                 reads=[r_ogT, r_wout], writes=[PR[pi]])
                        yield
                        jk, r_jk = pools["junk"].get()
                        S.op("act", lambda h: h.activation(out=jk[:, 0:512], in_=pd[:], func=AF.Square, accum_out=ss[:, half:half + 1]),
                             reads=[PR[pi]], writes=[r_jk, r_ss])
                        yield
                        S.op("dve", lambda h: h.tensor_copy(y[:, half * 512:(half + 1) * 512], pd[:]), reads=[PR[pi]], writes=[r_y])
                        yield
                    xr, r_xr = xt_pool.get()
                    S.dma("sp", lambda h: h.dma_start(out=xr[:], in_=self.x[r0:r0 + 128, :]), writes=[r_xr])
                    st_, r_st = pools["stat"].get()
                    S.op("dve", lambda h: h.tensor_tensor(out=st_[:, 0:1], in0=ss[:, 0:1], in1=ss[:, 1:2], op=ALU.add), reads=[r_ss], writes=[r_st])
                    yield
                    S.op("act", lambda h: h.activation(out=st_[:, 1:2], in_=st_[:, 0:1], func=AF.Ln, bias=self.eps_col[:, 0:1], scale=1.0 / D), reads=[r_st, self.r_const], writes=[r_st])
                    yield
                    S.op("act", lambda h: h.activation(out=st_[:, 1:2], in_=st_[:, 1:2], func=AF.Exp, scale=-0.5), reads=[r_st], writes=[r_st])
                    yield
                    S.op("dve", lambda h: h.scalar_tensor_tensor(out=y[:], in0=y[:], scalar=st_[:, 1:2], in1=gpost[:], op0=ALU.mult, op1=ALU.mult), reads=[r_y, r_st, r_gpost], writes=[r_y])
                    yield
                    S.op("dve", lambda h: h.tensor_tensor(out=y[:], in0=y[:], in1=xr[:], op=ALU.add), reads=[r_y, r_xr], writes=[r_y])
                    yield
                    S.dma("sp", lambda h: h.dma_start(out=self.xm[r0:r0 + 128, :], in_=y[:]), reads=[r_y])

            prev_wout = None
            for b in range(nblk):
                self._cur_b = b
                it, r_it = in_pool.get()
                for i_ in range(4):
                    S.dma("sp", lambda h: h.dma_start(out=it[:, i_], in_=self.qkvg[i_, :, :, b * TB:(b + 1) * TB]), writes=[r_it])
                scs = []
                for tt in range(NTT):
                    sc, r_sc = sc_pool.get()
                    gcT, r_gcT = gcT_pool.get()
                    ti = b * NTT + tt
                    S.dma("sp", lambda h: h.dma_start(out=sc[:], in_=self.scd[ti]), writes=[r_sc])
                    S.dma("sp", lambda h: h.dma_start(out=gcT[:], in_=self.gctd[ti]), writes=[r_gcT])
                    scs.append((sc, r_sc, gcT, r_gcT))
                gens = [gen_pre(tt * 2 + g, tt, g, it, r_it, *scs[tt]) for tt in range(NTT) for g in range(2)]
                import os
                run_gens(gens, stagger=int(os.environ.get("STG", "2")))
                ogT, r_ogT = ogT_pool.get()
                rgens = [gen_rec(g, it, r_it, ogT, r_ogT) for g in range(2)]
                if prev_wout is not None:
                    rgens.append(prev_wout)
                run_gens(rgens, stagger=1)
                prev_wout = gen_wout(b, ogT, r_ogT)
            run_gens([prev_wout])

    def phase_C(self):
        nc, S = self.nc, self.S
        T = self.T
        TB = 512 if T % 512 == 0 else 256
        NTT = TB // 128
        nblk = T // TB
        self._tpi = 0
        st = ExitStack()
        with st:
            wkv, r_wkv = self.load_w_kc(st, "wkv", self.w_kv, 2 * D)
            gkv, r_gkv = self.load_gain_bc(st, "gkv", 8)
            pools = {
                "junk": Pool_(nc, st, "junkC", [128, D], BF16, 1),
                "stat": Pool_(nc, st, "statC", [128, 4], F32, 8),
                "hb": Pool_(nc, st, "hbC", [128, D], BF16, 2),
                "tp_banks": [0, 1],
            }
            xt_pool = Pool_(nc, st, "xtC", [128, D], F32, 3)
            hT_pool = Pool_(nc, st, "hTC", [128, 8, TB], BF16, 2)
            kts_pool = Pool_(nc, st, "ktsC", [128, NTT, H, 128], BF16, 2)
            vs_pool = Pool_(nc, st, "vsC", [128, H, 128], BF16, 3)
            PS, PR = self.psb, self.psr
            bi = 0
            def do_norm(b):
                hT, r_hT = hT_pool.get()
                for tt in range(NTT):
                    r0 = b * TB + tt * 128
                    xt, r_xt = xt_pool.get()
                    S.dma("sp", lambda h, xt=xt, r0=r0: h.dma_start(out=xt[:], in_=self.x1[r0:r0 + 128, :]), writes=[r_xt])
                    self.norm_to_hT(xt, r_xt, gkv, r_gkv, hT, r_hT, tt * 128, pools)
                return hT, r_hT

            nxt = do_norm(0)
            for b in range(nblk):
                hT, r_hT = nxt
                kts, r_kts = kts_pool.get()
                for hh in range(H):
                    pi = 2 + (bi % 3)
                    bi += 1
                    pk = PS[pi][:, 0:TB]
                    for kc in range(8):
                        S.op("pe", lambda h, kc=kc: h.matmul(pk, wkv[:, kc, hh * 128:(hh + 1) * 128], hT[:, kc, :], start=(kc == 0), stop=(kc == 7)),
                             reads=[r_wkv.at(kc, hh * 128), r_hT], writes=[PR[pi]])
                    src = pk.rearrange("p (a b) -> p a b", a=NTT)
                    if hh % 2 == 0:
                        S.op("act", lambda h: h.copy(kts[:, :, hh, :], src), reads=[PR[pi]], writes=[r_kts])
                    else:
                        S.op("dve", lambda h: h.tensor_copy(kts[:, :, hh, :], src), reads=[PR[pi]], writes=[r_kts])
                if b + 1 < nblk:
                    nxt = do_norm(b + 1)
                kb0 = b * NTT
                S.dma("sp", lambda h: h.dma_start(out=self.KT[kb0:kb0 + NTT].rearrange("k p h s -> p k (h s)"), in_=kts[:].rearrange("p k h s -> p k (h s)")), reads=[r_kts])
                for tt in range(NTT):
                    vs, r_vs = vs_pool.get()
                    for half in range(2):
                        pi = 5 + (bi % 3)
                        bi += 1
                        pv = PS[pi]
                        for kc in range(8):
                            S.op("pe", lambda h, kc=kc: h.matmul(pv[:], hT[:, kc, tt * 128:(tt + 1) * 128], wkv[:, kc, D + half * 512:D + (half + 1) * 512], start=(kc == 0), stop=(kc == 7)),
                                 reads=[r_wkv.at(kc, D + half * 512), r_hT], writes=[PR[pi]])
                        dst = vs[:, half * 4:(half + 1) * 4, :]
                        src = pv[:].rearrange("p (a b) -> p a b", a=4)
                        if half == 0:
                            S.op("act", lambda h: h.copy(dst, src), reads=[PR[pi]], writes=[r_vs])
                        else:
                            S.op("dve", lambda h: h.tensor_copy(dst, src), reads=[PR[pi]], writes=[r_vs])
                    kb = b * NTT + tt
                    S.dma("sp", lambda h: h.dma_start(out=self.Vd[kb], in_=vs[:]), reads=[r_vs])

    def phase_D(self):
        nc, S = self.nc, self.S
        NQB = self.NQB
        self._tpi = 0
        st = ExitStack()
        with st:
            def sb(name, shape, dt):
                return st.enter_context(nc.sbuf_tensor(name, shape, dt))
            wq, r_wq = self.load_w_kc(st, "wq", self.w_q, D)
            wo, r_wo = self.load_w_kc(st, "wo", self.w_o, D)
            gpre, r_gpre = self.load_gain_bc(st, "gpreD", 1)
            gpost, r_gpost = self.load_gain_bc(st, "gpostD", 3)
            r_c = Res("constD")
            triI = sb("triI", [128, 128], BF16)
            triC = sb("triC", [128, 128], BF16)
            S.dma("pool", lambda h: h.dma_start(out=triI[:], in_=self.consts[:, 6, :]), writes=[r_c])
            S.dma("pool", lambda h: h.dma_start(out=triC[:], in_=self.consts[:, 7, :]), writes=[r_c])
            amask = sb("amask_sb", [128, 4, 128], F32)
            S.dma("sp", lambda h: h.dma_start(out=amask[:], in_=self.amask[:, :, :]), writes=[r_c])
            sel = sb("selD", [128, 4], F32)
            S.dma("sp", lambda h: h.dma_start(out=sel[:], in_=self.qsel[:, :]), writes=[r_c])
            pools = {
                "junk": Pool_(nc, st, "junkD", [128, D], BF16, 1),
                "stat": Pool_(nc, st, "statD", [128, 4], F32, 8),
                "hb": Pool_(nc, st, "hbD", [128, D], BF16, 2),
                "tp_banks": [7],
            }
            xl_pool = Pool_(nc, st, "xlD", [128, D], F32, 2)
            xq_pool = Pool_(nc, st, "xqD", [128, D], F32, 2)
            y_pool = Pool_(nc, st, "yD", [128, D], F32, 2)
            ss_pool = Pool_(nc, st, "ssD", [128, 2], F32, 4)
            hT_pool = Pool_(nc, st, "hTD", [128, 8, 128], BF16, 2)
            QT_pool = Pool_(nc, st, "QT", [128, H, 128], BF16, 2)
            kt_pool = Pool_(nc, st, "ktD", [128, H, 128], BF16, 7)
            v_pool = Pool_(nc, st, "vD", [128, H, 128], BF16, 7)
            e_pool = Pool_(nc, st, "eD", [128, 4, 128], F32, 12)
            sp_pool = Pool_(nc, st, "spD", [128, 4, 128], BF16, 12)
            eg_pool = Pool_(nc, st, "egD", [128, 4, 128], F32, 12)
            a_pool = Pool_(nc, st, "aD", [128, 4, 128], BF16, 12)
            osb_pool = Pool_(nc, st, "osbD", [128, H, 128], BF16, 2)
            oT_pool = Pool_(nc, st, "oTD", [128, H, 128], BF16, 2)
            PS, PR = self.psb, self.psr
            ZB, AB, OB = [0, 1, 6], [2, 3], [4, 5]

            def v4(ap):
                return ap.rearrange("p (a b) -> p a b", a=4)

            QTs = {}
            xqs = {}

            def gen_pro(m):
                xq, r_xq = xq_pool.get()
                xqs[m] = (xq, r_xq)
                for j in range(4):
                    xl, r_xl = xl_pool.get()
                    r0 = (4 * m + j) * 128
                    S.dma("sp", lambda h: h.dma_start(out=xl[:], in_=self.x1[r0:r0 + 128, :]), writes=[r_xl])
                    if j == 0:
                        S.op("dve", lambda h: h.tensor_scalar(out=xq[:], in0=xl[:], scalar1=sel[:, 0:1], scalar2=None, op0=ALU.mult), reads=[r_xl, r_c], writes=[r_xq])
                    else:
                        S.op("dve", lambda h: h.scalar_tensor_tensor(out=xq[:], in0=xl[:], scalar=sel[:, j:j + 1], in1=xq[:], op0=ALU.mult, op1=ALU.add), reads=[r_xl, r_c, r_xq], writes=[r_xq])
                    yield
                hT, r_hT = hT_pool.get()
                self.norm_to_hT(xq, r_xq, gpre, r_gpre, hT, r_hT, 0, pools)
                yield
                QT, r_QT = QT_pool.get()
                QTs[m] = (QT, r_QT)
                for g in range(2):
                    for hh in range(4):
                        hd = g * 4 + hh
                        for kc in range(8):
                            S.op("pe", lambda h: h.matmul(PS[7][:, hh * 128:(hh + 1) * 128], wq[:, kc, hd * 128:(hd + 1) * 128], hT[:, kc, :], start=(kc == 0), stop=(kc == 7)),
                                 reads=[r_wq, r_hT], writes=[PR[7]])
                    S.op("act", lambda h: h.activation(out=QT[:, g * 4:(g + 1) * 4, :], in_=v4(PS[7][:]), func=AF.Copy, scale=HD ** -0.5), reads=[PR[7]], writes=[r_QT])
                    yield

            def gen_epi(m, osb, r_osb):
                xq, r_xq = xqs[m]
                oT, r_oT = oT_pool.get()
                for half in range(2):
                    pt = PS[7].bitcast(BF16)
                    for j in range(4):
                        c = half * 4 + j
                        S.op("pe", lambda h: h.transpose(pt[:, j * 128:(j + 1) * 128], osb[:, c, :], self.ident_b[:]), reads=[r_osb, self.r_const], writes=[PR[7]])
                    S.op("act", lambda h: h.copy(oT[:, half * 4:half * 4 + 4, :], v4(pt[:, 0:512])), reads=[PR[7]], writes=[r_oT])
                    yield
                y, r_y = y_pool.get()
                ss, r_ss = ss_pool.get()
                for half in range(2):
                    pd = PS[7]
                    for c in range(8):
                        S.op("pe", lambda h: h.matmul(pd[:], oT[:, c, :], wo[:, c, half * 512:(half + 1) * 512], start=(c == 0), stop=(c == 7)), reads=[r_oT, r_wo], writes=[PR[7]])
                    jk, r_jk = pools["junk"].get()
                    S.op("act", lambda h: h.activation(out=jk[:, 0:512], in_=pd[:], func=AF.Square, accum_out=ss[:, half:half + 1]), reads=[PR[7]], writes=[r_jk, r_ss])
                    S.op("dve", lambda h: h.tensor_copy(y[:, half * 512:(half + 1) * 512], pd[:]), reads=[PR[7]], writes=[r_y])
                    yield
                self.post_norm_residual(y, r_y, ss, r_ss, gpost, r_gpost, xq, r_xq, pools)
                S.dma("sp", lambda h: h.dma_start(out=self.xm2[m * 128:(m + 1) * 128, :], in_=y[:]), reads=[r_y])

            def gen_side(epi, pro):
                if epi is not None:
                    yield from epi
                if pro is not None:
                    yield from pro

            for _ in gen_pro(0):
                pass
            pend_epi = None
            for m in range(NQB):
                QT, r_QT = QTs[m]
                for g in range(2):
                    S.op("dve", lambda h: h.memset(PS[OB[g]][:], 0.0), writes=[PR[OB[g]]])
                nkb = 4 * m + 4
                kv_tiles = {}

                def get_kv(kb):
                    if kb not in kv_tiles:
                        kt, r_kt = kt_pool.get()
                        vv, r_vv = v_pool.get()
                        S.dma("sp", lambda h: h.dma_start(out=kt[:], in_=self.KT[kb]), writes=[r_kt])
                        S.dma("sp", lambda h: h.dma_start(out=vv[:], in_=self.Vd[kb]), writes=[r_vv])
                        kv_tiles[kb] = (kt, r_kt, vv, r_vv)
                    return kv_tiles[kb]

                accseq = [nkb - 1, nkb - 1]

                def gen_unit(kidx, g, m=m, QT=QT, r_QT=r_QT, nkb=nkb):
                    kb = nkb - 1 - kidx
                    kt, r_kt, vv, r_vv = get_kv(kb)
                    zb = free_z.pop(0)
                    for hh in range(4):
                        hd = g * 4 + hh
                        S.op("pe", lambda h: h.matmul(PS[zb][:, hh * 128:(hh + 1) * 128], kt[:, hd, :], QT[:, hd, :], start=True, stop=True), reads=[r_kt, r_QT], writes=[PR[zb]])
                    yield
                    e_, r_e = e_pool.get()
                    S.op("act", lambda h: h.activation(out=e_[:], in_=v4(PS[zb][:]), func=AF.Exp), reads=[PR[zb]], writes=[r_e])
                    free_z.append(zb)
                    yield
                    if kb >= 4 * m:
                        mk = amask[:, kb - 4 * m, :].unsqueeze(1).to_broadcast([128, 4, 128])
                        S.op("dve", lambda h: h.tensor_tensor(out=e_[:], in0=e_[:], in1=mk, op=ALU.mult), reads=[r_e, r_c], writes=[r_e])
                        yield
                    sp, r_sp = sp_pool.get()
                    S.op("act", lambda h: h.activation(out=sp[:], in_=e_[:], func=AF.Ln, bias=self.one_col[:, 0:1], scale=1.0), reads=[r_e, self.r_const], writes=[r_sp])
                    yield
                    while accseq[g] != kb:
                        yield
                    S.op("pe", lambda h: h.matmul(PS[AB[g]][:], triI[:], sp[:].rearrange("p a b -> p (a b)"), start=(kidx == 0), stop=True, skip_group_check=True), reads=[r_sp, r_c], writes=[PR[AB[g]]])
                    yield
                    eg, r_eg = eg_pool.get()
                    S.op("act", lambda h: h.activation(out=eg[:], in_=v4(PS[AB[g]][:]), func=AF.Exp, scale=-1.0), reads=[PR[AB[g]]], writes=[r_eg])
                    S.op("pe", lambda h: h.matmul(PS[AB[g]][:], triC[:], sp[:].rearrange("p a b -> p (a b)"), start=False, stop=True, skip_group_check=True), reads=[r_sp, r_c], writes=[PR[AB[g]]])
                    accseq[g] = kb - 1
                    yield
                    a_, r_a = a_pool.get()
                    S.op("dve", lambda h: h.tensor_tensor(out=a_[:], in0=e_[:], in1=eg[:], op=ALU.mult), reads=[r_e, r_eg], writes=[r_a])
                    yield
                    for hh in range(4):
                        hd = g * 4 + hh
                        S.op("pe", lambda h: h.matmul(PS[OB[g]][:, hh * 128:(hh + 1) * 128], a_[:, hh, :], vv[:, hd, :], start=False, stop=(kidx == nkb - 1), skip_group_check=True), reads=[r_a, r_vv], writes=[PR[OB[g]]])

                free_z = list(ZB)
                pending = [(kidx, g) for kidx in range(nkb) for g in range(2)]
                active = []
                side = gen_side(pend_epi, gen_pro(m + 1) if m + 1 < NQB else None)
                side_done = False
                while pending or active or not side_done:
                    while pending and free_z and len(active) < 10:
                        active.append(gen_unit(*pending.pop(0)))
                        next(active[-1])
                    for gn in list(active):
                        try:
                            next(gn)
                        except StopIteration:
                            active.remove(gn)
                    if not side_done:
                        try:
                            next(side)
                        except StopIteration:
                            side_done = True
                osb, r_osb = osb_pool.get()
                S.op("act", lambda h: h.copy(osb[:, 0:4, :], v4(PS[OB[0]][:])), reads=[PR[OB[0]]], writes=[r_osb])
                S.op("dve", lambda h: h.tensor_copy(osb[:, 4:8, :], v4(PS[OB[1]][:])), reads=[PR[OB[1]]], writes=[r_osb])
                pend_epi = gen_epi(m, osb, r_osb)
            for _ in pend_epi:
                pass


def make_consts():
    c = np.zeros((128, 8, 128), np.float32)
    p = np.arange(128)[:, None]
    f = np.arange(128)[None, :]
    same = (p // 64) == (f // 64)
    c[:, 0, :] = np.eye(128, dtype=np.float32)
    c[:, 1, :] = (same & (p <= f)).astype(np.float32)
    c[:, 2, :] = same.astype(np.float32)
    c[:, 3, :] = np.where(same & (f >= p), 0.0, -1e5)
    c[:, 4, :] = np.where(same & (p > f), 0.0, 1e5)
    c[:, 5, :] = 1.0
    c[:, 6, :] = (p >= f).astype(np.float32)
    c[:, 7, :] = (p < f).astype(np.float32)
    return c


def make_amask(r):
    a = np.zeros((128, 4, 128), np.float32)
    s_ = np.arange(128)[:, None]
    t_ = np.arange(128)[None, :]
    for j in range(4):
        if j < r:
            a[:, j, :] = 1.0
        elif j == r:
            a[:, j, :] = (s_ < t_).astype(np.float32)
    return a


def make_qsel(r):
    q = np.zeros((128, 4), np.float32)
    q[:, r] = 1.0
    return q


def make_selall():
    sel = np.zeros((8, 1024), np.float32)
    for hh in range(8):
        sel[hh, hh * 128:(hh + 1) * 128] = 1
    return sel


_NC_CACHE = {}


def _get_nc(T, NQB):
    key = (T, NQB)
    if key not in _NC_CACHE:
        kb = KB(T, NQB, phases="ABCDE")
        kb.chain_dt = BF16
        _NC_CACHE[key] = kb.build()
    return _NC_CACHE[key]


def kernel(x, mix_pre_gain, mix_post_gain, mlp_pre_gain, mlp_post_gain, mlp_w_up, mlp_w_down,
           gdn_w_in, gdn_conv_w, gdn_a_log, gdn_dt_bias, gdn_out_gain, gdn_w_out,
           kv_gain, w_kv, sb_w_q, sb_w_o):
    f = lambda a: np.ascontiguousarray(np.asarray(a), dtype=np.float32)
    x = f(x)
    B, T, _ = x.shape
    n_cores = 8
    per_b = n_cores // B
    NQB = T // 128 // per_b
    gains = f(np.concatenate([f(mix_pre_gain), f(mix_post_gain), f(mlp_pre_gain), f(mlp_post_gain), f(kv_gain)[None]], 0))
    conv_w = f(gdn_conv_w)[0]
    common = {
        "gains": gains,
        "mlp_w_up": f(mlp_w_up),
        "mlp_w_down": f(mlp_w_down),
        "gdn_w_in": f(gdn_w_in)[0],
        "gdn_w_out": f(gdn_w_out)[0],
        "w_kv": f(w_kv),
        "sb_w_q": f(sb_w_q)[0],
        "sb_w_o": f(sb_w_o)[0],
        "consts": make_consts(),
        "convw_l": f(conv_w.reshape(4, 24, 128).transpose(2, 1, 0)),
        "small_bc": f(np.broadcast_to(np.stack([f(gdn_a_log)[0], f(gdn_dt_bias)[0]])[None], (128, 2, 8))),
        "ogain_bc": f(np.broadcast_to(f(gdn_out_gain)[0][None], (128, 128))),
        "selall": make_selall(),
    }
    in_maps = []
    for c in range(n_cores):
        b, r = c // per_b, c % per_b
        d = dict(common)
        d["x"] = x[b]
        d["qsel"] = make_qsel(r)
        d["amask"] = make_amask(r)
        in_maps.append(d)
    nc = _get_nc(T, NQB)
    res = run_bass_kernel_spmd(nc, in_maps, core_ids=list(range(n_cores)))
    out = np.zeros((B, T, D), np.float32)
    for c in range(n_cores):
        b, r = c // per_b, c % per_b
        o = res.results[c]["out"]
        for m in range(NQB):
            i = 4 * m + r
            out[b, i * 128:(i + 1) * 128, :] = o[m * 128:(m + 1) * 128, :]
    return out
```

```python
import numpy as np
from contextlib import ExitStack
import concourse.bass as bass
import concourse.mybir as mybir
from concourse.bass_utils import run_bass_kernel_spmd

F32 = mybir.dt.float32
BF16 = mybir.dt.bfloat16
I32 = mybir.dt.int32
U32 = mybir.dt.uint32
AF = mybir.ActivationFunctionType
ALU = mybir.AluOpType

D = 1024
H = 8
HD = 128
FF = 4096
EPS = 1e-6
GIN = 4 * D + 2 * H


class Res:
    __slots__ = ("name", "writer", "readers", "excl")

    def __init__(self, name="", excl=False):
        self.name = name
        self.writer = None
        self.readers = []
        self.excl = excl


class Tok:
    __slots__ = ("sem", "val", "knows", "eng")

    def __init__(self, sem, val, knows, eng):
        self.sem = sem
        self.val = val
        self.knows = knows
        self.eng = eng


class _Rec:
    def __init__(self):
        self.call = None

    def __getattr__(self, name):
        def f(*a, **k):
            self.call = (name, a, k)
            return self
        return f


def _record(fn):
    if fn is None:
        return None
    r = _Rec()
    fn(r)
    assert r.call is not None
    return r.call


class Eng:
    def __init__(self, name, sem):
        self.name = name
        self.sem = sem
        self.count = 0
        self.known = {}
        self.ops = []


class Sched:
    DMA_NS = 8

    def __init__(self, nc, stack):
        self.nc = nc
        self.engs = {}
        for n in ("pe", "act", "dve", "pool", "sp"):
            sem = stack.enter_context(nc.semaphore("s_" + n))
            self.engs[n] = Eng(n, sem)
        self.dsem = {}
        for q in ("sp", "pool", "act"):
            sems = [stack.enter_context(nc.semaphore("d_%s%d" % (q, i))) for i in range(self.DMA_NS)]
            self.dsem[q] = {"sems": sems, "uses": [0] * self.DMA_NS, "last": [None] * self.DMA_NS, "i": 0}
        self.nwaits = 0

    def _merge(self, known, tok):
        for s, v in tok.knows.items():
            if known.get(s, 0) < v:
                known[s] = v
        if known.get(tok.sem, 0) < tok.val:
            known[tok.sem] = tok.val

    def _deps(self, eng, reads, writes, extra=()):
        e = self.engs[eng]
        deps = list(extra)
        for r in reads:
            if r.writer is not None:
                deps.append((r.writer, True))
        for w in writes:
            if w.writer is not None:
                deps.append((w.writer, False))
            for t in w.readers:
                deps.append((t, False))
        best = {}
        for t, raw in deps:
            if t is None:
                continue
            if t.eng == eng and not raw:
                continue
            if e.known.get(t.sem, 0) >= t.val:
                continue
            k = id(t.sem)
            if k not in best or best[k][1] < t.val:
                best[k] = (t.sem, t.val)
            self._merge(e.known, t)
        waits = list(best.values())
        self.nwaits += len(waits)
        return waits

    def _finish(self, tok, reads, writes):
        for r in reads:
            r.readers.append(tok)
        for w in writes:
            w.writer = tok
            w.readers = []

    @staticmethod
    def _flat(xs):
        out = []
        for x in xs:
            if isinstance(x, (list, tuple)):
                out.extend(Sched._flat(x))
            else:
                out.append(x)
        return out

    def op(self, eng, fn, reads=(), writes=()):
        reads = self._flat(reads)
        writes = self._flat(writes)
        if any(r.excl for r in reads):
            writes = list(writes) + [r for r in reads if r.excl]
            reads = [r for r in reads if not r.excl]
        e = self.engs[eng]
        waits = self._deps(eng, reads, writes)
        e.count += 1
        tok = Tok(e.sem, e.count, dict(e.known), eng)
        e.ops.append((waits, _record(fn), (e.sem, 1)))
        self._finish(tok, reads, writes)
        return tok

    def dma(self, q, fn, reads=(), writes=()):
        reads = self._flat(reads)
        writes = self._flat(writes)
        e = self.engs[q]
        d = self.dsem[q]
        k = d["i"] % self.DMA_NS
        d["i"] += 1
        extra = [(d["last"][k], True)] if d["last"][k] is not None else []
        waits = self._deps(q, reads, writes, extra)
        d["uses"][k] += 1
        tok = Tok(d["sems"][k], 16 * d["uses"][k], dict(e.known), "dma_" + q)
        d["last"][k] = tok
        e.ops.append((waits, _record(fn), (d["sems"][k], 16)))
        self._finish(tok, reads, writes)
        return tok

    def barrier(self):
        toks = []
        for n, e in self.engs.items():
            if e.count:
                toks.append(Tok(e.sem, e.count, {}, n))
        for q, d in self.dsem.items():
            for t in d["last"]:
                if t is not None:
                    toks.append(t)
        for n, e in self.engs.items():
            waits = []
            for t in toks:
                if t.eng == n:
                    continue
                if e.known.get(t.sem, 0) >= t.val:
                    continue
                waits.append((t.sem, t.val))
                e.known[t.sem] = t.val
            if waits:
                e.ops.append((waits, None, None))

    def emit(self):
        nc = self.nc
        handles = {"pe": "tensor", "act": "scalar", "dve": "vector", "pool": "gpsimd", "sp": "sync"}
        with nc.Block() as block:
            for n, attr in handles.items():
                e = self.engs[n]

                def body(h, e=e):
                    for waits, fn, inc in e.ops:
                        for s, v in waits:
                            h.wait_ge(s, v)
                        if fn is not None:
                            getattr(h, fn[0])(*fn[1], **fn[2]).then_inc(inc[0], inc[1])
                getattr(block, attr)(body)


class WRes(list):
    def __init__(self):
        super().__init__()
        self.map = {}
        self.step = 1

    def at(self, kc, col):
        return self.map[(kc, (col // self.step) * self.step)]


class Pool_:
    def __init__(self, nc, stack, name, shape, dtype, n):
        self.tiles = [stack.enter_context(nc.sbuf_tensor("%s%d" % (name, i), shape, dtype)) for i in range(n)]
        self.res = [Res("%s%d" % (name, i)) for i in range(n)]
        self.i = 0

    def get(self):
        k = self.i % len(self.tiles)
        self.i += 1
        return self.tiles[k], self.res[k]


class KB:
    def __init__(self, T, NQB, phases="ABCDE", feed=(), dump=()):
        self.T = T
        self.NQB = NQB
        self.NT = T // 128
        self.phases = phases
        self.feed = set(feed)
        self.dump = set(dump)
        self.nc = bass.Bass("TRN2", target_bir_lowering=False)
        self.chain_dt = F32
        self.a_stage = 9
        self.d_stage = 9

    def din(self, name, shape, dt=F32):
        return self.nc.dram_tensor(name, list(shape), dt, kind="ExternalInput")

    def dscratch(self, name, shape, dt=F32):
        kind = "Internal"
        if name in self.feed:
            kind = "ExternalInput"
        elif name in self.dump:
            kind = "ExternalOutput"
        return self.nc.dram_tensor(name, list(shape), dt, kind=kind)

    def build(self):
        nc = self.nc
        T, NQB, NT = self.T, self.NQB, self.NT
        self.x = self.din("x", [T, D])
        self.gains = self.din("gains", [9, D])
        self.w_up = self.din("mlp_w_up", [2, D, FF])
        self.w_down = self.din("mlp_w_down", [2, FF, D])
        self.w_in = self.din("gdn_w_in", [D, GIN])
        self.w_out = self.din("gdn_w_out", [D, D])
        self.w_kv = self.din("w_kv", [D, 2 * D])
        self.w_q = self.din("sb_w_q", [D, D])
        self.w_o = self.din("sb_w_o", [D, D])
        self.consts = self.din("consts", [128, 8, 128])
        self.convw_l = self.din("convw_l", [128, 24, 4])
        self.small_bc = self.din("small_bc", [128, 2, 8])
        self.ogain_bc = self.din("ogain_bc", [128, 128])
        self.selall_d = self.din("selall", [8, 1024])
        if "dbg_bg" in self.dump:
            self.dbg_bg = self.nc.dram_tensor("dbg_bg", [T, 2, 8], F32, kind="ExternalOutput")
        if "dbg_qkv" in self.dump:
            self.dbg_qkv = self.nc.dram_tensor("dbg_qkv", [4, 128, 8, T], BF16, kind="ExternalOutput")
        if "dbg_o" in self.dump:
            self.dbg_o = self.nc.dram_tensor("dbg_o", [T, D], F32, kind="ExternalOutput")
        self.qsel = self.din("qsel", [128, 4])
        if "dbg_ao" in self.dump:
            self.dbg_ao = self.nc.dram_tensor("dbg_ao", [NQB * 128, D], F32, kind="ExternalOutput")
        self.amask = self.din("amask", [128, 4, 128])
        self.out = nc.dram_tensor("out", [NQB * 128, D], F32, kind="ExternalOutput")
        self.xm = self.dscratch("xm", [T, D])
        self.x1 = self.dscratch("x1", [T, D])
        self.xm2 = self.dscratch("xm2", [NQB * 128, D])
        self.qkvg = self.dscratch("qkvg", [4, 128, H, T], BF16)
        self.scd = self.dscratch("scd", [NT, 128, 12, 8])
        self.gctd = self.dscratch("gctd", [NT, 8, 128])
        self.KT = self.dscratch("KT", [NT, 128, H, 128], BF16)
        self.Vd = self.dscratch("Vd", [NT, 128, H, 128], BF16)
        with ExitStack() as st:
            self.S = S = Sched(nc, st)
            self.psb = [st.enter_context(nc.psum_tensor("psb%d" % i, [128, 512], F32)) for i in range(8)]
            self.psr = [Res("ps%d" % i, excl=True) for i in range(8)]
            self.ident_f = st.enter_context(nc.sbuf_tensor("ident_f", [128, 128], F32))
            self.ident_b = st.enter_context(nc.sbuf_tensor("ident_b", [128, 128], BF16))
            self.r_const = Res("const")
            self.eps_col = st.enter_context(nc.sbuf_tensor("eps_col", [128, 2], F32))
            S.op("pool", lambda h: h.memset(self.eps_col[:], EPS), writes=[self.r_const])
            self.one_col = st.enter_context(nc.sbuf_tensor("one_col", [128, 2], F32))
            S.op("pool", lambda h: h.memset(self.one_col[:], 1.0), writes=[self.r_const])
            S.dma("sp", lambda h: h.dma_start(out=self.ident_f[:], in_=self.consts[:, 0, :]), writes=[self.r_const])
            S.dma("pool", lambda h: h.dma_start(out=self.ident_b[:], in_=self.consts[:, 0, :]), writes=[self.r_const])
            self.final_toks = []
            if "A" in self.phases or "1" in self.phases:
                self.phase_A1()
                S.barrier()
            if "A" in self.phases or "2" in self.phases:
                self.phase_A2()
                S.barrier()
            if "B" in self.phases:
                with ExitStack() as ps:
                    self.phase_mlp(ps, 0, self.xm, self.x1, T)
                S.barrier()
            if "C" in self.phases:
                self.phase_C()
                S.barrier()
            if "D" in self.phases:
                self.phase_D()
                S.barrier()
            if "E" in self.phases:
                with ExitStack() as ps:
                    self.phase_mlp(ps, 1, self.xm2, self.out, NQB * 128)
                S.barrier()
            S.emit()
        return nc

    def dbgdump(self, name, ap, res, shape):
        if "dbgx" not in self.dump:
            return
        if not hasattr(self, "_dbgdone"):
            self._dbgdone = {}
        if name in self._dbgdone:
            return
        t = self.nc.dram_tensor("dbg_" + name, list(shape), F32, kind="ExternalOutput")
        self._dbgdone[name] = t
        self.S.dma("pool", lambda h: h.dma_start(out=t.ap(), in_=ap), reads=list(res))

    def load_gain_bc(self, st, name, row):
        t = st.enter_context(self.nc.sbuf_tensor(name, [128, D], F32))
        r = Res(name)
        self.S.dma("sp", lambda h: h.dma_start(out=t[:], in_=self.gains[row, :].partition_broadcast(128)), writes=[r])
        return t, r

    def load_w_kc(self, st, name, w_ap, ncols, q="pool"):
        t = st.enter_context(self.nc.sbuf_tensor(name, [128, 8, ncols], BF16))
        r = WRes()
        wv = w_ap.rearrange("(kc p) n -> p kc n", p=128)
        step = 1024
        r.step = step
        for c0 in range(0, ncols, step):
            for kc in range(8):
                c1 = min(ncols, c0 + step)
                ri = Res(name)
                r.append(ri)
                r.map[(kc, c0)] = ri
                self.S.dma(q, lambda h, kc=kc, c0=c0, c1=c1: h.dma_start(out=t[:, kc, c0:c1], in_=wv[:, kc, c0:c1]), writes=[ri])
        return t, r

    def rms_rstd(self, ss_ap, rstd_ap, r_ss, r_rstd, n):
        S = self.S
        S.op("act", lambda h: h.activation(out=rstd_ap, in_=ss_ap, func=AF.Ln, bias=self.eps_col[:, 0:1], scale=1.0 / n),
             reads=[r_ss, self.r_const], writes=[r_rstd])
        S.op("act", lambda h: h.activation(out=rstd_ap, in_=rstd_ap, func=AF.Exp, scale=-0.5),
             reads=[r_rstd], writes=[r_rstd])

    def norm_to_hT(self, xt, r_xt, gain, r_gain, hT, r_hT, col0, pools):
        S = self.S
        junk, r_junk = pools["junk"].get()
        st_, r_st = pools["stat"].get()
        hb, r_hb = pools["hb"].get()
        S.op("act", lambda h: h.activation(out=junk[:], in_=xt[:], func=AF.Square, accum_out=st_[:, 0:1]),
             reads=[r_xt], writes=[r_junk, r_st])
        self.rms_rstd(st_[:, 0:1], st_[:, 1:2], r_st, r_st, D)
        S.op("dve", lambda h: h.scalar_tensor_tensor(out=hb[:], in0=xt[:], scalar=st_[:, 1:2], in1=gain[:], op0=ALU.mult, op1=ALU.mult),
             reads=[r_xt, r_st, r_gain], writes=[r_hb])
        for half in range(2):
            pi = pools["tp_banks"][self._tpi % len(pools["tp_banks"])]
            self._tpi += 1
            pst = self.psb[pi].bitcast(BF16)
            for j in range(4):
                kc = half * 4 + j
                S.op("pe", lambda h, j=j, kc=kc, pst=pst: h.transpose(pst[:, j * 128:(j + 1) * 128], hb[:, kc * 128:(kc + 1) * 128], self.ident_b[:]),
                     reads=[r_hb, self.r_const], writes=[self.psr[pi]])
            eng = "act" if half == 0 else "dve"
            src = pst[:, 0:512].rearrange("p (a b) -> p a b", a=4)
            dst = hT[:, half * 4:half * 4 + 4, col0:col0 + 128]
            if eng == "act":
                S.op("act", lambda h, src=src, dst=dst: h.copy(dst, src), reads=[self.psr[pi]], writes=[r_hT])
            else:
                S.op("dve", lambda h, src=src, dst=dst: h.tensor_copy(dst, src), reads=[self.psr[pi]], writes=[r_hT])

    def post_norm_residual(self, y, r_y, ss_ap, r_ss, gain, r_gain, xres, r_xres, pools):
        S = self.S
        st_, r_st = pools["stat"].get()
        S.op("dve", lambda h: h.tensor_tensor(out=st_[:, 0:1], in0=ss_ap[:, 0:1], in1=ss_ap[:, 1:2], op=ALU.add), reads=[r_ss], writes=[r_st])
        self.rms_rstd(st_[:, 0:1], st_[:, 1:2], r_st, r_st, D)
        S.op("dve", lambda h: h.scalar_tensor_tensor(out=y[:], in0=y[:], scalar=st_[:, 1:2], in1=gain[:], op0=ALU.mult, op1=ALU.mult),
             reads=[r_y, r_st, r_gain], writes=[r_y])
        S.op("dve", lambda h: h.tensor_tensor(out=y[:], in0=y[:], in1=xres[:], op=ALU.add), reads=[r_y, r_xres], writes=[r_y])

    def phase_mlp(self, st, layer, src, dst, ntok):
        nc, S = self.nc, self.S
        TB = 512 if ntok % 512 == 0 else 256
        nblk = ntok // TB
        self._tpi = 0
        wup, r_wup = self.load_w_kc(st, "wupL%d" % layer, self.w_up[layer], FF)
        wdn = st.enter_context(nc.sbuf_tensor("wdnL%d" % layer, [128, 32, D], BF16))
        r_wdn = []
        wdv = self.w_down[layer].rearrange("(fc p) n -> p fc n", p=128)
        for fc in range(0, 32, 2):
            ri = Res("wdnL%d" % layer)
            r_wdn.append(ri)
            S.dma("pool", lambda h, fc=fc: h.dma_start(out=wdn[:, fc:fc + 2, :], in_=wdv[:, fc:fc + 2, :]), writes=[ri])
        gpre, r_gpre = self.load_gain_bc(st, "gpreL%d" % layer, 4 + layer)
        gpost, r_gpost = self.load_gain_bc(st, "gpostL%d" % layer, 6 + layer)
        pools = {
            "junk": Pool_(nc, st, "junkL%d" % layer, [128, D], BF16, 1),
            "stat": Pool_(nc, st, "statL%d" % layer, [128, 4], F32, 8),
            "hb": Pool_(nc, st, "hbL%d" % layer, [128, D], BF16, 1),
            "tp_banks": [0, 1],
        }
        xt_pool = Pool_(nc, st, "xtL%d" % layer, [128, D], F32, 2)
        y_pool = Pool_(nc, st, "yL%d" % layer, [128, D], F32, 2)
        hT_pool = Pool_(nc, st, "hTL%d" % layer, [128, 8, TB], BF16, 2)
        actT_pool = Pool_(nc, st, "actTL%d" % layer, [128, 32, TB], BF16, 1)
        relu_pool = Pool_(nc, st, "reluL%d" % layer, [128, TB], F32, 1)
        ss_pool = Pool_(nc, st, "ssL%d" % layer, [128, 2], F32, 4)
        up_banks = [2, 3]
        dn_banks = [4, 5, 6, 7]
        upi = 0
        stage = getattr(self, "dbg_stage", 9)
        if stage < 1:
            return
        def do_norm(b):
            hT, r_hT = hT_pool.get()
            for tt in range(TB // 128):
                r0 = b * TB + tt * 128
                xt, r_xt = xt_pool.get()
                S.dma("sp", lambda h, xt=xt, r0=r0: h.dma_start(out=xt[:], in_=src[r0:r0 + 128, :]), writes=[r_xt])
                self.norm_to_hT(xt, r_xt, gpre, r_gpre, hT, r_hT, tt * 128, pools)
            return hT, r_hT

        nxt = do_norm(0)
        for b in range(nblk):
            hT, r_hT = nxt
            if stage < 2:
                continue
            actT, r_actT = actT_pool.get()
            for fc in range(32):
                pi = up_banks[upi % 2]
                upi += 1
                pu = self.psb[pi][:, 0:TB]
                for kc in range(8):
                    S.op("pe", lambda h, pu=pu, kc=kc, fc=fc, hT=hT: h.matmul(pu, wup[:, kc, fc * 128:(fc + 1) * 128], hT[:, kc, :], start=(kc == 0), stop=(kc == 7)),
                         reads=[r_wup.at(kc, fc * 128), r_hT], writes=[self.psr[pi]])
                rl, r_rl = relu_pool.get()
                S.op("act", lambda h, pu=pu, rl=rl: h.activation(out=rl[:], in_=pu, func=AF.Relu), reads=[self.psr[pi]], writes=[r_rl])
                S.op("dve", lambda h, rl=rl, fc=fc, actT=actT: h.tensor_tensor(out=actT[:, fc, :], in0=rl[:], in1=rl[:], op=ALU.mult), reads=[r_rl], writes=[r_actT])
            if b + 1 < nblk:
                nxt = do_norm(b + 1)
            if stage < 3:
                continue
            for tt in range(TB // 128):
                r0 = b * TB + tt * 128
                y, r_y = y_pool.get()
                ss, r_ss = ss_pool.get()
                for half in range(2):
                    pi = dn_banks[(tt * 2 + half) % 4]
                    pd = self.psb[pi]
                    for fc in range(32):
                        S.op("pe", lambda h, pd=pd, fc=fc, tt=tt, half=half, actT=actT: h.matmul(pd[:], actT[:, fc, tt * 128:(tt + 1) * 128], wdn[:, fc, half * 512:(half + 1) * 512], start=(fc == 0), stop=(fc == 31)),
                             reads=[r_actT, r_wdn[fc // 2]], writes=[self.psr[pi]])
                    import os
                    jk, r_jk = pools["junk"].get()
                    if not os.environ.get("NOSQ"):
                        S.op("act", lambda h, pd=pd, jk=jk, ss=ss, half=half: h.activation(out=jk[:, 0:512], in_=pd[:], func=AF.Square, accum_out=ss[:, half:half + 1]),
                             reads=[self.psr[pi]], writes=[r_jk, r_ss])
                    if not os.environ.get("NOCP"):
                        S.op("dve", lambda h, pd=pd, y=y, half=half: h.tensor_copy(y[:, half * 512:(half + 1) * 512], pd[:]), reads=[self.psr[pi]], writes=[r_y])
                if stage < 4:
                    continue
                xr, r_xr = xt_pool.get()
                S.dma("sp", lambda h, xr=xr, r0=r0: h.dma_start(out=xr[:], in_=src[r0:r0 + 128, :]), writes=[r_xr])
                self.post_norm_residual(y, r_y, ss, r_ss, gpost, r_gpost, xr, r_xr, pools)
                if stage < 5:
                    continue
                t = S.dma("sp", lambda h, y=y, r0=r0: h.dma_start(out=dst[r0:r0 + 128, :], in_=y[:]), reads=[r_y])
                self.final_toks.append(t)

    def phase_A1(self):
        nc, S = self.nc, self.S
        T = self.T
        TB = 512 if T % 512 == 0 else 256
        NTT = TB // 128
        nblk = T // TB
        self._tpi = 0
        st = ExitStack()
        with st:
            def sb(name, shape, dt):
                return st.enter_context(nc.sbuf_tensor(name, shape, dt))
            win, r_win = self.load_w_kc(st, "win", self.w_in, GIN)
            gpre, r_gpre = self.load_gain_bc(st, "gpreA", 0)
            r_c = Res("constA1")
            cst = sb("cstA1", [128, 3, 128], F32)
            S.dma("sp", lambda h: h.dma_start(out=cst[:], in_=self.consts[:, 0:3, :]), writes=[r_c])
            ident_f = cst[:, 0, :]
            UTbd = cst[:, 1, :]
            BDones = cst[:, 2, :]
            ones_b = sb("ones_b", [128, 128], BF16)
            S.op("pool", lambda h: h.memset(ones_b[:], 1.0), writes=[r_c])
            cw = sb("cw", [128, 24, 4], F32)
            S.dma("sp", lambda h: h.dma_start(out=cw[:], in_=self.convw_l[:, :, :]), writes=[r_c])
            small = sb("small", [128, 2, 8], F32)
            S.dma("sp", lambda h: h.dma_start(out=small[:], in_=self.small_bc[:, :, :]), writes=[r_c])
            negA = sb("negA", [128, 8], F32)
            S.op("act", lambda h: h.activation(out=negA[:], in_=small[:, 0, :], func=AF.Exp), reads=[r_c], writes=[r_c])
            S.op("dve", lambda h: h.tensor_scalar(out=negA[:], in0=negA[:], scalar1=-1.0, scalar2=None, op0=ALU.mult), reads=[r_c], writes=[r_c])
            dtb = small[:, 1, :]
            cm = sb("cm", [128, 2], F32)
            S.op("pool", lambda h: h.memset(cm[:], 0.0), writes=[r_c])
            S.op("pool", lambda h: h.memset(cm[0:64, 0:1], 1.0), writes=[r_c])
            S.op("pool", lambda h: h.memset(cm[64:128, 1:2], 1.0), writes=[r_c])
            halo = sb("halo", [128, 24, 3], BF16)
            r_halo = Res("halo")
            S.op("pool", lambda h: h.memset(halo[:], 0.0), writes=[r_halo])
            dg = sb("dg", [128, 24, 4, 128], BF16)
            for cc_ in range(24):
                for k_ in range(4):
                    S.op("dve", lambda h: h.tensor_scalar(out=dg[:, cc_, k_, :], in0=ident_f, scalar1=cw[:, cc_, k_:k_ + 1], scalar2=None, op0=ALU.mult), reads=[r_c], writes=[r_c])
            slot_banks = [3, 4, 5, 6, 7]
            pools = {
                "junk": Pool_(nc, st, "junkA", [128, D], BF16, 1),
                "stat": Pool_(nc, st, "statA", [128, 4], F32, 8),
                "hb": Pool_(nc, st, "hbA", [128, D], BF16, 2),
                "tp_banks": [0],
            }
            xt_pool = Pool_(nc, st, "xtA", [128, D], F32, 2)
            hT_pool = Pool_(nc, st, "hTA", [128, 8, TB], BF16, 2)
            pre_pool = Pool_(nc, st, "pre", [128, TB + 4], BF16, 8)
            e_pool = Pool_(nc, st, "eA", [128, TB], F32, 8)
            sil_pool = Pool_(nc, st, "sil", [128, TB], F32, 8)
            sqb_pool = Pool_(nc, st, "sqb", [128, TB], BF16, 8)
            rinv_pool = Pool_(nc, st, "rinv", [128, TB], F32, 8)
            oc_pool = Pool_(nc, st, "ocA", [128, TB], BF16, 8)
            sc_pool = Pool_(nc, st, "sc", [128, 12, 8], F32, 4)
            gcT_pool = Pool_(nc, st, "gcT", [8, 128], F32, 4)
            PS, PR = self.psb, self.psr
            pbi = 0
            hTs = {}

            def gen_norm(b):
                hT, r_hT = hT_pool.get()
                hTs[b] = (hT, r_hT)
                for tt in range(NTT):
                    r0 = b * TB + tt * 128
                    xt, r_xt = xt_pool.get()
                    S.dma("sp", lambda h: h.dma_start(out=xt[:], in_=self.x[r0:r0 + 128, :]), writes=[r_xt])
                    self.norm_to_hT(xt, r_xt, gpre, r_gpre, hT, r_hT, tt * 128, pools)
                    yield

            for _ in gen_norm(0):
                pass
            for b in range(nblk):
                hT, r_hT = hTs[b]
                def gen_sc(b=b, hT=hT, r_hT=r_hT):
                    for tt in range(NTT):
                        sc, r_sc = sc_pool.get()
                        gcT, r_gcT = gcT_pool.get()
                        pb = PS[1][:, 0:16]
                        for kc in range(8):
                            S.op("pe", lambda h: h.matmul(pb, hT[:, kc, tt * 128:(tt + 1) * 128], win[:, kc, 4 * D:4 * D + 16], start=(kc == 0), stop=(kc == 7)),
                                 reads=[r_hT, r_win], writes=[PR[1]])
                        S.op("act", lambda h: h.activation(out=sc[:, 9, :], in_=pb[:, 0:8], func=AF.Exp, scale=-1.0), reads=[PR[1]], writes=[r_sc])
                        S.op("dve", lambda h: h.tensor_tensor(out=sc[:, 1, :], in0=pb[:, 8:16], in1=dtb, op=ALU.add), reads=[PR[1], r_c], writes=[r_sc])
                        S.op("dve", lambda h: h.tensor_scalar(out=sc[:, 9, :], in0=sc[:, 9, :], scalar1=1.0, scalar2=None, op0=ALU.add), reads=[r_sc], writes=[r_sc])
                        S.op("dve", lambda h: h.reciprocal(out=sc[:, 0, :], in_=sc[:, 9, :]), reads=[r_sc], writes=[r_sc])
                        yield
                        S.op("act", lambda h: h.activation(out=sc[:, 1, :], in_=sc[:, 1, :], func=AF.Exp), reads=[r_sc], writes=[r_sc])
                        S.op("act", lambda h: h.activation(out=sc[:, 1, :], in_=sc[:, 1, :], func=AF.Ln, bias=self.one_col[:, 0:1], scale=1.0), reads=[r_sc, self.r_const], writes=[r_sc])
                        yield
                        S.op("dve", lambda h: h.tensor_tensor(out=sc[:, 1, :], in0=sc[:, 1, :], in1=negA[:], op=ALU.mult), reads=[r_sc, r_c], writes=[r_sc])
                        pg = PS[2][:, 0:16]
                        S.op("pe", lambda h: h.matmul(pg[:, 0:8], UTbd, sc[:, 1, :], start=True, stop=True), reads=[r_sc, r_c], writes=[PR[2]])
                        S.op("pe", lambda h: h.matmul(pg[:, 8:16], BDones, sc[:, 1, :], start=True, stop=True), reads=[r_sc, r_c], writes=[PR[2]])
                        S.op("dve", lambda h: h.tensor_copy(sc[:, 2:4, :], pg.rearrange("p (a b) -> p a b", a=2)), reads=[PR[2]], writes=[r_sc])
                        yield
                        S.op("act", lambda h: h.activation(out=sc[:, 4, :], in_=sc[:, 2, :], func=AF.Exp), reads=[r_sc], writes=[r_sc])
                        S.op("dve", lambda h: h.tensor_tensor(out=sc[:, 5, :], in0=sc[:, 4, :], in1=sc[:, 0, :], op=ALU.mult), reads=[r_sc], writes=[r_sc])
                        S.op("dve", lambda h: h.tensor_tensor(out=sc[:, 6, :], in0=sc[:, 3, :], in1=sc[:, 2, :], op=ALU.subtract), reads=[r_sc], writes=[r_sc])
                        S.op("act", lambda h: h.activation(out=sc[:, 6, :], in_=sc[:, 6, :], func=AF.Exp), reads=[r_sc], writes=[r_sc])
                        yield
                        S.op("dve", lambda h: h.tensor_scalar(out=sc[:, 7, :], in0=sc[:, 6, :], scalar1=cm[:, 0:1], scalar2=None, op0=ALU.mult), reads=[r_sc, r_c], writes=[r_sc])
                        S.op("dve", lambda h: h.tensor_scalar(out=sc[:, 8, :], in0=sc[:, 6, :], scalar1=cm[:, 1:2], scalar2=None, op0=ALU.mult), reads=[r_sc, r_c], writes=[r_sc])
                        S.op("pe", lambda h: h.transpose(PS[0][0:8, 0:128], sc[:, 2, :], ident_f), reads=[r_sc, r_c], writes=[PR[0]])
                        S.op("dve", lambda h: h.tensor_copy(gcT[:], PS[0][0:8, 0:128]), reads=[PR[0]], writes=[r_gcT])
                        ti = b * NTT + tt
                        S.dma("sp", lambda h: h.dma_start(out=self.scd[ti], in_=sc[:]), reads=[r_sc])
                        S.dma("sp", lambda h: h.dma_start(out=self.gctd[ti], in_=gcT[:]), reads=[r_gcT])
                        yield
                        if "dbg_bg" in self.dump:
                            r0 = b * TB + tt * 128
                            S.dma("sp", lambda h: h.dma_start(out=self.dbg_bg[r0:r0 + 128, :, :], in_=sc[:, 0:2, :]), reads=[r_sc])

                def gen_chunk(cc, hT=hT, r_hT=r_hT, b=b):
                    oc, r_oc = oc_pool.get()
                    which_, hh_ = cc // 8, cc % 8

                    def store():
                        S.dma("sp", lambda h: h.dma_start(out=self.qkvg[which_, :, hh_, b * TB:(b + 1) * TB], in_=oc[:]), reads=[r_oc])
                    pi = free_banks.pop(0)
                    pp = PS[pi][:, 0:TB]
                    for kc in range(8):
                        S.op("pe", lambda h: h.matmul(pp, win[:, kc, cc * 128:(cc + 1) * 128], hT[:, kc, :], start=(kc == 0), stop=(kc == 7)),
                             reads=[r_hT, r_win.at(kc, cc * 128)], writes=[PR[pi]])
                    yield
                    e_, r_e = e_pool.get()
                    if cc >= 24:
                        hh = cc - 24
                        S.op("act", lambda h: h.activation(out=e_[:], in_=pp, func=AF.Exp, scale=-1.0), reads=[PR[pi]], writes=[r_e])
                        yield
                        S.op("act", lambda h: h.activation(out=e_[:], in_=e_[:], func=AF.Ln, bias=self.one_col[:, 0:1], scale=1.0), reads=[r_e, self.r_const], writes=[r_e])
                        yield
                        S.op("act", lambda h: h.activation(out=e_[:], in_=e_[:], func=AF.Exp, scale=-1.0), reads=[r_e], writes=[r_e])
                        yield
                        S.op("dve", lambda h: h.tensor_tensor(out=oc[:], in0=pp, in1=e_[:], op=ALU.mult), reads=[PR[pi], r_e], writes=[r_oc])
                        store()
                        free_banks.append(pi)
                        return
                    pre, r_pre = pre_pool.get()
                    S.op("pool", lambda h: h.tensor_copy(pre[:, 0:3], halo[:, cc, :]), reads=[r_halo], writes=[r_pre])
                    S.op("dve", lambda h: h.tensor_copy(pre[:, 3:3 + TB], pp), reads=[PR[pi]], writes=[r_pre])
                    yield
                    S.op("pool", lambda h: h.tensor_copy(halo[:, cc, :], pre[:, TB:TB + 3]), reads=[r_pre], writes=[r_halo])
                    for k_ in range(4):
                        S.op("pe", lambda h: h.matmul(pp, dg[:, cc, k_, :], pre[:, k_:k_ + TB], start=(k_ == 0), stop=(k_ == 3)), reads=[r_pre, r_c], writes=[PR[pi]])
                    yield
                    S.op("act", lambda h: h.activation(out=e_[:], in_=pp, func=AF.Exp, scale=-1.0), reads=[PR[pi]], writes=[r_e])
                    yield
                    S.op("act", lambda h: h.activation(out=e_[:], in_=e_[:], func=AF.Ln, bias=self.one_col[:, 0:1], scale=1.0), reads=[r_e, self.r_const], writes=[r_e])
                    yield
                    S.op("act", lambda h: h.activation(out=e_[:], in_=e_[:], func=AF.Exp, scale=-1.0), reads=[r_e], writes=[r_e])
                    yield
                    which, hh = cc // 8, cc % 8
                    if which == 2:
                        S.op("dve", lambda h: h.tensor_tensor(out=oc[:], in0=pp, in1=e_[:], op=ALU.mult), reads=[PR[pi], r_e], writes=[r_oc])
                        store()
                        free_banks.append(pi)
                        return
                    sil, r_sil = sil_pool.get()
                    sqb, r_sqb = sqb_pool.get()
                    rinv, r_rinv = rinv_pool.get()
                    S.op("dve", lambda h: h.tensor_tensor(out=sil[:], in0=pp, in1=e_[:], op=ALU.mult), reads=[PR[pi], r_e], writes=[r_sil])
                    yield
                    S.op("dve", lambda h: h.tensor_tensor(out=sqb[:], in0=sil[:], in1=sil[:], op=ALU.mult), reads=[r_sil], writes=[r_sqb])
                    yield
                    S.op("pe", lambda h: h.matmul(pp, ones_b[:], sqb[:], start=True, stop=True), reads=[r_sqb, r_c], writes=[PR[pi]])
                    yield
                    S.op("act", lambda h: h.activation(out=rinv[:], in_=pp, func=AF.Ln, bias=self.eps_col[:, 0:1], scale=1.0), reads=[PR[pi], self.r_const], writes=[r_rinv])
                    yield
                    S.op("act", lambda h: h.activation(out=rinv[:], in_=rinv[:], func=AF.Exp, scale=-0.5), reads=[r_rinv], writes=[r_rinv])
                    yield
                    scl = HD ** -0.5 if which == 0 else 1.0
                    S.op("dve", lambda h: h.scalar_tensor_tensor(out=oc[:], in0=sil[:], scalar=scl, in1=rinv[:], op0=ALU.mult, op1=ALU.mult),
                         reads=[r_sil, r_rinv], writes=[r_oc])
                    store()
                    free_banks.append(pi)

                NFL = len(slot_banks)
                free_banks = list(slot_banks)
                pending = list(range(32))
                active = [gen_sc()]
                if b + 1 < nblk:
                    active.append(gen_norm(b + 1))
                NFL += len(active)
                rnd = 0
                while pending or active:
                    if pending and free_banks and rnd % 2 == 0:
                        active.append(gen_chunk(pending.pop(0)))
                    rnd += 1
                    for gn in list(active):
                        try:
                            next(gn)
                        except StopIteration:
                            active.remove(gn)

    def phase_A2(self):
        nc, S = self.nc, self.S
        T = self.T
        TB = 256
        NTT = TB // 128
        nblk = T // TB
        TD = self.chain_dt
        st = ExitStack()
        with st:
            def sb(name, shape, dt):
                return st.enter_context(nc.sbuf_tensor(name, shape, dt))
            wout, r_wout = self.load_w_kc(st, "wout", self.w_out, D)
            gpost, r_gpost = self.load_gain_bc(st, "gpostA", 2)
            r_c = Res("constA2")
            cst = sb("cstA2", [128, 2, 128], F32)
            S.dma("sp", lambda h: h.dma_start(out=cst[:], in_=self.consts[:, 3:5, :]), writes=[r_c])
            negmaskU = cst[:, 0, :]
            posmaskL = cst[:, 1, :]
            ident4 = sb("ident4", [128, 4, 128], F32)
            for hh in range(4):
                S.dma("sp", lambda h: h.dma_start(out=ident4[:, hh, :], in_=self.consts[:, 0, :]), writes=[r_c])
            ogain = sb("ogain", [128, 128], F32)
            S.dma("sp", lambda h: h.dma_start(out=ogain[:], in_=self.ogain_bc[:, :]), writes=[r_c])
            selall = sb("selall_sb", [8, 1024], F32)
            S.dma("sp", lambda h: h.dma_start(out=selall[:], in_=self.selall_d[:, :]), writes=[r_c])
            Sf = sb("Sf", [128, 8, 128], F32)
            r_Sf = [Res("Sf0"), Res("Sf1")]
            S.op("pool", lambda h: h.memset(Sf[:], 0.0), writes=r_Sf)
            Sb = [sb("Sb%d" % i, [128, 8, 128], BF16) for i in range(2)]
            r_Sb = [[Res("Sb%d_%d" % (i, g)) for g in range(2)] for i in range(2)]
            for i in range(2):
                S.op("pool", lambda h: h.memset(Sb[i][:], 0.0), writes=r_Sb[i])
            pools = {
                "junk": Pool_(nc, st, "junkA2", [128, D], BF16, 1),
                "stat": Pool_(nc, st, "statA2", [128, 4], F32, 8),
            }
            xt_pool = Pool_(nc, st, "xtA2", [128, D], F32, 2)
            y_pool = Pool_(nc, st, "yA2", [128, D], F32, 1)
            ss_pool = Pool_(nc, st, "ssA2", [128, 2], F32, 4)
            in_pool = Pool_(nc, st, "qkvgin", [128, 4, 8, TB], BF16, 2)
            ogT_pool = Pool_(nc, st, "ogT", [128, 8, TB], BF16, 2)
            sc_pool = Pool_(nc, st, "sc2", [128, 12, 8], F32, 4)
            gcT_pool = Pool_(nc, st, "gcT2", [8, 128], F32, 4)
            NCH = NTT * 2

            def chain_bufs(ci):
                d = {}
                def mk(nm, dt, n=1):
                    ts = [sb("%s_%d_%d" % (nm, ci, i), [128, 4, 128], dt) for i in range(n)]
                    rs = [Res("%s_%d_%d" % (nm, ci, i)) for i in range(n)]
                    d[nm] = (ts, rs)
                for nm, dt, n in (("kbg", BF16, 1), ("kg0", BF16, 1), ("kg1", BF16, 1), ("vb", BF16, 1), ("dd", F32, 1), ("eT", F32, 1),
                                  ("attnT", BF16, 1), ("egrow", F32, 1), ("qg0", BF16, 1), ("qg1", BF16, 1), ("wT0", BF16, 1), ("wT1", BF16, 1),
                                  ("Lp", TD, 2), ("Np", TD, 2), ("Pp", TD, 2), ("TT", BF16, 1), ("u", F32, 1), ("vnew", BF16, 1)):
                    mk(nm, dt, n)
                d["eL"] = d["dd"]
                for nm in ("qg0", "qg1", "wT0", "wT1", "vnew"):
                    t_ = d[nm][0][0]
                    S.op("pool", lambda h: h.memset(t_[:], 0.0), writes=[d[nm][1][0]])
                return d
            CB = [chain_bufs(ci) for ci in range(NCH)]
            tmpS_pool = Pool_(nc, st, "tmpS", [128, 4, 128], F32, 2)
            osq_pool = Pool_(nc, st, "osq", [128, 4, 128], F32, 1)
            on_pool = Pool_(nc, st, "on", [128, 4, 128], BF16, 2)
            ident_td = self.ident_b if TD == BF16 else self.ident_f
            PS, PR = self.psb, self.psr

            def v4(ap):
                return ap.rearrange("p (a b) -> p a b", a=4)

            def bview(pi):
                return PS[pi].bitcast(BF16)

            def gen_pre(ci, tt, g, it, r_it, sc, r_sc, gcT, r_gcT):
                qT, kT, vT = it[:, 0], it[:, 1], it[:, 2]
                B = CB[ci]
                b0, b1 = 2 * ci, 2 * ci + 1
                h0 = g * 4
                tc0 = tt * 128
                kbg, r_kbg = B["kbg"][0][0], B["kbg"][1][0]
                kg0, r_kg0 = B["kg0"][0][0], B["kg0"][1][0]
                kg1, r_kg1 = B["kg1"][0][0], B["kg1"][1][0]
                vb, r_vb = B["vb"][0][0], B["vb"][1][0]
                dd, r_dd = B["dd"][0][0], B["dd"][1][0]
                eT, r_eT = B["eT"][0][0], B["eT"][1][0]
                eL, r_eL = B["eL"][0][0], B["eL"][1][0]
                attnT, r_attnT = B["attnT"][0][0], B["attnT"][1][0]
                egrow, r_egrow = B["egrow"][0][0], B["egrow"][1][0]
                qg0, qg1 = B["qg0"][0][0], B["qg1"][0][0]
                r_qg0, r_qg1 = B["qg0"][1][0], B["qg1"][1][0]
                wT0, wT1 = B["wT0"][0][0], B["wT1"][0][0]
                r_wT0, r_wT1 = B["wT0"][1][0], B["wT1"][1][0]
                TT, r_TT = B["TT"][0][0], B["TT"][1][0]
                u, r_u = B["u"][0][0], B["u"][1][0]

                def bc(slot):
                    return sc[:, slot, h0:h0 + 4].unsqueeze(2).to_broadcast([128, 4, 128])
                pt = bview(b0)
                for hh in range(4):
                    S.op("pe", lambda h: h.transpose(pt[:, hh * 128:(hh + 1) * 128], kT[:, h0 + hh, tc0:tc0 + 128], self.ident_b[:]), reads=[r_it, self.r_const], writes=[PR[b0]])
                pt4 = v4(pt[:, 0:512])
                yield
                S.op("dve", lambda h: h.tensor_tensor(out=kbg[:], in0=pt4, in1=bc(5), op=ALU.mult), reads=[PR[b0], r_sc], writes=[r_kbg])
                S.op("dve", lambda h: h.tensor_tensor(out=kg0[:], in0=pt4, in1=bc(7), op=ALU.mult), reads=[PR[b0], r_sc], writes=[r_kg0])
                S.op("dve", lambda h: h.tensor_tensor(out=kg1[:], in0=pt4, in1=bc(8), op=ALU.mult), reads=[PR[b0], r_sc], writes=[r_kg1])
                pt2 = bview(b1)
                for hh in range(4):
                    S.op("pe", lambda h: h.transpose(pt2[:, hh * 128:(hh + 1) * 128], vT[:, h0 + hh, tc0:tc0 + 128], self.ident_b[:]), reads=[r_it, self.r_const], writes=[PR[b1]])
                yield
                S.op("dve", lambda h: h.tensor_tensor(out=vb[:], in0=v4(pt2[:, 0:512]), in1=bc(0), op=ALU.mult), reads=[PR[b1], r_sc], writes=[r_vb])
                for hh in range(4):
                    S.op("pe", lambda h: h.matmul(PS[b0][:, hh * 128:(hh + 1) * 128], selall[0:8, (h0 + hh) * 128:(h0 + hh + 1) * 128], gcT[:], start=True, stop=True), reads=[r_gcT, r_c], writes=[PR[b0]])
                yield
                gcb = v4(PS[b0][:])
                negU4 = negmaskU.unsqueeze(1).to_broadcast([128, 4, 128])
                posL4 = posmaskL.unsqueeze(1).to_broadcast([128, 4, 128])
                S.op("dve", lambda h: h.tensor_tensor(out=dd[:], in0=gcb, in1=bc(2), op=ALU.subtract), reads=[PR[b0], r_sc], writes=[r_dd])
                S.op("act", lambda h: h.activation(out=egrow[:], in_=gcb, func=AF.Exp), reads=[PR[b0]], writes=[r_egrow])
                for hh in range(4):
                    S.op("pe", lambda h: h.matmul(PS[b1][:, hh * 128:(hh + 1) * 128], kT[:, h0 + hh, tc0:tc0 + 128], qT[:, h0 + hh, tc0:tc0 + 128], start=True, stop=True), reads=[r_it], writes=[PR[b1]])
                yield
                S.op("dve", lambda h: h.tensor_tensor(out=eT[:], in0=dd[:], in1=negU4, op=ALU.add), reads=[r_dd, r_c], writes=[r_eT])
                S.op("dve", lambda h: h.tensor_tensor(out=eL[:], in0=dd[:], in1=posL4, op=ALU.add), reads=[r_dd, r_c], writes=[r_eL])
                S.op("dve", lambda h: h.tensor_tensor(out=qg0[:, :, 0:64], in0=qT[:, h0:h0 + 4, tc0:tc0 + 64], in1=egrow[:, :, 0:64], op=ALU.mult), reads=[r_it, r_egrow], writes=[r_qg0])
                S.op("dve", lambda h: h.tensor_tensor(out=qg1[:, :, 64:128], in0=qT[:, h0:h0 + 4, tc0 + 64:tc0 + 128], in1=egrow[:, :, 64:128], op=ALU.mult), reads=[r_it, r_egrow], writes=[r_qg1])
                yield
                S.op("act", lambda h: h.activation(out=eT[:], in_=eT[:], func=AF.Exp), reads=[r_eT], writes=[r_eT])
                S.op("act", lambda h: h.activation(out=eL[:], in_=eL[:], func=AF.Exp, scale=-1.0), reads=[r_eL], writes=[r_eL])
                for hh in range(4):
                    S.op("pe", lambda h: h.matmul(PS[b0][:, hh * 128:(hh + 1) * 128], kT[:, h0 + hh, tc0:tc0 + 128], kT[:, h0 + hh, tc0:tc0 + 128], start=True, stop=True), reads=[r_it], writes=[PR[b0]])
                yield
                S.op("dve", lambda h: h.tensor_tensor(out=attnT[:], in0=v4(PS[b1][:]), in1=eT[:], op=ALU.mult), reads=[PR[b1], r_eT], writes=[r_attnT])
                S.op("dve", lambda h: h.tensor_tensor(out=eL[:], in0=eL[:], in1=bc(0), op=ALU.mult), reads=[r_eL, r_sc], writes=[r_eL])
                yield
                Lts, Lrs = B["Lp"]
                Nts, Nrs = B["Np"]
                Pts, Prs = B["Pp"]
                li = ni = pi_ = 0
                Lp, r_L = Lts[0], Lrs[0]
                S.op("dve", lambda h: h.tensor_tensor(out=Lp[:], in0=v4(PS[b0][:]), in1=eL[:], op=ALU.mult), reads=[PR[b0], r_eL], writes=[r_L])
                yield
                Np, r_N = Nts[0], Nrs[0]
                Pp, r_P = Pts[0], Prs[0]
                pta = bview(b1) if TD == BF16 else PS[b1]
                for hh in range(4):
                    S.op("pe", lambda h: h.transpose(pta[:, hh * 128:(hh + 1) * 128], Lp[:, hh, :], ident_td[:]), reads=[r_L, self.r_const], writes=[PR[b1]])
                yield
                S.op("act", lambda h: h.copy(Np[:], v4(pta[:, 0:512])), reads=[PR[b1]], writes=[r_N])
                S.op("dve", lambda h: h.tensor_tensor(out=Pp[:], in0=ident4[:], in1=v4(pta[:, 0:512]), op=ALU.subtract), reads=[PR[b1], r_c], writes=[r_P])
                yield
                for lvl in range(1, 6):
                    li ^= 1
                    L2, r_L2 = Lts[li], Lrs[li]
                    for hh in range(4):
                        S.op("pe", lambda h: h.matmul(PS[b0][:, hh * 128:(hh + 1) * 128], Np[:, hh, :], Lp[:, hh, :], start=True, stop=True), reads=[r_N, r_L], writes=[PR[b0]])
                    if lvl < 5:
                        ni ^= 1
                        N2, r_N2 = Nts[ni], Nrs[ni]
                        for hh in range(4):
                            S.op("pe", lambda h: h.matmul(PS[b1][:, hh * 128:(hh + 1) * 128], Lp[:, hh, :], Np[:, hh, :], start=True, stop=True), reads=[r_N, r_L], writes=[PR[b1]])
                    yield
                    S.op("act", lambda h: h.copy(L2[:], v4(PS[b0][:])), reads=[PR[b0]], writes=[r_L2])
                    if lvl < 5:
                        S.op("dve", lambda h: h.tensor_copy(N2[:], v4(PS[b1][:])), reads=[PR[b1]], writes=[r_N2])
                    yield
                    for hh in range(4):
                        S.op("pe", lambda h: h.matmul(PS[b0][:, hh * 128:(hh + 1) * 128], ident_td[:], Pp[:, hh, :], start=True, stop=False), reads=[self.r_const, r_P], writes=[PR[b0]])
                        S.op("pe", lambda h: h.matmul(PS[b0][:, hh * 128:(hh + 1) * 128], L2[:, hh, :], Pp[:, hh, :], start=False, stop=True), reads=[r_L2, r_P], writes=[PR[b0]])
                    yield
                    if lvl < 5:
                        pi_ ^= 1
                        P2, r_P2 = Pts[pi_], Prs[pi_]
                        S.op("dve" if lvl % 2 else "act", (lambda h: h.tensor_copy(P2[:], v4(PS[b0][:]))) if lvl % 2 else (lambda h: h.copy(P2[:], v4(PS[b0][:]))), reads=[PR[b0]], writes=[r_P2])
                        Np, r_N = N2, r_N2
                        Pp, r_P = P2, r_P2
                    else:
                        S.op("act", lambda h: h.copy(TT[:], v4(PS[b0][:])), reads=[PR[b0]], writes=[r_TT])
                    Lp, r_L = L2, r_L2
                    yield
                for hh in range(4):
                    S.op("pe", lambda h: h.matmul(PS[b0][:, hh * 128:(hh + 1) * 128], kbg[:, hh, :], TT[:, hh, :], start=True, stop=True), reads=[r_kbg, r_TT], writes=[PR[b0]])
                for hh in range(4):
                    S.op("pe", lambda h: h.matmul(PS[b1][:, hh * 128:(hh + 1) * 128], TT[:, hh, :], vb[:, hh, :], start=True, stop=True), reads=[r_vb, r_TT], writes=[PR[b1]])
                yield
                S.op("act", lambda h: h.copy(wT0[:, :, 0:64], v4(PS[b0][:])[:, :, 0:64]), reads=[PR[b0]], writes=[r_wT0])
                S.op("act", lambda h: h.copy(wT1[:, :, 64:128], v4(PS[b0][:])[:, :, 64:128]), reads=[PR[b0]], writes=[r_wT1])
                S.op("dve", lambda h: h.tensor_copy(u[:], v4(PS[b1][:])), reads=[PR[b1]], writes=[r_u])
                yield

            def gen_rec(g, it, r_it, ogT, r_ogT):
                gateT = it[:, 3]
                h0 = g * 4
                bw, bd, bo = 3 * g, 3 * g + 1, 3 * g + 2
                for tt in range(NTT):
                    ci = tt * 2 + g
                    B = CB[ci]
                    tc0 = tt * 128
                    u, r_u = B["u"][0][0], B["u"][1][0]
                    vnew, r_vnew = B["vnew"][0][0], B["vnew"][1][0]
                    egrow, r_egrow = B["egrow"][0][0], B["egrow"][1][0]
                    attnT, r_attnT = B["attnT"][0][0], B["attnT"][1][0]
                    for c in range(2):
                        cs = slice(c * 64, (c + 1) * 64)
                        Sb_in, r_Sb_in = Sb[c][:, h0:h0 + 4, :], r_Sb[c][g]
                        Sb_out, r_Sb_out = Sb[1 - c][:, h0:h0 + 4, :], r_Sb[1 - c][g]
                        wTc, r_wTc = (B["wT0"][0][0], B["wT0"][1][0]) if c == 0 else (B["wT1"][0][0], B["wT1"][1][0])
                        kgc, r_kgc = (B["kg0"][0][0], B["kg0"][1][0]) if c == 0 else (B["kg1"][0][0], B["kg1"][1][0])
                        for hh in range(4):
                            S.op("pe", lambda h: h.matmul(PS[bw][:, hh * 128:(hh + 1) * 128], wTc[:, hh, :], Sb_in[:, hh, :], start=True, stop=True), reads=[r_wTc, r_Sb_in], writes=[PR[bw]])
                        yield
                        S.op("dve", lambda h: h.tensor_tensor(out=vnew[cs, :, :], in0=u[cs, :, :], in1=v4(PS[bw][:])[cs, :, :], op=ALU.subtract), reads=[PR[bw], r_u], writes=[r_vnew])
                        yield
                        if c == 1:
                            Sb0 = Sb[0][:, h0:h0 + 4, :]
                            qg0, qg1 = B["qg0"][0][0], B["qg1"][0][0]
                            for hh in range(4):
                                oc = PS[bo][:, hh * 128:(hh + 1) * 128]
                                S.op("pe", lambda h: h.matmul(oc, qg0[:, hh, :], Sb0[:, hh, :], start=True, stop=False), reads=[B["qg0"][1][0], r_Sb[0][g]], writes=[PR[bo]])
                                S.op("pe", lambda h: h.matmul(oc, qg1[:, hh, :], Sb_in[:, hh, :], start=False, stop=False), reads=[B["qg1"][1][0], r_Sb_in], writes=[PR[bo]])
                                S.op("pe", lambda h: h.matmul(oc, attnT[:, hh, :], vnew[:, hh, :], start=False, stop=True), reads=[r_attnT, r_vnew], writes=[PR[bo]])
                        for hh in range(4):
                            S.op("pe", lambda h: h.matmul(PS[bd][:, hh * 128:(hh + 1) * 128], kgc[:, hh, :], vnew[:, hh, :], start=True, stop=True), reads=[r_kgc, r_vnew], writes=[PR[bd]])
                        tmpS, r_tmpS = tmpS_pool.get()
                        col = 63 if c == 0 else 127
                        eglb = egrow[:, :, col:col + 1].to_broadcast([128, 4, 128])
                        S.op("dve", lambda h: h.tensor_tensor(out=tmpS[:], in0=Sf[:, h0:h0 + 4, :], in1=eglb, op=ALU.mult), reads=[r_Sf[g], r_egrow], writes=[r_tmpS])
                        yield
                        S.op("dve", lambda h: h.tensor_tensor(out=Sf[:, h0:h0 + 4, :], in0=tmpS[:], in1=v4(PS[bd][:]), op=ALU.add), reads=[PR[bd], r_tmpS], writes=[r_Sf[g]])
                        yield
                        S.op("act", lambda h: h.copy(Sb_out, Sf[:, h0:h0 + 4, :]), reads=[r_Sf[g]], writes=[r_Sb_out])
                        yield
                    osq, r_osq = osq_pool.get()
                    on, r_on = on_pool.get()
                    stt_, r_stt = pools["stat"].get()
                    o4 = v4(PS[bo][:])
                    if "dbg_o" in self.dump:
                        S.op("act", lambda h: h.copy(osq[:], o4), reads=[PR[bo]], writes=[r_osq])
                        r0 = self._cur_b * TB + tt * 128
                        S.dma("sp", lambda h: h.dma_start(out=self.dbg_o[r0:r0 + 128, h0 * 128:(h0 + 4) * 128], in_=osq[:].rearrange("p a b -> p (a b)")), reads=[r_osq])
                    S.op("act", lambda h: h.activation(out=osq[:], in_=o4, func=AF.Square), reads=[PR[bo]], writes=[r_osq])
                    S.op("dve", lambda h: h.tensor_reduce(out=stt_[:, 0:4], in_=osq[:], axis=mybir.AxisListType.X, op=ALU.add), reads=[r_osq], writes=[r_stt])
                    yield
                    self.rms_rstd(stt_[:, 0:4], stt_[:, 0:4], r_stt, r_stt, HD)
                    rsb = stt_[:, 0:4].unsqueeze(2).to_broadcast([128, 4, 128])
                    ogb = ogain[:].unsqueeze(1).to_broadcast([128, 4, 128])
                    yield
                    S.op("dve", lambda h: h.tensor_tensor(out=osq[:], in0=o4, in1=rsb, op=ALU.mult), reads=[PR[bo], r_stt], writes=[r_osq])
                    S.op("dve", lambda h: h.tensor_tensor(out=on[:], in0=osq[:], in1=ogb, op=ALU.mult), reads=[r_osq, r_c], writes=[r_on])
                    yield
                    pt = bview(bw)
                    for hh in range(4):
                        S.op("pe", lambda h: h.transpose(pt[:, hh * 128:(hh + 1) * 128], on[:, hh, :], self.ident_b[:]), reads=[r_on, self.r_const], writes=[PR[bw]])
                    yield
                    S.op("dve", lambda h: h.tensor_tensor(out=ogT[:, h0:h0 + 4, tc0:tc0 + 128], in0=v4(pt[:, 0:512]), in1=gateT[:, h0:h0 + 4, tc0:tc0 + 128], op=ALU.mult), reads=[PR[bw], r_it], writes=[r_ogT])
                    yield

            def run_gens(gens, stagger=3):
                pend = list(gens)
                act_ = []
                rnd = 0
                while pend or act_:
                    if pend and rnd % stagger == 0:
                        act_.append(pend.pop(0))
                    rnd += 1
                    for gn in list(act_):
                        try:
                            next(gn)
                        except StopIteration:
                            act_.remove(gn)

            def gen_wout(b, ogT, r_ogT):
                for tt in range(NTT):
                    r0 = b * TB + tt * 128
                    y, r_y = y_pool.get()
                    ss, r_ss = ss_pool.get()
                    for half in range(2):
                        pi = 6 + half
                        pd = PS[pi]
                        for hh in range(8):
                            S.op("pe", lambda h: h.matmul(pd[:], ogT[:, hh, tt * 128:(tt + 1) * 128], wout[:, hh, half * 512:(half + 1) * 512], start=(hh == 0), stop=(hh == 7)),
                                 reads=[r_ogT, r_wout], writes=[PR[pi]])
                        yield
                        jk, r_jk = pools["junk"].get()
                        S.op("act", lambda h: h.activation(out=jk[:, 0:512], in_=pd[:], func=AF.Square, accum_out=ss[:, half:half + 1]),
                             reads=[PR[pi]], writes=[r_jk, r_ss])
                        yield
                        S.op("dve", lambda h: h.tensor_copy(y[:, half * 512:(half + 1) * 512], pd[:]), reads=[PR[pi]], writes=[r_y])
                        yield
                    xr, r_xr = xt_pool.get()
                    S.dma("sp", lambda h: h.dma_start(out=xr[:], in_=self.x[r0:r0 + 128, :]), writes=[r_xr])
                    st_, r_st = pools["stat"].get()
                    S.op("dve", lambda h: h.tensor_tensor(out=st_[:, 0:1], in0=ss[:, 0:1], in1=ss[:, 1:2], op=ALU.add), reads=[r_ss], writes=[r_st])
                    yield
                    S.op("act", lambda h: h.activation(out=st_[:, 1:2], in_=st_[:, 0:1], func=AF.Ln, bias=self.eps_col[:, 0:1], scale=1.0 / D), reads=[r_st, self.r_const], writes=[r_st])
                    yield
                    S.op("act", lambda h: h.activation(out=st_[:, 1:2], in_=st_[:, 1:2], func=AF.Exp, scale=-0.5), reads=[r_st], writes=[r_st])
                    yield
                    S.op("dve", lambda h: h.scalar_tensor_tensor(out=y[:], in0=y[:], scalar=st_[:, 1:2], in1=gpost[:], op0=ALU.mult, op1=ALU.mult), reads=[r_y, r_st, r_gpost], writes=[r_y])
                    yield
                    S.op("dve", lambda h: h.tensor_tensor(out=y[:], in0=y[:], in1=xr[:], op=ALU.add), reads=[r_y, r_xr], writes=[r_y])
                    yield
                    S.dma("sp", lambda h: h.dma_start(out=self.xm[r0:r0 + 128, :], in_=y[:]), reads=[r_y])

            prev_wout = None
            for b in range(nblk):
                self._cur_b = b
                it, r_it = in_pool.get()
                for i_ in range(4):
                    S.dma("sp", lambda h: h.dma_start(out=it[:, i_], in_=self.qkvg[i_, :, :, b * TB:(b + 1) * TB]), writes=[r_it])
                scs = []
                for tt in range(NTT):
                    sc, r_sc = sc_pool.get()
                    gcT, r_gcT = gcT_pool.get()
                    ti = b * NTT + tt
                    S.dma("sp", lambda h: h.dma_start(out=sc[:], in_=self.scd[ti]), writes=[r_sc])
                    S.dma("sp", lambda h: h.dma_start(out=gcT[:], in_=self.gctd[ti]), writes=[r_gcT])
                    scs.append((sc, r_sc, gcT, r_gcT))
                gens = [gen_pre(tt * 2 + g, tt, g, it, r_it, *scs[tt]) for tt in range(NTT) for g in range(2)]
                import os
                run_gens(gens, stagger=int(os.environ.get("STG", "2")))
                ogT, r_ogT = ogT_pool.get()
                rgens = [gen_rec(g, it, r_it, ogT, r_ogT) for g in range(2)]
                if prev_wout is not None:
                    rgens.append(prev_wout)
                run_gens(rgens, stagger=1)
                prev_wout = gen_wout(b, ogT, r_ogT)
            run_gens([prev_wout])

    def phase_C(self):
        nc, S = self.nc, self.S
        T = self.T
        TB = 512 if T % 512 == 0 else 256
        NTT = TB // 128
        nblk = T // TB
        self._tpi = 0
        st = ExitStack()
        with st:
            wkv, r_wkv = self.load_w_kc(st, "wkv", self.w_kv, 2 * D)
            gkv, r_gkv = self.load_gain_bc(st, "gkv", 8)
            pools = {
                "junk": Pool_(nc, st, "junkC", [128, D], BF16, 1),
                "stat": Pool_(nc, st, "statC", [128, 4], F32, 8),
                "hb": Pool_(nc, st, "hbC", [128, D], BF16, 2),
                "tp_banks": [0, 1],
            }
            xt_pool = Pool_(nc, st, "xtC", [128, D], F32, 3)
            hT_pool = Pool_(nc, st, "hTC", [128, 8, TB], BF16, 2)
            kts_pool = Pool_(nc, st, "ktsC", [128, NTT, H, 128], BF16, 2)
            vs_pool = Pool_(nc, st, "vsC", [128, H, 128], BF16, 3)
            PS, PR = self.psb, self.psr
            bi = 0
            def do_norm(b):
                hT, r_hT = hT_pool.get()
                for tt in range(NTT):
                    r0 = b * TB + tt * 128
                    xt, r_xt = xt_pool.get()
                    S.dma("sp", lambda h, xt=xt, r0=r0: h.dma_start(out=xt[:], in_=self.x1[r0:r0 + 128, :]), writes=[r_xt])
                    self.norm_to_hT(xt, r_xt, gkv, r_gkv, hT, r_hT, tt * 128, pools)
                return hT, r_hT

            nxt = do_norm(0)
            for b in range(nblk):
                hT, r_hT = nxt
                kts, r_kts = kts_pool.get()
                for hh in range(H):
                    pi = 2 + (bi % 3)
                    bi += 1
                    pk = PS[pi][:, 0:TB]
                    for kc in range(8):
                        S.op("pe", lambda h, kc=kc: h.matmul(pk, wkv[:, kc, hh * 128:(hh + 1) * 128], hT[:, kc, :], start=(kc == 0), stop=(kc == 7)),
                             reads=[r_wkv.at(kc, hh * 128), r_hT], writes=[PR[pi]])
                    src = pk.rearrange("p (a b) -> p a b", a=NTT)
                    if hh % 2 == 0:
                        S.op("act", lambda h: h.copy(kts[:, :, hh, :], src), reads=[PR[pi]], writes=[r_kts])
                    else:
                        S.op("dve", lambda h: h.tensor_copy(kts[:, :, hh, :], src), reads=[PR[pi]], writes=[r_kts])
                if b + 1 < nblk:
                    nxt = do_norm(b + 1)
                kb0 = b * NTT
                S.dma("sp", lambda h: h.dma_start(out=self.KT[kb0:kb0 + NTT].rearrange("k p h s -> p k (h s)"), in_=kts[:].rearrange("p k h s -> p k (h s)")), reads=[r_kts])
                for tt in range(NTT):
                    vs, r_vs = vs_pool.get()
                    for half in range(2):
                        pi = 5 + (bi % 3)
                        bi += 1
                        pv = PS[pi]
                        for kc in range(8):
                            S.op("pe", lambda h, kc=kc: h.matmul(pv[:], hT[:, kc, tt * 128:(tt + 1) * 128], wkv[:, kc, D + half * 512:D + (half + 1) * 512], start=(kc == 0), stop=(kc == 7)),
                                 reads=[r_wkv.at(kc, D + half * 512), r_hT], writes=[PR[pi]])
                        dst = vs[:, half * 4:(half + 1) * 4, :]
                        src = pv[:].rearrange("p (a b) -> p a b", a=4)
                        if half == 0:
                            S.op("act", lambda h: h.copy(dst, src), reads=[PR[pi]], writes=[r_vs])
                        else:
                            S.op("dve", lambda h: h.tensor_copy(dst, src), reads=[PR[pi]], writes=[r_vs])
                    kb = b * NTT + tt
                    S.dma("sp", lambda h: h.dma_start(out=self.Vd[kb], in_=vs[:]), reads=[r_vs])

    def phase_D(self):
        nc, S = self.nc, self.S
        NQB = self.NQB
        self._tpi = 0
        st = ExitStack()
        with st:
            def sb(name, shape, dt):
                return st.enter_context(nc.sbuf_tensor(name, shape, dt))
            wq, r_wq = self.load_w_kc(st, "wq", self.w_q, D)
            wo, r_wo = self.load_w_kc(st, "wo", self.w_o, D)
            gpre, r_gpre = self.load_gain_bc(st, "gpreD", 1)
            gpost, r_gpost = self.load_gain_bc(st, "gpostD", 3)
            r_c = Res("constD")
            triI = sb("triI", [128, 128], BF16)
            triC = sb("triC", [128, 128], BF16)
            S.dma("pool", lambda h: h.dma_start(out=triI[:], in_=self.consts[:, 6, :]), writes=[r_c])
            S.dma("pool", lambda h: h.dma_start(out=triC[:], in_=self.consts[:, 7, :]), writes=[r_c])
            amask = sb("amask_sb", [128, 4, 128], F32)
            S.dma("sp", lambda h: h.dma_start(out=amask[:], in_=self.amask[:, :, :]), writes=[r_c])
            sel = sb("selD", [128, 4], F32)
            S.dma("sp", lambda h: h.dma_start(out=sel[:], in_=self.qsel[:, :]), writes=[r_c])
            pools = {
                "junk": Pool_(nc, st, "junkD", [128, D], BF16, 1),
                "stat": Pool_(nc, st, "statD", [128, 4], F32, 8),
                "hb": Pool_(nc, st, "hbD", [128, D], BF16, 2),
                "tp_banks": [7],
            }
            xl_pool = Pool_(nc, st, "xlD", [128, D], F32, 2)
            xq_pool = Pool_(nc, st, "xqD", [128, D], F32, 2)
            y_pool = Pool_(nc, st, "yD", [128, D], F32, 2)
            ss_pool = Pool_(nc, st, "ssD", [128, 2], F32, 4)
            hT_pool = Pool_(nc, st, "hTD", [128, 8, 128], BF16, 2)
            QT_pool = Pool_(nc, st, "QT", [128, H, 128], BF16, 2)
            kt_pool = Pool_(nc, st, "ktD", [128, H, 128], BF16, 7)
            v_pool = Pool_(nc, st, "vD", [128, H, 128], BF16, 7)
            e_pool = Pool_(nc, st, "eD", [128, 4, 128], F32, 12)
            sp_pool = Pool_(nc, st, "spD", [128, 4, 128], BF16, 12)
            eg_pool = Pool_(nc, st, "egD", [128, 4, 128], F32, 12)
            a_pool = Pool_(nc, st, "aD", [128, 4, 128], BF16, 12)
            osb_pool = Pool_(nc, st, "osbD", [128, H, 128], BF16, 2)
            oT_pool = Pool_(nc, st, "oTD", [128, H, 128], BF16, 2)
            PS, PR = self.psb, self.psr
            ZB, AB, OB = [0, 1, 6], [2, 3], [4, 5]

            def v4(ap):
                return ap.rearrange("p (a b) -> p a b", a=4)

            QTs = {}
            xqs = {}

            def gen_pro(m):
                xq, r_xq = xq_pool.get()
                xqs[m] = (xq, r_xq)
                for j in range(4):
                    xl, r_xl = xl_pool.get()
                    r0 = (4 * m + j) * 128
                    S.dma("sp", lambda h: h.dma_start(out=xl[:], in_=self.x1[r0:r0 + 128, :]), writes=[r_xl])
                    if j == 0:
                        S.op("dve", lambda h: h.tensor_scalar(out=xq[:], in0=xl[:], scalar1=sel[:, 0:1], scalar2=None, op0=ALU.mult), reads=[r_xl, r_c], writes=[r_xq])
                    else:
                        S.op("dve", lambda h: h.scalar_tensor_tensor(out=xq[:], in0=xl[:], scalar=sel[:, j:j + 1], in1=xq[:], op0=ALU.mult, op1=ALU.add), reads=[r_xl, r_c, r_xq], writes=[r_xq])
                    yield
                hT, r_hT = hT_pool.get()
                self.norm_to_hT(xq, r_xq, gpre, r_gpre, hT, r_hT, 0, pools)
                yield
                QT, r_QT = QT_pool.get()
                QTs[m] = (QT, r_QT)
                for g in range(2):
                    for hh in range(4):
                        hd = g * 4 + hh
                        for kc in range(8):
                            S.op("pe", lambda h: h.matmul(PS[7][:, hh * 128:(hh + 1) * 128], wq[:, kc, hd * 128:(hd + 1) * 128], hT[:, kc, :], start=(kc == 0), stop=(kc == 7)),
                                 reads=[r_wq, r_hT], writes=[PR[7]])
                    S.op("act", lambda h: h.activation(out=QT[:, g * 4:(g + 1) * 4, :], in_=v4(PS[7][:]), func=AF.Copy, scale=HD ** -0.5), reads=[PR[7]], writes=[r_QT])
                    yield

            def gen_epi(m, osb, r_osb):
                xq, r_xq = xqs[m]
                oT, r_oT = oT_pool.get()
                for half in range(2):
                    pt = PS[7].bitcast(BF16)
                    for j in range(4):
                        c = half * 4 + j
                        S.op("pe", lambda h: h.transpose(pt[:, j * 128:(j + 1) * 128], osb[:, c, :], self.ident_b[:]), reads=[r_osb, self.r_const], writes=[PR[7]])
                    S.op("act", lambda h: h.copy(oT[:, half * 4:half * 4 + 4, :], v4(pt[:, 0:512])), reads=[PR[7]], writes=[r_oT])
                    yield
                y, r_y = y_pool.get()
                ss, r_ss = ss_pool.get()
                for half in range(2):
                    pd = PS[7]
                    for c in range(8):
                        S.op("pe", lambda h: h.matmul(pd[:], oT[:, c, :], wo[:, c, half * 512:(half + 1) * 512], start=(c == 0), stop=(c == 7)), reads=[r_oT, r_wo], writes=[PR[7]])
                    jk, r_jk = pools["junk"].get()
                    S.op("act", lambda h: h.activation(out=jk[:, 0:512], in_=pd[:], func=AF.Square, accum_out=ss[:, half:half + 1]), reads=[PR[7]], writes=[r_jk, r_ss])
                    S.op("dve", lambda h: h.tensor_copy(y[:, half * 512:(half + 1) * 512], pd[:]), reads=[PR[7]], writes=[r_y])
                    yield
                self.post_norm_residual(y, r_y, ss, r_ss, gpost, r_gpost, xq, r_xq, pools)
                S.dma("sp", lambda h: h.dma_start(out=self.xm2[m * 128:(m + 1) * 128, :], in_=y[:]), reads=[r_y])

            def gen_side(epi, pro):
                if epi is not None:
                    yield from epi
                if pro is not None:
                    yield from pro

            for _ in gen_pro(0):
                pass
            pend_epi = None
            for m in range(NQB):
                QT, r_QT = QTs[m]
                for g in range(2):
                    S.op("dve", lambda h: h.memset(PS[OB[g]][:], 0.0), writes=[PR[OB[g]]])
                nkb = 4 * m + 4
                kv_tiles = {}

                def get_kv(kb):
                    if kb not in kv_tiles:
                        kt, r_kt = kt_pool.get()
                        vv, r_vv = v_pool.get()
                        S.dma("sp", lambda h: h.dma_start(out=kt[:], in_=self.KT[kb]), writes=[r_kt])
                        S.dma("sp", lambda h: h.dma_start(out=vv[:], in_=self.Vd[kb]), writes=[r_vv])
                        kv_tiles[kb] = (kt, r_kt, vv, r_vv)
                    return kv_tiles[kb]

                accseq = [nkb - 1, nkb - 1]

                def gen_unit(kidx, g, m=m, QT=QT, r_QT=r_QT, nkb=nkb):
                    kb = nkb - 1 - kidx
                    kt, r_kt, vv, r_vv = get_kv(kb)
                    zb = free_z.pop(0)
                    for hh in range(4):
                        hd = g * 4 + hh
                        S.op("pe", lambda h: h.matmul(PS[zb][:, hh * 128:(hh + 1) * 128], kt[:, hd, :], QT[:, hd, :], start=True, stop=True), reads=[r_kt, r_QT], writes=[PR[zb]])
                    yield
                    e_, r_e = e_pool.get()
                    S.op("act", lambda h: h.activation(out=e_[:], in_=v4(PS[zb][:]), func=AF.Exp), reads=[PR[zb]], writes=[r_e])
                    free_z.append(zb)
                    yield
                    if kb >= 4 * m:
                        mk = amask[:, kb - 4 * m, :].unsqueeze(1).to_broadcast([128, 4, 128])
                        S.op("dve", lambda h: h.tensor_tensor(out=e_[:], in0=e_[:], in1=mk, op=ALU.mult), reads=[r_e, r_c], writes=[r_e])
                        yield
                    sp, r_sp = sp_pool.get()
                    S.op("act", lambda h: h.activation(out=sp[:], in_=e_[:], func=AF.Ln, bias=self.one_col[:, 0:1], scale=1.0), reads=[r_e, self.r_const], writes=[r_sp])
                    yield
                    while accseq[g] != kb:
                        yield
                    S.op("pe", lambda h: h.matmul(PS[AB[g]][:], triI[:], sp[:].rearrange("p a b -> p (a b)"), start=(kidx == 0), stop=True, skip_group_check=True), reads=[r_sp, r_c], writes=[PR[AB[g]]])
                    yield
                    eg, r_eg = eg_pool.get()
                    S.op("act", lambda h: h.activation(out=eg[:], in_=v4(PS[AB[g]][:]), func=AF.Exp, scale=-1.0), reads=[PR[AB[g]]], writes=[r_eg])
                    S.op("pe", lambda h: h.matmul(PS[AB[g]][:], triC[:], sp[:].rearrange("p a b -> p (a b)"), start=False, stop=True, skip_group_check=True), reads=[r_sp, r_c], writes=[PR[AB[g]]])
                    accseq[g] = kb - 1
                    yield
                    a_, r_a = a_pool.get()
                    S.op("dve", lambda h: h.tensor_tensor(out=a_[:], in0=e_[:], in1=eg[:], op=ALU.mult), reads=[r_e, r_eg], writes=[r_a])
                    yield
                    for hh in range(4):
                        hd = g * 4 + hh
                        S.op("pe", lambda h: h.matmul(PS[OB[g]][:, hh * 128:(hh + 1) * 128], a_[:, hh, :], vv[:, hd, :], start=False, stop=(kidx == nkb - 1), skip_group_check=True), reads=[r_a, r_vv], writes=[PR[OB[g]]])

                free_z = list(ZB)
                pending = [(kidx, g) for kidx in range(nkb) for g in range(2)]
                active = []
                side = gen_side(pend_epi, gen_pro(m + 1) if m + 1 < NQB else None)
                side_done = False
                while pending or active or not side_done:
                    while pending and free_z and len(active) < 10:
                        active.append(gen_unit(*pending.pop(0)))
                        next(active[-1])
                    for gn in list(active):
                        try:
                            next(gn)
                        except StopIteration:
                            active.remove(gn)
                    if not side_done:
                        try:
                            next(side)
                        except StopIteration:
                            side_done = True
                osb, r_osb = osb_pool.get()
                S.op("act", lambda h: h.copy(osb[:, 0:4, :], v4(PS[OB[0]][:])), reads=[PR[OB[0]]], writes=[r_osb])
                S.op("dve", lambda h: h.tensor_copy(osb[:, 4:8, :], v4(PS[OB[1]][:])), reads=[PR[OB[1]]], writes=[r_osb])
                pend_epi = gen_epi(m, osb, r_osb)
            for _ in pend_epi:
                pass


def make_consts():
    c = np.zeros((128, 8, 128), np.float32)
    p = np.arange(128)[:, None]
    f = np.arange(128)[None, :]
    same = (p // 64) == (f // 64)
    c[:, 0, :] = np.eye(128, dtype=np.float32)
    c[:, 1, :] = (same & (p <= f)).astype(np.float32)
    c[:, 2, :] = same.astype(np.float32)
    c[:, 3, :] = np.where(same & (f >= p), 0.0, -1e5)
    c[:, 4, :] = np.where(same & (p > f), 0.0, 1e5)
    c[:, 5, :] = 1.0
    c[:, 6, :] = (p >= f).astype(np.float32)
    c[:, 7, :] = (p < f).astype(np.float32)
    return c


def make_amask(r):
    a = np.zeros((128, 4, 128), np.float32)
    s_ = np.arange(128)[:, None]
    t_ = np.arange(128)[None, :]
    for j in range(4):
        if j < r:
            a[:, j, :] = 1.0
        elif j == r:
            a[:, j, :] = (s_ < t_).astype(np.float32)
    return a


def make_qsel(r):
    q = np.zeros((128, 4), np.float32)
    q[:, r] = 1.0
    return q


def make_selall():
    sel = np.zeros((8, 1024), np.float32)
    for hh in range(8):
        sel[hh, hh * 128:(hh + 1) * 128] = 1
    return sel


_NC_CACHE = {}


def _get_nc(T, NQB):
    key = (T, NQB)
    if key not in _NC_CACHE:
        kb = KB(T, NQB, phases="ABCDE")
        kb.chain_dt = BF16
        _NC_CACHE[key] = kb.build()
    return _NC_CACHE[key]


def kernel(x, mix_pre_gain, mix_post_gain, mlp_pre_gain, mlp_post_gain, mlp_w_up, mlp_w_down,
           gdn_w_in, gdn_conv_w, gdn_a_log, gdn_dt_bias, gdn_out_gain, gdn_w_out,
           kv_gain, w_kv, sb_w_q, sb_w_o):
    f = lambda a: np.ascontiguousarray(np.asarray(a), dtype=np.float32)
    x = f(x)
    B, T, _ = x.shape
    n_cores = 8
    per_b = n_cores // B
    NQB = T // 128 // per_b
    gains = f(np.concatenate([f(mix_pre_gain), f(mix_post_gain), f(mlp_pre_gain), f(mlp_post_gain), f(kv_gain)[None]], 0))
    conv_w = f(gdn_conv_w)[0]
    common = {
        "gains": gains,
        "mlp_w_up": f(mlp_w_up),
        "mlp_w_down": f(mlp_w_down),
        "gdn_w_in": f(gdn_w_in)[0],
        "gdn_w_out": f(gdn_w_out)[0],
        "w_kv": f(w_kv),
        "sb_w_q": f(sb_w_q)[0],
        "sb_w_o": f(sb_w_o)[0],
        "consts": make_consts(),
        "convw_l": f(conv_w.reshape(4, 24, 128).transpose(2, 1, 0)),
        "small_bc": f(np.broadcast_to(np.stack([f(gdn_a_log)[0], f(gdn_dt_bias)[0]])[None], (128, 2, 8))),
        "ogain_bc": f(np.broadcast_to(f(gdn_out_gain)[0][None], (128, 128))),
        "selall": make_selall(),
    }
    in_maps = []
    for c in range(n_cores):
        b, r = c // per_b, c % per_b
        d = dict(common)
        d["x"] = x[b]
        d["qsel"] = make_qsel(r)
        d["amask"] = make_amask(r)
        in_maps.append(d)
    nc = _get_nc(T, NQB)
    res = run_bass_kernel_spmd(nc, in_maps, core_ids=list(range(n_cores)))
    out = np.zeros((B, T, D), np.float32)
    for c in range(n_cores):
        b, r = c // per_b, c % per_b
        o = res.results[c]["out"]
        for m in range(NQB):
            i = 4 * m + r
            out[b, i * 128:(i + 1) * 128, :] = o[m * 128:(m + 1) * 128, :]
    return out
```

```python
import numpy as np
from contextlib import ExitStack
import concourse.bass as bass
import concourse.mybir as mybir
from concourse.bass_utils import run_bass_kernel_spmd

F32 = mybir.dt.float32
BF16 = mybir.dt.bfloat16
I32 = mybir.dt.int32
U32 = mybir.dt.uint32
AF = mybir.ActivationFunctionType
ALU = mybir.AluOpType

D = 1024
H = 8
HD = 128
FF = 4096
EPS = 1e-6
GIN = 4 * D + 2 * H


class Res:
    __slots__ = ("name", "writer", "readers", "excl")

    def __init__(self, name="", excl=False):
        self.name = name
        self.writer = None
        self.readers = []
        self.excl = excl


class Tok:
    __slots__ = ("sem", "val", "knows", "eng")

    def __init__(self, sem, val, knows, eng):
        self.sem = sem
        self.val = val
        self.knows = knows
        self.eng = eng


class _Rec:
    def __init__(self):
        self.call = None

    def __getattr__(self, name):
        def f(*a, **k):
            self.call = (name, a, k)
            return self
        return f


def _record(fn):
    if fn is None:
        return None
    r = _Rec()
    fn(r)
    assert r.call is not None
    return r.call


class Eng:
    def __init__(self, name, sem):
        self.name = name
        self.sem = sem
        self.count = 0
        self.known = {}
        self.ops = []


class Sched:
    DMA_NS = 8

    def __init__(self, nc, stack):
        self.nc = nc
        self.engs = {}
        for n in ("pe", "act", "dve", "pool", "sp"):
            sem = stack.enter_context(nc.semaphore("s_" + n))
            self.engs[n] = Eng(n, sem)
        self.dsem = {}
        for q in ("sp", "pool", "act"):
            sems = [stack.enter_context(nc.semaphore("d_%s%d" % (q, i))) for i in range(self.DMA_NS)]
            self.dsem[q] = {"sems": sems, "uses": [0] * self.DMA_NS, "last": [None] * self.DMA_NS, "i": 0}
        self.nwaits = 0

    def _merge(self, known, tok):
        for s, v in tok.knows.items():
            if known.get(s, 0) < v:
                known[s] = v
        if known.get(tok.sem, 0) < tok.val:
            known[tok.sem] = tok.val

    def _deps(self, eng, reads, writes, extra=()):
        e = self.engs[eng]
        deps = list(extra)
        for r in reads:
            if r.writer is not None:
                deps.append((r.writer, True))
        for w in writes:
            if w.writer is not None:
                deps.append((w.writer, False))
            for t in w.readers:
                deps.append((t, False))
        best = {}
        for t, raw in deps:
            if t is None:
                continue
            if t.eng == eng and not raw:
                continue
            if e.known.get(t.sem, 0) >= t.val:
                continue
            k = id(t.sem)
            if k not in best or best[k][1] < t.val:
                best[k] = (t.sem, t.val)
            self._merge(e.known, t)
        waits = list(best.values())
        self.nwaits += len(waits)
        return waits

    def _finish(self, tok, reads, writes):
        for r in reads:
            r.readers.append(tok)
        for w in writes:
            w.writer = tok
            w.readers = []

    @staticmethod
    def _flat(xs):
        out = []
        for x in xs:
            if isinstance(x, (list, tuple)):
                out.extend(Sched._flat(x))
            else:
                out.append(x)
        return out

    def op(self, eng, fn, reads=(), writes=()):
        reads = self._flat(reads)
        writes = self._flat(writes)
        if any(r.excl for r in reads):
            writes = list(writes) + [r for r in reads if r.excl]
            reads = [r for r in reads if not r.excl]
        e = self.engs[eng]
        waits = self._deps(eng, reads, writes)
        e.count += 1
        tok = Tok(e.sem, e.count, dict(e.known), eng)
        e.ops.append((waits, _record(fn), (e.sem, 1)))
        self._finish(tok, reads, writes)
        return tok

    def dma(self, q, fn, reads=(), writes=()):
        reads = self._flat(reads)
        writes = self._flat(writes)
        e = self.engs[q]
        d = self.dsem[q]
        k = d["i"] % self.DMA_NS
        d["i"] += 1
        extra = [(d["last"][k], True)] if d["last"][k] is not None else []
        waits = self._deps(q, reads, writes, extra)
        d["uses"][k] += 1
        tok = Tok(d["sems"][k], 16 * d["uses"][k], dict(e.known), "dma_" + q)
        d["last"][k] = tok
        e.ops.append((waits, _record(fn), (d["sems"][k], 16)))
        self._finish(tok, reads, writes)
        return tok

    def barrier(self):
        toks = []
        for n, e in self.engs.items():
            if e.count:
                toks.append(Tok(e.sem, e.count, {}, n))
        for q, d in self.dsem.items():
            for t in d["last"]:
                if t is not None:
                    toks.append(t)
        for n, e in self.engs.items():
            waits = []
            for t in toks:
                if t.eng == n:
                    continue
                if e.known.get(t.sem, 0) >= t.val:
                    continue
                waits.append((t.sem, t.val))
                e.known[t.sem] = t.val
            if waits:
                e.ops.append((waits, None, None))

    def emit(self):
        nc = self.nc
        handles = {"pe": "tensor", "act": "scalar", "dve": "vector", "pool": "gpsimd", "sp": "sync"}
        with nc.Block() as block:
            for n, attr in handles.items():
                e = self.engs[n]

                def body(h, e=e):
                    for waits, fn, inc in e.ops:
                        for s, v in waits:
                            h.wait_ge(s, v)
                        if fn is not None:
                            getattr(h, fn[0])(*fn[1], **fn[2]).then_inc(inc[0], inc[1])
                getattr(block, attr)(body)


class WRes(list):
    def __init__(self):
        super().__init__()
        self.map = {}
        self.step = 1

    def at(self, kc, col):
        return self.map[(kc, (col // self.step) * self.step)]


class Pool_:
    def __init__(self, nc, stack, name, shape, dtype, n):
        self.tiles = [stack.enter_context(nc.sbuf_tensor("%s%d" % (name, i), shape, dtype)) for i in range(n)]
        self.res = [Res("%s%d" % (name, i)) for i in range(n)]
        self.i = 0

    def get(self):
        k = self.i % len(self.tiles)
        self.i += 1
        return self.tiles[k], self.res[k]


class KB:
    def __init__(self, T, NQB, phases="ABCDE", feed=(), dump=()):
        self.T = T
        self.NQB = NQB
        self.NT = T // 128
        self.phases = phases
        self.feed = set(feed)
        self.dump = set(dump)
        self.nc = bass.Bass("TRN2", target_bir_lowering=False)
        self.chain_dt = F32
        self.a_stage = 9
        self.d_stage = 9

    def din(self, name, shape, dt=F32):
        return self.nc.dram_tensor(name, list(shape), dt, kind="ExternalInput")

    def dscratch(self, name, shape, dt=F32):
        kind = "Internal"
        if name in self.feed:
            kind = "ExternalInput"
        elif name in self.dump:
            kind = "ExternalOutput"
        return self.nc.dram_tensor(name, list(shape), dt, kind=kind)

    def build(self):
        nc = self.nc
        T, NQB, NT = self.T, self.NQB, self.NT
        self.x = self.din("x", [T, D])
        self.gains = self.din("gains", [9, D])
        self.w_up = self.din("mlp_w_up", [2, D, FF])
        self.w_down = self.din("mlp_w_down", [2, FF, D])
        self.w_in = self.din("gdn_w_in", [D, GIN])
        self.w_out = self.din("gdn_w_out", [D, D])
        self.w_kv = self.din("w_kv", [D, 2 * D])
        self.w_q = self.din("sb_w_q", [D, D])
        self.w_o = self.din("sb_w_o", [D, D])
        self.consts = self.din("consts", [128, 8, 128])
        self.convw_l = self.din("convw_l", [128, 24, 4])
        self.small_bc = self.din("small_bc", [128, 2, 8])
        self.ogain_bc = self.din("ogain_bc", [128, 128])
        self.selall_d = self.din("selall", [8, 1024])
        if "dbg_bg" in self.dump:
            self.dbg_bg = self.nc.dram_tensor("dbg_bg", [T, 2, 8], F32, kind="ExternalOutput")
        if "dbg_qkv" in self.dump:
            self.dbg_qkv = self.nc.dram_tensor("dbg_qkv", [4, 128, 8, T], BF16, kind="ExternalOutput")
        if "dbg_o" in self.dump:
            self.dbg_o = self.nc.dram_tensor("dbg_o", [T, D], F32, kind="ExternalOutput")
        self.qsel = self.din("qsel", [128, 4])
        if "dbg_ao" in self.dump:
            self.dbg_ao = self.nc.dram_tensor("dbg_ao", [NQB * 128, D], F32, kind="ExternalOutput")
        self.amask = self.din("amask", [128, 4, 128])
        self.out = nc.dram_tensor("out", [NQB * 128, D], F32, kind="ExternalOutput")
        self.xm = self.dscratch("xm", [T, D])
        self.x1 = self.dscratch("x1", [T, D])
        self.xm2 = self.dscratch("xm2", [NQB * 128, D])
        self.qkvg = self.dscratch("qkvg", [4, 128, H, T], BF16)
        self.scd = self.dscratch("scd", [NT, 128, 12, 8])
        self.gctd = self.dscratch("gctd", [NT, 8, 128])
        self.KT = self.dscratch("KT", [NT, 128, H, 128], BF16)
        self.Vd = self.dscratch("Vd", [NT, 128, H, 128], BF16)
        with ExitStack() as st:
            self.S = S = Sched(nc, st)
            self.psb = [st.enter_context(nc.psum_tensor("psb%d" % i, [128, 512], F32)) for i in range(8)]
            self.psr = [Res("ps%d" % i, excl=True) for i in range(8)]
            self.ident_f = st.enter_context(nc.sbuf_tensor("ident_f", [128, 128], F32))
            self.ident_b = st.enter_context(nc.sbuf_tensor("ident_b", [128, 128], BF16))
            self.r_const = Res("const")
            self.eps_col = st.enter_context(nc.sbuf_tensor("eps_col", [128, 2], F32))
            S.op("pool", lambda h: h.memset(self.eps_col[:], EPS), writes=[self.r_const])
            self.one_col = st.enter_context(nc.sbuf_tensor("one_col", [128, 2], F32))
            S.op("pool", lambda h: h.memset(self.one_col[:], 1.0), writes=[self.r_const])
            S.dma("sp", lambda h: h.dma_start(out=self.ident_f[:], in_=self.consts[:, 0, :]), writes=[self.r_const])
            S.dma("pool", lambda h: h.dma_start(out=self.ident_b[:], in_=self.consts[:, 0, :]), writes=[self.r_const])
            self.final_toks = []
            if "A" in self.phases or "1" in self.phases:
                self.phase_A1()
                S.barrier()
            if "A" in self.phases or "2" in self.phases:
                self.phase_A2()
                S.barrier()
            if "B" in self.phases:
                with ExitStack() as ps:
                    self.phase_mlp(ps, 0, self.xm, self.x1, T)
                S.barrier()
            if "C" in self.phases:
                self.phase_C()
                S.barrier()
            if "D" in self.phases:
                self.phase_D()
                S.barrier()
            if "E" in self.phases:
                with ExitStack() as ps:
                    self.phase_mlp(ps, 1, self.xm2, self.out, NQB * 128)
                S.barrier()
            S.emit()
        return nc

    def dbgdump(self, name, ap, res, shape):
        if "dbgx" not in self.dump:
            return
        if not hasattr(self, "_dbgdone"):
            self._dbgdone = {}
        if name in self._dbgdone:
            return
        t = self.nc.dram_tensor("dbg_" + name, list(shape), F32, kind="ExternalOutput")
        self._dbgdone[name] = t
        self.S.dma("pool", lambda h: h.dma_start(out=t.ap(), in_=ap), reads=list(res))

    def load_gain_bc(self, st, name, row):
        t = st.enter_context(self.nc.sbuf_tensor(name, [128, D], F32))
        r = Res(name)
        self.S.dma("sp", lambda h: h.dma_start(out=t[:], in_=self.gains[row, :].partition_broadcast(128)), writes=[r])
        return t, r

    def load_w_kc(self, st, name, w_ap, ncols, q="pool"):
        t = st.enter_context(self.nc.sbuf_tensor(name, [128, 8, ncols], BF16))
        r = WRes()
        wv = w_ap.rearrange("(kc p) n -> p kc n", p=128)
        step = 1024
        r.step = step
        for c0 in range(0, ncols, step):
            for kc in range(8):
                c1 = min(ncols, c0 + step)
                ri = Res(name)
                r.append(ri)
                r.map[(kc, c0)] = ri
                self.S.dma(q, lambda h, kc=kc, c0=c0, c1=c1: h.dma_start(out=t[:, kc, c0:c1], in_=wv[:, kc, c0:c1]), writes=[ri])
        return t, r

    def rms_rstd(self, ss_ap, rstd_ap, r_ss, r_rstd, n):
        S = self.S
        S.op("act", lambda h: h.activation(out=rstd_ap, in_=ss_ap, func=AF.Ln, bias=self.eps_col[:, 0:1], scale=1.0 / n),
             reads=[r_ss, self.r_const], writes=[r_rstd])
        S.op("act", lambda h: h.activation(out=rstd_ap, in_=rstd_ap, func=AF.Exp, scale=-0.5),
             reads=[r_rstd], writes=[r_rstd])

    def norm_to_hT(self, xt, r_xt, gain, r_gain, hT, r_hT, col0, pools):
        S = self.S
        junk, r_junk = pools["junk"].get()
        st_, r_st = pools["stat"].get()
        hb, r_hb = pools["hb"].get()
        S.op("act", lambda h: h.activation(out=junk[:], in_=xt[:], func=AF.Square, accum_out=st_[:, 0:1]),
             reads=[r_xt], writes=[r_junk, r_st])
        self.rms_rstd(st_[:, 0:1], st_[:, 1:2], r_st, r_st, D)
        S.op("dve", lambda h: h.scalar_tensor_tensor(out=hb[:], in0=xt[:], scalar=st_[:, 1:2], in1=gain[:], op0=ALU.mult, op1=ALU.mult),
             reads=[r_xt, r_st, r_gain], writes=[r_hb])
        for half in range(2):
            pi = pools["tp_banks"][self._tpi % len(pools["tp_banks"])]
            self._tpi += 1
            pst = self.psb[pi].bitcast(BF16)
            for j in range(4):
                kc = half * 4 + j
                S.op("pe", lambda h, j=j, kc=kc, pst=pst: h.transpose(pst[:, j * 128:(j + 1) * 128], hb[:, kc * 128:(kc + 1) * 128], self.ident_b[:]),
                     reads=[r_hb, self.r_const], writes=[self.psr[pi]])
            eng = "act" if half == 0 else "dve"
            src = pst[:, 0:512].rearrange("p (a b) -> p a b", a=4)
            dst = hT[:, half * 4:half * 4 + 4, col0:col0 + 128]
            if eng == "act":
                S.op("act", lambda h, src=src, dst=dst: h.copy(dst, src), reads=[self.psr[pi]], writes=[r_hT])
            else:
                S.op("dve", lambda h, src=src, dst=dst: h.tensor_copy(dst, src), reads=[self.psr[pi]], writes=[r_hT])

    def post_norm_residual(self, y, r_y, ss_ap, r_ss, gain, r_gain, xres, r_xres, pools):
        S = self.S
        st_, r_st = pools["stat"].get()
        S.op("dve", lambda h: h.tensor_tensor(out=st_[:, 0:1], in0=ss_ap[:, 0:1], in1=ss_ap[:, 1:2], op=ALU.add), reads=[r_ss], writes=[r_st])
        self.rms_rstd(st_[:, 0:1], st_[:, 1:2], r_st, r_st, D)
        S.op("dve", lambda h: h.scalar_tensor_tensor(out=y[:], in0=y[:], scalar=st_[:, 1:2], in1=gain[:], op0=ALU.mult, op1=ALU.mult),
             reads=[r_y, r_st, r_gain], writes=[r_y])
        S.op("dve", lambda h: h.tensor_tensor(out=y[:], in0=y[:], in1=xres[:], op=ALU.add), reads=[r_y, r_xres], writes=[r_y])

    def phase_mlp(self, st, layer, src, dst, ntok):
        nc, S = self.nc, self.S
        TB = 512 if ntok % 512 == 0 else 256
        nblk = ntok // TB
        self._tpi = 0
        wup, r_wup = self.load_w_kc(st, "wupL%d" % layer, self.w_up[layer], FF)
        wdn = st.enter_context(nc.sbuf_tensor("wdnL%d" % layer, [128, 32, D], BF16))
        r_wdn = []
        wdv = self.w_down[layer].rearrange("(fc p) n -> p fc n", p=128)
        for fc in range(0, 32, 2):
            ri = Res("wdnL%d" % layer)
            r_wdn.append(ri)
            S.dma("pool", lambda h, fc=fc: h.dma_start(out=wdn[:, fc:fc + 2, :], in_=wdv[:, fc:fc + 2, :]), writes=[ri])
        gpre, r_gpre = self.load_gain_bc(st, "gpreL%d" % layer, 4 + layer)
        gpost, r_gpost = self.load_gain_bc(st, "gpostL%d" % layer, 6 + layer)
        pools = {
            "junk": Pool_(nc, st, "junkL%d" % layer, [128, D], BF16, 1),
            "stat": Pool_(nc, st, "statL%d" % layer, [128, 4], F32, 8),
            "hb": Pool_(nc, st, "hbL%d" % layer, [128, D], BF16, 1),
            "tp_banks": [0, 1],
        }
        xt_pool = Pool_(nc, st, "xtL%d" % layer, [128, D], F32, 2)
        y_pool = Pool_(nc, st, "yL%d" % layer, [128, D], F32, 2)
        hT_pool = Pool_(nc, st, "hTL%d" % layer, [128, 8, TB], BF16, 2)
        actT_pool = Pool_(nc, st, "actTL%d" % layer, [128, 32, TB], BF16, 1)
        relu_pool = Pool_(nc, st, "reluL%d" % layer, [128, TB], F32, 1)
        ss_pool = Pool_(nc, st, "ssL%d" % layer, [128, 2], F32, 4)
        up_banks = [2, 3]
        dn_banks = [4, 5, 6, 7]
        upi = 0
        stage = getattr(self, "dbg_stage", 9)
        if stage < 1:
            return
        def do_norm(b):
            hT, r_hT = hT_pool.get()
            for tt in range(TB // 128):
                r0 = b * TB + tt * 128
                xt, r_xt = xt_pool.get()
                S.dma("sp", lambda h, xt=xt, r0=r0: h.dma_start(out=xt[:], in_=src[r0:r0 + 128, :]), writes=[r_xt])
                self.norm_to_hT(xt, r_xt, gpre, r_gpre, hT, r_hT, tt * 128, pools)
            return hT, r_hT

        nxt = do_norm(0)
        for b in range(nblk):
            hT, r_hT = nxt
            if stage < 2:
                continue
            actT, r_actT = actT_pool.get()
            for fc in range(32):
                pi = up_banks[upi % 2]
                upi += 1
                pu = self.psb[pi][:, 0:TB]
                for kc in range(8):
                    S.op("pe", lambda h, pu=pu, kc=kc, fc=fc, hT=hT: h.matmul(pu, wup[:, kc, fc * 128:(fc + 1) * 128], hT[:, kc, :], start=(kc == 0), stop=(kc == 7)),
                         reads=[r_wup.at(kc, fc * 128), r_hT], writes=[self.psr[pi]])
                rl, r_rl = relu_pool.get()
                S.op("act", lambda h, pu=pu, rl=rl: h.activation(out=rl[:], in_=pu, func=AF.Relu), reads=[self.psr[pi]], writes=[r_rl])
                S.op("dve", lambda h, rl=rl, fc=fc, actT=actT: h.tensor_tensor(out=actT[:, fc, :], in0=rl[:], in1=rl[:], op=ALU.mult), reads=[r_rl], writes=[r_actT])
            if b + 1 < nblk:
                nxt = do_norm(b + 1)
            if stage < 3:
                continue
            for tt in range(TB // 128):
                r0 = b * TB + tt * 128
                y, r_y = y_pool.get()
                ss, r_ss = ss_pool.get()
                for half in range(2):
                    pi = dn_banks[(tt * 2 + half) % 4]
                    pd = self.psb[pi]
                    for fc in range(32):
                        S.op("pe", lambda h, pd=pd, fc=fc, tt=tt, half=half, actT=actT: h.matmul(pd[:], actT[:, fc, tt * 128:(tt + 1) * 128], wdn[:, fc, half * 512:(half + 1) * 512], start=(fc == 0), stop=(fc == 31)),
                             reads=[r_actT, r_wdn[fc // 2]], writes=[self.psr[pi]])
                    import os
                    jk, r_jk = pools["junk"].get()
                    if not os.environ.get("NOSQ"):
                        S.op("act", lambda h, pd=pd, jk=jk, ss=ss, half=half: h.activation(out=jk[:, 0:512], in_=pd[:], func=AF.Square, accum_out=ss[:, half:half + 1]),
                             reads=[self.psr[pi]], writes=[r_jk, r_ss])
                    if not os.environ.get("NOCP"):
                        S.op("dve", lambda h, pd=pd, y=y, half=half: h.tensor_copy(y[:, half * 512:(half + 1) * 512], pd[:]), reads=[self.psr[pi]], writes=[r_y])
                if stage < 4:
                    continue
                xr, r_xr = xt_pool.get()
                S.dma("sp", lambda h, xr=xr, r0=r0: h.dma_start(out=xr[:], in_=src[r0:r0 + 128, :]), writes=[r_xr])
                self.post_norm_residual(y, r_y, ss, r_ss, gpost, r_gpost, xr, r_xr, pools)
                if stage < 5:
                    continue
                t = S.dma("sp", lambda h, y=y, r0=r0: h.dma_start(out=dst[r0:r0 + 128, :], in_=y[:]), reads=[r_y])
                self.final_toks.append(t)

    def phase_A1(self):
        nc, S = self.nc, self.S
        T = self.T
        TB = 512 if T % 512 == 0 else 256
        NTT = TB // 128
        nblk = T // TB
        self._tpi = 0
        st = ExitStack()
        with st:
            def sb(name, shape, dt):
                return st.enter_context(nc.sbuf_tensor(name, shape, dt))
            win, r_win = self.load_w_kc(st, "win", self.w_in, GIN)
            gpre, r_gpre = self.load_gain_bc(st, "gpreA", 0)
            r_c = Res("constA1")
            cst = sb("cstA1", [128, 3, 128], F32)
            S.dma("sp", lambda h: h.dma_start(out=cst[:], in_=self.consts[:, 0:3, :]), writes=[r_c])
            ident_f = cst[:, 0, :]
            UTbd = cst[:, 1, :]
            BDones = cst[:, 2, :]
            ones_b = sb("ones_b", [128, 128], BF16)
            S.op("pool", lambda h: h.memset(ones_b[:], 1.0), writes=[r_c])
            cw = sb("cw", [128, 24, 4], F32)
            S.dma("sp", lambda h: h.dma_start(out=cw[:], in_=self.convw_l[:, :, :]), writes=[r_c])
            small = sb("small", [128, 2, 8], F32)
            S.dma("sp", lambda h: h.dma_start(out=small[:], in_=self.small_bc[:, :, :]), writes=[r_c])
            negA = sb("negA", [128, 8], F32)
            S.op("act", lambda h: h.activation(out=negA[:], in_=small[:, 0, :], func=AF.Exp), reads=[r_c], writes=[r_c])
            S.op("dve", lambda h: h.tensor_scalar(out=negA[:], in0=negA[:], scalar1=-1.0, scalar2=None, op0=ALU.mult), reads=[r_c], writes=[r_c])
            dtb = small[:, 1, :]
            cm = sb("cm", [128, 2], F32)
            S.op("pool", lambda h: h.memset(cm[:], 0.0), writes=[r_c])
            S.op("pool", lambda h: h.memset(cm[0:64, 0:1], 1.0), writes=[r_c])
            S.op("pool", lambda h: h.memset(cm[64:128, 1:2], 1.0), writes=[r_c])
            halo = sb("halo", [128, 24, 3], BF16)
            r_halo = Res("halo")
            S.op("pool", lambda h: h.memset(halo[:], 0.0), writes=[r_halo])
            dg = sb("dg", [128, 24, 4, 128], BF16)
            for cc_ in range(24):
                for k_ in range(4):
                    S.op("dve", lambda h: h.tensor_scalar(out=dg[:, cc_, k_, :], in0=ident_f, scalar1=cw[:, cc_, k_:k_ + 1], scalar2=None, op0=ALU.mult), reads=[r_c], writes=[r_c])
            slot_banks = [3, 4, 5, 6, 7]
            pools = {
                "junk": Pool_(nc, st, "junkA", [128, D], BF16, 1),
                "stat": Pool_(nc, st, "statA", [128, 4], F32, 8),
                "hb": Pool_(nc, st, "hbA", [128, D], BF16, 2),
                "tp_banks": [0],
            }
            xt_pool = Pool_(nc, st, "xtA", [128, D], F32, 2)
            hT_pool = Pool_(nc, st, "hTA", [128, 8, TB], BF16, 2)
            pre_pool = Pool_(nc, st, "pre", [128, TB + 4], BF16, 8)
            e_pool = Pool_(nc, st, "eA", [128, TB], F32, 8)
            sil_pool = Pool_(nc, st, "sil", [128, TB], F32, 8)
            sqb_pool = Pool_(nc, st, "sqb", [128, TB], BF16, 8)
            rinv_pool = Pool_(nc, st, "rinv", [128, TB], F32, 8)
            oc_pool = Pool_(nc, st, "ocA", [128, TB], BF16, 8)
            sc_pool = Pool_(nc, st, "sc", [128, 12, 8], F32, 4)
            gcT_pool = Pool_(nc, st, "gcT", [8, 128], F32, 4)
            PS, PR = self.psb, self.psr
            pbi = 0
            hTs = {}

            def gen_norm(b):
                hT, r_hT = hT_pool.get()
                hTs[b] = (hT, r_hT)
                for tt in range(NTT):
                    r0 = b * TB + tt * 128
                    xt, r_xt = xt_pool.get()
                    S.dma("sp", lambda h: h.dma_start(out=xt[:], in_=self.x[r0:r0 + 128, :]), writes=[r_xt])
                    self.norm_to_hT(xt, r_xt, gpre, r_gpre, hT, r_hT, tt * 128, pools)
                    yield

            for _ in gen_norm(0):
                pass
            for b in range(nblk):
                hT, r_hT = hTs[b]
                def gen_sc(b=b, hT=hT, r_hT=r_hT):
                    for tt in range(NTT):
                        sc, r_sc = sc_pool.get()
                        gcT, r_gcT = gcT_pool.get()
                        pb = PS[1][:, 0:16]
                        for kc in range(8):
                            S.op("pe", lambda h: h.matmul(pb, hT[:, kc, tt * 128:(tt + 1) * 128], win[:, kc, 4 * D:4 * D + 16], start=(kc == 0), stop=(kc == 7)),
                                 reads=[r_hT, r_win], writes=[PR[1]])
                        S.op("act", lambda h: h.activation(out=sc[:, 9, :], in_=pb[:, 0:8], func=AF.Exp, scale=-1.0), reads=[PR[1]], writes=[r_sc])
                        S.op("dve", lambda h: h.tensor_tensor(out=sc[:, 1, :], in0=pb[:, 8:16], in1=dtb, op=ALU.add), reads=[PR[1], r_c], writes=[r_sc])
                        S.op("dve", lambda h: h.tensor_scalar(out=sc[:, 9, :], in0=sc[:, 9, :], scalar1=1.0, scalar2=None, op0=ALU.add), reads=[r_sc], writes=[r_sc])
                        S.op("dve", lambda h: h.reciprocal(out=sc[:, 0, :], in_=sc[:, 9, :]), reads=[r_sc], writes=[r_sc])
                        yield
                        S.op("act", lambda h: h.activation(out=sc[:, 1, :], in_=sc[:, 1, :], func=AF.Exp), reads=[r_sc], writes=[r_sc])
                        S.op("act", lambda h: h.activation(out=sc[:, 1, :], in_=sc[:, 1, :], func=AF.Ln, bias=self.one_col[:, 0:1], scale=1.0), reads=[r_sc, self.r_const], writes=[r_sc])
                        yield
                        S.op("dve", lambda h: h.tensor_tensor(out=sc[:, 1, :], in0=sc[:, 1, :], in1=negA[:], op=ALU.mult), reads=[r_sc, r_c], writes=[r_sc])
                        pg = PS[2][:, 0:16]
                        S.op("pe", lambda h: h.matmul(pg[:, 0:8], UTbd, sc[:, 1, :], start=True, stop=True), reads=[r_sc, r_c], writes=[PR[2]])
                        S.op("pe", lambda h: h.matmul(pg[:, 8:16], BDones, sc[:, 1, :], start=True, stop=True), reads=[r_sc, r_c], writes=[PR[2]])
                        S.op("dve", lambda h: h.tensor_copy(sc[:, 2:4, :], pg.rearrange("p (a b) -> p a b", a=2)), reads=[PR[2]], writes=[r_sc])
                        yield
                        S.op("act", lambda h: h.activation(out=sc[:, 4, :], in_=sc[:, 2, :], func=AF.Exp), reads=[r_sc], writes=[r_sc])
                        S.op("dve", lambda h: h.tensor_tensor(out=sc[:, 5, :], in0=sc[:, 4, :], in1=sc[:, 0, :], op=ALU.mult), reads=[r_sc], writes=[r_sc])
                        S.op("dve", lambda h: h.tensor_tensor(out=sc[:, 6, :], in0=sc[:, 3, :], in1=sc[:, 2, :], op=ALU.subtract), reads=[r_sc], writes=[r_sc])
                        S.op("act", lambda h: h.activation(out=sc[:, 6, :], in_=sc[:, 6, :], func=AF.Exp), reads=[r_sc], writes=[r_sc])
                        yield
                        S.op("dve", lambda h: h.tensor_scalar(out=sc[:, 7, :], in0=sc[:, 6, :], scalar1=cm[:, 0:1], scalar2=None, op0=ALU.mult), reads=[r_sc, r_c], writes=[r_sc])
                        S.op("dve", lambda h: h.tensor_scalar(out=sc[:, 8, :], in0=sc[:, 6, :], scalar1=cm[:, 1:2], scalar2=None, op0=ALU.mult), reads=[r_sc, r_c], writes=[r_sc])
                        S.op("pe", lambda h: h.transpose(PS[0][0:8, 0:128], sc[:, 2, :], ident_f), reads=[r_sc, r_c], writes=[PR[0]])
                        S.op("dve", lambda h: h.tensor_copy(gcT[:], PS[0][0:8, 0:128]), reads=[PR[0]], writes=[r_gcT])
                        ti = b * NTT + tt
                        S.dma("sp", lambda h: h.dma_start(out=self.scd[ti], in_=sc[:]), reads=[r_sc])
                        S.dma("sp", lambda h: h.dma_start(out=self.gctd[ti], in_=gcT[:]), reads=[r_gcT])
                        yield
                        if "dbg_bg" in self.dump:
                            r0 = b * TB + tt * 128
                            S.dma("sp", lambda h: h.dma_start(out=self.dbg_bg[r0:r0 + 128, :, :], in_=sc[:, 0:2, :]), reads=[r_sc])

                def gen_chunk(cc, hT=hT, r_hT=r_hT, b=b):
                    oc, r_oc = oc_pool.get()
                    which_, hh_ = cc // 8, cc % 8

                    def store():
                        S.dma("sp", lambda h: h.dma_start(out=self.qkvg[which_, :, hh_, b * TB:(b + 1) * TB], in_=oc[:]), reads=[r_oc])
                    pi = free_banks.pop(0)
                    pp = PS[pi][:, 0:TB]
                    for kc in range(8):
                        S.op("pe", lambda h: h.matmul(pp, win[:, kc, cc * 128:(cc + 1) * 128], hT[:, kc, :], start=(kc == 0), stop=(kc == 7)),
                             reads=[r_hT, r_win.at(kc, cc * 128)], writes=[PR[pi]])
                    yield
                    e_, r_e = e_pool.get()
                    if cc >= 24:
                        hh = cc - 24
                        S.op("act", lambda h: h.activation(out=e_[:], in_=pp, func=AF.Exp, scale=-1.0), reads=[PR[pi]], writes=[r_e])
                        yield
                        S.op("act", lambda h: h.activation(out=e_[:], in_=e_[:], func=AF.Ln, bias=self.one_col[:, 0:1], scale=1.0), reads=[r_e, self.r_const], writes=[r_e])
                        yield
                        S.op("act", lambda h: h.activation(out=e_[:], in_=e_[:], func=AF.Exp, scale=-1.0), reads=[r_e], writes=[r_e])
                        yield
                        S.op("dve", lambda h: h.tensor_tensor(out=oc[:], in0=pp, in1=e_[:], op=ALU.mult), reads=[PR[pi], r_e], writes=[r_oc])
                        store()
                        free_banks.append(pi)
                        return
                    pre, r_pre = pre_pool.get()
                    S.op("pool", lambda h: h.tensor_copy(pre[:, 0:3], halo[:, cc, :]), reads=[r_halo], writes=[r_pre])
                    S.op("dve", lambda h: h.tensor_copy(pre[:, 3:3 + TB], pp), reads=[PR[pi]], writes=[r_pre])
                    yield
                    S.op("pool", lambda h: h.tensor_copy(halo[:, cc, :], pre[:, TB:TB + 3]), reads=[r_pre], writes=[r_halo])
                    for k_ in range(4):
                        S.op("pe", lambda h: h.matmul(pp, dg[:, cc, k_, :], pre[:, k_:k_ + TB], start=(k_ == 0), stop=(k_ == 3)), reads=[r_pre, r_c], writes=[PR[pi]])
                    yield
                    S.op("act", lambda h: h.activation(out=e_[:], in_=pp, func=AF.Exp, scale=-1.0), reads=[PR[pi]], writes=[r_e])
                    yield
                    S.op("act", lambda h: h.activation(out=e_[:], in_=e_[:], func=AF.Ln, bias=self.one_col[:, 0:1], scale=1.0), reads=[r_e, self.r_const], writes=[r_e])
                    yield
                    S.op("act", lambda h: h.activation(out=e_[:], in_=e_[:], func=AF.Exp, scale=-1.0), reads=[r_e], writes=[r_e])
                    yield
                    which, hh = cc // 8, cc % 8
                    if which == 2:
                        S.op("dve", lambda h: h.tensor_tensor(out=oc[:], in0=pp, in1=e_[:], op=ALU.mult), reads=[PR[pi], r_e], writes=[r_oc])
                        store()
                        free_banks.append(pi)
                        return
                    sil, r_sil = sil_pool.get()
                    sqb, r_sqb = sqb_pool.get()
                    rinv, r_rinv = rinv_pool.get()
                    S.op("dve", lambda h: h.tensor_tensor(out=sil[:], in0=pp, in1=e_[:], op=ALU.mult), reads=[PR[pi], r_e], writes=[r_sil])
                    yield
                    S.op("dve", lambda h: h.tensor_tensor(out=sqb[:], in0=sil[:], in1=sil[:], op=ALU.mult), reads=[r_sil], writes=[r_sqb])
                    yield
                    S.op("pe", lambda h: h.matmul(pp, ones_b[:], sqb[:], start=True, stop=True), reads=[r_sqb, r_c], writes=[PR[pi]])
                    yield
                    S.op("act", lambda h: h.activation(out=rinv[:], in_=pp, func=AF.Ln, bias=self.eps_col[:, 0:1], scale=1.0), reads=[PR[pi], self.r_const], writes=[r_rinv])
                    yield
                    S.op("act", lambda h: h.activation(out=rinv[:], in_=rinv[:], func=AF.Exp, scale=-0.5), reads=[r_rinv], writes=[r_rinv])
                    yield
                    scl = HD ** -0.5 if which == 0 else 1.0
                    S.op("dve", lambda h: h.scalar_tensor_tensor(out=oc[:], in0=sil[:], scalar=scl, in1=rinv[:], op0=ALU.mult, op1=ALU.mult),
                         reads=[r_sil, r_rinv], writes=[r_oc])
                    store()
                    free_banks.append(pi)

                NFL = len(slot_banks)
                free_banks = list(slot_banks)
                pending = list(range(32))
                active = [gen_sc()]
                if b + 1 < nblk:
                    active.append(gen_norm(b + 1))
                NFL += len(active)
                rnd = 0
                while pending or active:
                    if pending and free_banks and rnd % 2 == 0:
                        active.append(gen_chunk(pending.pop(0)))
                    rnd += 1
                    for gn in list(active):
                        try:
                            next(gn)
                        except StopIteration:
                            active.remove(gn)

    def phase_A2(self):
        nc, S = self.nc, self.S
        T = self.T
        TB = 256
        NTT = TB // 128
        nblk = T // TB
        TD = self.chain_dt
        st = ExitStack()
        with st:
            def sb(name, shape, dt):
                return st.enter_context(nc.sbuf_tensor(name, shape, dt))
            wout, r_wout = self.load_w_kc(st, "wout", self.w_out, D)
            gpost, r_gpost = self.load_gain_bc(st, "gpostA", 2)
            r_c = Res("constA2")
            cst = sb("cstA2", [128, 2, 128], F32)
            S.dma("sp", lambda h: h.dma_start(out=cst[:], in_=self.consts[:, 3:5, :]), writes=[r_c])
            negmaskU = cst[:, 0, :]
            posmaskL = cst[:, 1, :]
            ident4 = sb("ident4", [128, 4, 128], F32)
            for hh in range(4):
                S.dma("sp", lambda h: h.dma_start(out=ident4[:, hh, :], in_=self.consts[:, 0, :]), writes=[r_c])
            ogain = sb("ogain", [128, 128], F32)
            S.dma("sp", lambda h: h.dma_start(out=ogain[:], in_=self.ogain_bc[:, :]), writes=[r_c])
            selall = sb("selall_sb", [8, 1024], F32)
            S.dma("sp", lambda h: h.dma_start(out=selall[:], in_=self.selall_d[:, :]), writes=[r_c])
            Sf = sb("Sf", [128, 8, 128], F32)
            r_Sf = [Res("Sf0"), Res("Sf1")]
            S.op("pool", lambda h: h.memset(Sf[:], 0.0), writes=r_Sf)
            Sb = [sb("Sb%d" % i, [128, 8, 128], BF16) for i in range(2)]
            r_Sb = [[Res("Sb%d_%d" % (i, g)) for g in range(2)] for i in range(2)]
            for i in range(2):
                S.op("pool", lambda h: h.memset(Sb[i][:], 0.0), writes=r_Sb[i])
            pools = {
                "junk": Pool_(nc, st, "junkA2", [128, D], BF16, 1),
                "stat": Pool_(nc, st, "statA2", [128, 4], F32, 8),
            }
            xt_pool = Pool_(nc, st, "xtA2", [128, D], F32, 2)
            y_pool = Pool_(nc, st, "yA2", [128, D], F32, 1)
            ss_pool = Pool_(nc, st, "ssA2", [128, 2], F32, 4)
            in_pool = Pool_(nc, st, "qkvgin", [128, 4, 8, TB], BF16, 2)
            ogT_pool = Pool_(nc, st, "ogT", [128, 8, TB], BF16, 2)
            sc_pool = Pool_(nc, st, "sc2", [128, 12, 8], F32, 4)
            gcT_pool = Pool_(nc, st, "gcT2", [8, 128], F32, 4)
            NCH = NTT * 2

            def chain_bufs(ci):
                d = {}
                def mk(nm, dt, n=1):
                    ts = [sb("%s_%d_%d" % (nm, ci, i), [128, 4, 128], dt) for i in range(n)]
                    rs = [Res("%s_%d_%d" % (nm, ci, i)) for i in range(n)]
                    d[nm] = (ts, rs)
                for nm, dt, n in (("kbg", BF16, 1), ("kg0", BF16, 1), ("kg1", BF16, 1), ("vb", BF16, 1), ("dd", F32, 1), ("eT", F32, 1),
                                  ("attnT", BF16, 1), ("egrow", F32, 1), ("qg0", BF16, 1), ("qg1", BF16, 1), ("wT0", BF16, 1), ("wT1", BF16, 1),
                                  ("Lp", TD, 2), ("Np", TD, 2), ("Pp", TD, 2), ("TT", BF16, 1), ("u", F32, 1), ("vnew", BF16, 1)):
                    mk(nm, dt, n)
                d["eL"] = d["dd"]
                for nm in ("qg0", "qg1", "wT0", "wT1", "vnew"):
                    t_ = d[nm][0][0]
                    S.op("pool", lambda h: h.memset(t_[:], 0.0), writes=[d[nm][1][0]])
                return d
            CB = [chain_bufs(ci) for ci in range(NCH)]
            tmpS_pool = Pool_(nc, st, "tmpS", [128, 4, 128], F32, 2)
            osq_pool = Pool_(nc, st, "osq", [128, 4, 128], F32, 1)
            on_pool = Pool_(nc, st, "on", [128, 4, 128], BF16, 2)
            ident_td = self.ident_b if TD == BF16 else self.ident_f
            PS, PR = self.psb, self.psr

            def v4(ap):
                return ap.rearrange("p (a b) -> p a b", a=4)

            def bview(pi):
                return PS[pi].bitcast(BF16)

            def gen_pre(ci, tt, g, it, r_it, sc, r_sc, gcT, r_gcT):
                qT, kT, vT = it[:, 0], it[:, 1], it[:, 2]
                B = CB[ci]
                b0, b1 = 2 * ci, 2 * ci + 1
                h0 = g * 4
                tc0 = tt * 128
                kbg, r_kbg = B["kbg"][0][0], B["kbg"][1][0]
                kg0, r_kg0 = B["kg0"][0][0], B["kg0"][1][0]
                kg1, r_kg1 = B["kg1"][0][0], B["kg1"][1][0]
                vb, r_vb = B["vb"][0][0], B["vb"][1][0]
                dd, r_dd = B["dd"][0][0], B["dd"][1][0]
                eT, r_eT = B["eT"][0][0], B["eT"][1][0]
                eL, r_eL = B["eL"][0][0], B["eL"][1][0]
                attnT, r_attnT = B["attnT"][0][0], B["attnT"][1][0]
                egrow, r_egrow = B["egrow"][0][0], B["egrow"][1][0]
                qg0, qg1 = B["qg0"][0][0], B["qg1"][0][0]
                r_qg0, r_qg1 = B["qg0"][1][0], B["qg1"][1][0]
                wT0, wT1 = B["wT0"][0][0], B["wT1"][0][0]
                r_wT0, r_wT1 = B["wT0"][1][0], B["wT1"][1][0]
                TT, r_TT = B["TT"][0][0], B["TT"][1][0]
                u, r_u = B["u"][0][0], B["u"][1][0]

                def bc(slot):
                    return sc[:, slot, h0:h0 + 4].unsqueeze(2).to_broadcast([128, 4, 128])
                pt = bview(b0)
                for hh in range(4):
                    S.op("pe", lambda h: h.transpose(pt[:, hh * 128:(hh + 1) * 128], kT[:, h0 + hh, tc0:tc0 + 128], self.ident_b[:]), reads=[r_it, self.r_const], writes=[PR[b0]])
                pt4 = v4(pt[:, 0:512])
                yield
                S.op("dve", lambda h: h.tensor_tensor(out=kbg[:], in0=pt4, in1=bc(5), op=ALU.mult), reads=[PR[b0], r_sc], writes=[r_kbg])
                S.op("dve", lambda h: h.tensor_tensor(out=kg0[:], in0=pt4, in1=bc(7), op=ALU.mult), reads=[PR[b0], r_sc], writes=[r_kg0])
                S.op("dve", lambda h: h.tensor_tensor(out=kg1[:], in0=pt4, in1=bc(8), op=ALU.mult), reads=[PR[b0], r_sc], writes=[r_kg1])
                pt2 = bview(b1)
                for hh in range(4):
                    S.op("pe", lambda h: h.transpose(pt2[:, hh * 128:(hh + 1) * 128], vT[:, h0 + hh, tc0:tc0 + 128], self.ident_b[:]), reads=[r_it, self.r_const], writes=[PR[b1]])
                yield
                S.op("dve", lambda h: h.tensor_tensor(out=vb[:], in0=v4(pt2[:, 0:512]), in1=bc(0), op=ALU.mult), reads=[PR[b1], r_sc], writes=[r_vb])
                for hh in range(4):
                    S.op("pe", lambda h: h.matmul(PS[b0][:, hh * 128:(hh + 1) * 128], selall[0:8, (h0 + hh) * 128:(h0 + hh + 1) * 128], gcT[:], start=True, stop=True), reads=[r_gcT, r_c], writes=[PR[b0]])
                yield
                gcb = v4(PS[b0][:])
                negU4 = negmaskU.unsqueeze(1).to_broadcast([128, 4, 128])
                posL4 = posmaskL.unsqueeze(1).to_broadcast([128, 4, 128])
                S.op("dve", lambda h: h.tensor_tensor(out=dd[:], in0=gcb, in1=bc(2), op=ALU.subtract), reads=[PR[b0], r_sc], writes=[r_dd])
                S.op("act", lambda h: h.activation(out=egrow[:], in_=gcb, func=AF.Exp), reads=[PR[b0]], writes=[r_egrow])
                for hh in range(4):
                    S.op("pe", lambda h: h.matmul(PS[b1][:, hh * 128:(hh + 1) * 128], kT[:, h0 + hh, tc0:tc0 + 128], qT[:, h0 + hh, tc0:tc0 + 128], start=True, stop=True), reads=[r_it], writes=[PR[b1]])
                yield
                S.op("dve", lambda h: h.tensor_tensor(out=eT[:], in0=dd[:], in1=negU4, op=ALU.add), reads=[r_dd, r_c], writes=[r_eT])
                S.op("dve", lambda h: h.tensor_tensor(out=eL[:], in0=dd[:], in1=posL4, op=ALU.add), reads=[r_dd, r_c], writes=[r_eL])
                S.op("dve", lambda h: h.tensor_tensor(out=qg0[:, :, 0:64], in0=qT[:, h0:h0 + 4, tc0:tc0 + 64], in1=egrow[:, :, 0:64], op=ALU.mult), reads=[r_it, r_egrow], writes=[r_qg0])
                S.op("dve", lambda h: h.tensor_tensor(out=qg1[:, :, 64:128], in0=qT[:, h0:h0 + 4, tc0 + 64:tc0 + 128], in1=egrow[:, :, 64:128], op=ALU.mult), reads=[r_it, r_egrow], writes=[r_qg1])
                yield
                S.op("act", lambda h: h.activation(out=eT[:], in_=eT[:], func=AF.Exp), reads=[r_eT], writes=[r_eT])
                S.op("act", lambda h: h.activation(out=eL[:], in_=eL[:], func=AF.Exp, scale=-1.0), reads=[r_eL], writes=[r_eL])
                for hh in range(4):
                    S.op("pe", lambda h: h.matmul(PS[b0][:, hh * 128:(hh + 1) * 128], kT[:, h0 + hh, tc0:tc0 + 128], kT[:, h0 + hh, tc0:tc0 + 128], start=True, stop=True), reads=[r_it], writes=[PR[b0]])
                yield
                S.op("dve", lambda h: h.tensor_tensor(out=attnT[:], in0=v4(PS[b1][:]), in1=eT[:], op=ALU.mult), reads=[PR[b1], r_eT], writes=[r_attnT])
                S.op("dve", lambda h: h.tensor_tensor(out=eL[:], in0=eL[:], in1=bc(0), op=ALU.mult), reads=[r_eL, r_sc], writes=[r_eL])
                yield
                Lts, Lrs = B["Lp"]
                Nts, Nrs = B["Np"]
                Pts, Prs = B["Pp"]
                li = ni = pi_ = 0
                Lp, r_L = Lts[0], Lrs[0]
                S.op("dve", lambda h: h.tensor_tensor(out=Lp[:], in0=v4(PS[b0][:]), in1=eL[:], op=ALU.mult), reads=[PR[b0], r_eL], writes=[r_L])
                yield
                Np, r_N = Nts[0], Nrs[0]
                Pp, r_P = Pts[0], Prs[0]
                pta = bview(b1) if TD == BF16 else PS[b1]
                for hh in range(4):
                    S.op("pe", lambda h: h.transpose(pta[:, hh * 128:(hh + 1) * 128], Lp[:, hh, :], ident_td[:]), reads=[r_L, self.r_const], writes=[PR[b1]])
                yield
                S.op("act", lambda h: h.copy(Np[:], v4(pta[:, 0:512])), reads=[PR[b1]], writes=[r_N])
                S.op("dve", lambda h: h.tensor_tensor(out=Pp[:], in0=ident4[:], in1=v4(pta[:, 0:512]), op=ALU.subtract), reads=[PR[b1], r_c], writes=[r_P])
                yield
                for lvl in range(1, 6):
                    li ^= 1
                    L2, r_L2 = Lts[li], Lrs[li]
                    for hh in range(4):
                        S.op("pe", lambda h: h.matmul(PS[b0][:, hh * 128:(hh + 1) * 128], Np[:, hh, :], Lp[:, hh, :], start=True, stop=True), reads=[r_N, r_L], writes=[PR[b0]])
                    if lvl < 5:
                        ni ^= 1
                        N2, r_N2 = Nts[ni], Nrs[ni]
                        for hh in range(4):
                            S.op("pe", lambda h: h.matmul(PS[b1][:, hh * 128:(hh + 1) * 128], Lp[:, hh, :], Np[:, hh, :], start=True, stop=True), reads=[r_N, r_L], writes=[PR[b1]])
                    yield
                    S.op("act", lambda h: h.copy(L2[:], v4(PS[b0][:])), reads=[PR[b0]], writes=[r_L2])
                    if lvl < 5:
                        S.op("dve", lambda h: h.tensor_copy(N2[:], v4(PS[b1][:])), reads=[PR[b1]], writes=[r_N2])
                    yield
                    for hh in range(4):
                        S.op("pe", lambda h: h.matmul(PS[b0][:, hh * 128:(hh + 1) * 128], ident_td[:], Pp[:, hh, :], start=True, stop=False), reads=[self.r_const, r_P], writes=[PR[b0]])
                        S.op("pe", lambda h: h.matmul(PS[b0][:, hh * 128:(hh + 1) * 128], L2[:, hh, :], Pp[:, hh, :], start=False, stop=True), reads=[r_L2, r_P], writes=[PR[b0]])
                    yield
                    if lvl < 5:
                        pi_ ^= 1
                        P2, r_P2 = Pts[pi_], Prs[pi_]
                        S.op("dve" if lvl % 2 else "act", (lambda h: h.tensor_copy(P2[:], v4(PS[b0][:]))) if lvl % 2 else (lambda h: h.copy(P2[:], v4(PS[b0][:]))), reads=[PR[b0]], writes=[r_P2])
                        Np, r_N = N2, r_N2
                        Pp, r_P = P2, r_P2
                    else:
                        S.op("act", lambda h: h.copy(TT[:], v4(PS[b0][:])), reads=[PR[b0]], writes=[r_TT])
                    Lp, r_L = L2, r_L2
                    yield
                for hh in range(4):
                    S.op("pe", lambda h: h.matmul(PS[b0][:, hh * 128:(hh + 1) * 128], kbg[:, hh, :], TT[:, hh, :], start=True, stop=True), reads=[r_kbg, r_TT], writes=[PR[b0]])
                for hh in range(4):
                    S.op("pe", lambda h: h.matmul(PS[b1][:, hh * 128:(hh + 1) * 128], TT[:, hh, :], vb[:, hh, :], start=True, stop=True), reads=[r_vb, r_TT], writes=[PR[b1]])
                yield
                S.op("act", lambda h: h.copy(wT0[:, :, 0:64], v4(PS[b0][:])[:, :, 0:64]), reads=[PR[b0]], writes=[r_wT0])
                S.op("act", lambda h: h.copy(wT1[:, :, 64:128], v4(PS[b0][:])[:, :, 64:128]), reads=[PR[b0]], writes=[r_wT1])
                S.op("dve", lambda h: h.tensor_copy(u[:], v4(PS[b1][:])), reads=[PR[b1]], writes=[r_u])
                yield

            spawned = []

            def gen_post(g, tt, bo, it, r_it, ogT, r_ogT):
                gateT = it[:, 3]
                h0 = g * 4
                tc0 = tt * 128
                osq, r_osq = osq_pool.get()
                on, r_on = on_pool.get()
                stt_, r_stt = pools["stat"].get()
                o4 = v4(PS[bo][:])
                if "dbg_o" in self.dump:
                    S.op("act", lambda h: h.copy(osq[:], o4), reads=[PR[bo]], writes=[r_osq])
                    r0 = self._cur_b * TB + tt * 128
                    S.dma("sp", lambda h: h.dma_start(out=self.dbg_o[r0:r0 + 128, h0 * 128:(h0 + 4) * 128], in_=osq[:].rearrange("p a b -> p (a b)")), reads=[r_osq])
                S.op("act", lambda h: h.activation(out=osq[:], in_=o4, func=AF.Square), reads=[PR[bo]], writes=[r_osq])
                S.op("dve", lambda h: h.tensor_reduce(out=stt_[:, 0:4], in_=osq[:], axis=mybir.AxisListType.X, op=ALU.add), reads=[r_osq], writes=[r_stt])
                yield
                self.rms_rstd(stt_[:, 0:4], stt_[:, 0:4], r_stt, r_stt, HD)
                rsb = stt_[:, 0:4].unsqueeze(2).to_broadcast([128, 4, 128])
                ogb = ogain[:].unsqueeze(1).to_broadcast([128, 4, 128])
                yield
                S.op("dve", lambda h: h.tensor_tensor(out=osq[:], in0=o4, in1=rsb, op=ALU.mult), reads=[PR[bo], r_stt], writes=[r_osq])
                S.op("dve", lambda h: h.tensor_tensor(out=on[:], in0=osq[:], in1=ogb, op=ALU.mult), reads=[r_osq, r_c], writes=[r_on])
                yield
                pt = bview(bo)
                for hh in range(4):
                    S.op("pe", lambda h: h.transpose(pt[:, hh * 128:(hh + 1) * 128], on[:, hh, :], self.ident_b[:]), reads=[r_on, self.r_const], writes=[PR[bo]])
                yield
                S.op("dve", lambda h: h.tensor_tensor(out=ogT[:, h0:h0 + 4, tc0:tc0 + 128], in0=v4(pt[:, 0:512]), in1=gateT[:, h0:h0 + 4, tc0:tc0 + 128], op=ALU.mult), reads=[PR[bo], r_it], writes=[r_ogT])
                yield

            def gen_rec(g, it, r_it, ogT, r_ogT):
                h0 = g * 4
                bx = 3 * g
                for tt in range(NTT):
                    bo = 3 * g + 1 + (tt % 2)
                    ci = tt * 2 + g
                    B = CB[ci]
                    u, r_u = B["u"][0][0], B["u"][1][0]
                    vnew, r_vnew = B["vnew"][0][0], B["vnew"][1][0]
                    egrow, r_egrow = B["egrow"][0][0], B["egrow"][1][0]
                    attnT, r_attnT = B["attnT"][0][0], B["attnT"][1][0]
                    for c in range(2):
                        cs = slice(c * 64, (c + 1) * 64)
                        Sb_in, r_Sb_in = Sb[c][:, h0:h0 + 4, :], r_Sb[c][g]
                        Sb_out, r_Sb_out = Sb[1 - c][:, h0:h0 + 4, :], r_Sb[1 - c][g]
                        wTc, r_wTc = (B["wT0"][0][0], B["wT0"][1][0]) if c == 0 else (B["wT1"][0][0], B["wT1"][1][0])
                        kgc, r_kgc = (B["kg0"][0][0], B["kg0"][1][0]) if c == 0 else (B["kg1"][0][0], B["kg1"][1][0])
                        for hh in range(4):
                            S.op("pe", lambda h: h.matmul(PS[bx][:, hh * 128:(hh + 1) * 128], wTc[:, hh, :], Sb_in[:, hh, :], start=True, stop=True), reads=[r_wTc, r_Sb_in], writes=[PR[bx]])
                        tmpS, r_tmpS = tmpS_pool.get()
                        col = 63 if c == 0 else 127
                        eglb = egrow[:, :, col:col + 1].to_broadcast([128, 4, 128])
                        S.op("dve", lambda h: h.tensor_tensor(out=tmpS[:], in0=Sf[:, h0:h0 + 4, :], in1=eglb, op=ALU.mult), reads=[r_Sf[g], r_egrow], writes=[r_tmpS])
                        yield
                        S.op("dve", lambda h: h.tensor_tensor(out=vnew[cs, :, :], in0=u[cs, :, :], in1=v4(PS[bx][:])[cs, :, :], op=ALU.subtract), reads=[PR[bx], r_u], writes=[r_vnew])
                        yield
                        for hh in range(4):
                            S.op("pe", lambda h: h.matmul(PS[bx][:, hh * 128:(hh + 1) * 128], kgc[:, hh, :], vnew[:, hh, :], start=True, stop=True), reads=[r_kgc, r_vnew], writes=[PR[bx]])
                        if c == 1:
                            Sb0 = Sb[0][:, h0:h0 + 4, :]
                            qg0, qg1 = B["qg0"][0][0], B["qg1"][0][0]
                            for hh in range(4):
                                oc = PS[bo][:, hh * 128:(hh + 1) * 128]
                                S.op("pe", lambda h: h.matmul(oc, qg0[:, hh, :], Sb0[:, hh, :], start=True, stop=False), reads=[B["qg0"][1][0], r_Sb[0][g]], writes=[PR[bo]])
                                S.op("pe", lambda h: h.matmul(oc, qg1[:, hh, :], Sb_in[:, hh, :], start=False, stop=False), reads=[B["qg1"][1][0], r_Sb_in], writes=[PR[bo]])
                                S.op("pe", lambda h: h.matmul(oc, attnT[:, hh, :], vnew[:, hh, :], start=False, stop=True), reads=[r_attnT, r_vnew], writes=[PR[bo]])
                        yield
                        S.op("dve", lambda h: h.tensor_tensor(out=Sf[:, h0:h0 + 4, :], in0=tmpS[:], in1=v4(PS[bx][:]), op=ALU.add), reads=[PR[bx], r_tmpS], writes=[r_Sf[g]])
                        yield
                        S.op("act", lambda h: h.copy(Sb_out, Sf[:, h0:h0 + 4, :]), reads=[r_Sf[g]], writes=[r_Sb_out])
                        yield
                    spawned.append(gen_post(g, tt, bo, it, r_it, ogT, r_ogT))

            def run_gens(gens, stagger=3):
                pend = list(gens)
                act_ = []
                rnd = 0
                while pend or act_:
                    if pend and rnd % stagger == 0:
                        act_.append(pend.pop(0))
                    rnd += 1
                    for gn in list(act_):
                        try:
                            next(gn)
                        except StopIteration:
                            act_.remove(gn)

            def gen_wout(b, ogT, r_ogT):
                for tt in range(NTT):
                    r0 = b * TB + tt * 128
                    y, r_y = y_pool.get()
                    ss, r_ss = ss_pool.get()
                    for half in range(2):
                        pi = 6 + half
                        pd = PS[pi]
                        for hh in range(8):
                            S.op("pe", lambda h: h.matmul(pd[:], ogT[:, hh, tt * 128:(tt + 1) * 128], wout[:, hh, half * 512:(half + 1) * 512], start=(hh == 0), stop=(hh == 7)),
                                 reads=[r_ogT, r_wout], writes=[PR[pi]])
                        yield
                        jk, r_jk = pools["junk"].get()
                        S.op("act", lambda h: h.activation(out=jk[:, 0:512], in_=pd[:], func=AF.Square, accum_out=ss[:, half:half + 1]),
                             reads=[PR[pi]], writes=[r_jk, r_ss])
                        yield
                        S.op("dve", lambda h: h.tensor_copy(y[:, half * 512:(half + 1) * 512], pd[:]), reads=[PR[pi]], writes=[r_y])
                        yield
                    xr, r_xr = xt_pool.get()
                    S.dma("sp", lambda h: h.dma_start(out=xr[:], in_=self.x[r0:r0 + 128, :]), writes=[r_xr])
                    st_, r_st = pools["stat"].get()
                    S.op("dve", lambda h: h.tensor_tensor(out=st_[:, 0:1], in0=ss[:, 0:1], in1=ss[:, 1:2], op=ALU.add), reads=[r_ss], writes=[r_st])
                    yield
                    S.op("act", lambda h: h.activation(out=st_[:, 1:2], in_=st_[:, 0:1], func=AF.Ln, bias=self.eps_col[:, 0:1], scale=1.0 / D), reads=[r_st, self.r_const], writes=[r_st])
                    yield
                    S.op("act", lambda h: h.activation(out=st_[:, 1:2], in_=st_[:, 1:2], func=AF.Exp, scale=-0.5), reads=[r_st], writes=[r_st])
                    yield
                    S.op("dve", lambda h: h.scalar_tensor_tensor(out=y[:], in0=y[:], scalar=st_[:, 1:2], in1=gpost[:], op0=ALU.mult, op1=ALU.mult), reads=[r_y, r_st, r_gpost], writes=[r_y])
                    yield
                    S.op("dve", lambda h: h.tensor_tensor(out=y[:], in0=y[:], in1=xr[:], op=ALU.add), reads=[r_y, r_xr], writes=[r_y])
                    yield
                    S.dma("sp", lambda h: h.dma_start(out=self.xm[r0:r0 + 128, :], in_=y[:]), reads=[r_y])

            prev_wout = None
            for b in range(nblk):
                self._cur_b = b
                it, r_it = in_pool.get()
                for i_ in range(4):
                    S.dma("sp", lambda h: h.dma_start(out=it[:, i_], in_=self.qkvg[i_, :, :, b * TB:(b + 1) * TB]), writes=[r_it])
                scs = []
                for tt in range(NTT):
                    sc, r_sc = sc_pool.get()
                    gcT, r_gcT = gcT_pool.get()
                    ti = b * NTT + tt
                    S.dma("sp", lambda h: h.dma_start(out=sc[:], in_=self.scd[ti]), writes=[r_sc])
                    S.dma("sp", lambda h: h.dma_start(out=gcT[:], in_=self.gctd[ti]), writes=[r_gcT])
                    scs.append((sc, r_sc, gcT, r_gcT))
                gens = [gen_pre(tt * 2 + g, tt, g, it, r_it, *scs[tt]) for tt in range(NTT) for g in range(2)]
                import os
                run_gens(gens, stagger=int(os.environ.get("STG", "2")))
                ogT, r_ogT = ogT_pool.get()
                rgens = [gen_rec(g, it, r_it, ogT, r_ogT) for g in range(2)]
                if prev_wout is not None:
                    rgens.append(prev_wout)
                while rgens:
                    for gn in list(rgens):
                        try:
                            next(gn)
                        except StopIteration:
                            rgens.remove(gn)
                    if spawned:
                        rgens.extend(spawned)
                        del spawned[:]
                prev_wout = gen_wout(b, ogT, r_ogT)
            run_gens([prev_wout])

    def phase_C(self):
        nc, S = self.nc, self.S
        T = self.T
        TB = 512 if T % 512 == 0 else 256
        NTT = TB // 128
        nblk = T // TB
        self._tpi = 0
        st = ExitStack()
        with st:
            wkv, r_wkv = self.load_w_kc(st, "wkv", self.w_kv, 2 * D)
            gkv, r_gkv = self.load_gain_bc(st, "gkv", 8)
            pools = {
                "junk": Pool_(nc, st, "junkC", [128, D], BF16, 1),
                "stat": Pool_(nc, st, "statC", [128, 4], F32, 8),
                "hb": Pool_(nc, st, "hbC", [128, D], BF16, 2),
                "tp_banks": [0, 1],
            }
            xt_pool = Pool_(nc, st, "xtC", [128, D], F32, 3)
            hT_pool = Pool_(nc, st, "hTC", [128, 8, TB], BF16, 2)
            kts_pool = Pool_(nc, st, "ktsC", [128, NTT, H, 128], BF16, 2)
            vs_pool = Pool_(nc, st, "vsC", [128, H, 128], BF16, 3)
            PS, PR = self.psb, self.psr
            bi = 0
            def do_norm(b):
                hT, r_hT = hT_pool.get()
                for tt in range(NTT):
                    r0 = b * TB + tt * 128
                    xt, r_xt = xt_pool.get()
                    S.dma("sp", lambda h, xt=xt, r0=r0: h.dma_start(out=xt[:], in_=self.x1[r0:r0 + 128, :]), writes=[r_xt])
                    self.norm_to_hT(xt, r_xt, gkv, r_gkv, hT, r_hT, tt * 128, pools)
                return hT, r_hT

            nxt = do_norm(0)
            for b in range(nblk):
                hT, r_hT = nxt
                kts, r_kts = kts_pool.get()
                for hh in range(H):
                    pi = 2 + (bi % 3)
                    bi += 1
                    pk = PS[pi][:, 0:TB]
                    for kc in range(8):
                        S.op("pe", lambda h, kc=kc: h.matmul(pk, wkv[:, kc, hh * 128:(hh + 1) * 128], hT[:, kc, :], start=(kc == 0), stop=(kc == 7)),
                             reads=[r_wkv.at(kc, hh * 128), r_hT], writes=[PR[pi]])
                    src = pk.rearrange("p (a b) -> p a b", a=NTT)
                    if hh % 2 == 0:
                        S.op("act", lambda h: h.copy(kts[:, :, hh, :], src), reads=[PR[pi]], writes=[r_kts])
                    else:
                        S.op("dve", lambda h: h.tensor_copy(kts[:, :, hh, :], src), reads=[PR[pi]], writes=[r_kts])
                if b + 1 < nblk:
                    nxt = do_norm(b + 1)
                kb0 = b * NTT
                S.dma("sp", lambda h: h.dma_start(out=self.KT[kb0:kb0 + NTT].rearrange("k p h s -> p k (h s)"), in_=kts[:].rearrange("p k h s -> p k (h s)")), reads=[r_kts])
                for tt in range(NTT):
                    vs, r_vs = vs_pool.get()
                    for half in range(2):
                        pi = 5 + (bi % 3)
                        bi += 1
                        pv = PS[pi]
                        for kc in range(8):
                            S.op("pe", lambda h, kc=kc: h.matmul(pv[:], hT[:, kc, tt * 128:(tt + 1) * 128], wkv[:, kc, D + half * 512:D + (half + 1) * 512], start=(kc == 0), stop=(kc == 7)),
                                 reads=[r_wkv.at(kc, D + half * 512), r_hT], writes=[PR[pi]])
                        dst = vs[:, half * 4:(half + 1) * 4, :]
                        src = pv[:].rearrange("p (a b) -> p a b", a=4)
                        if half == 0:
                            S.op("act", lambda h: h.copy(dst, src), reads=[PR[pi]], writes=[r_vs])
                        else:
                            S.op("dve", lambda h: h.tensor_copy(dst, src), reads=[PR[pi]], writes=[r_vs])
                    kb = b * NTT + tt
                    S.dma("sp", lambda h: h.dma_start(out=self.Vd[kb], in_=vs[:]), reads=[r_vs])

    def phase_D(self):
        nc, S = self.nc, self.S
        NQB = self.NQB
        self._tpi = 0
        st = ExitStack()
        with st:
            def sb(name, shape, dt):
                return st.enter_context(nc.sbuf_tensor(name, shape, dt))
            wq, r_wq = self.load_w_kc(st, "wq", self.w_q, D)
            wo, r_wo = self.load_w_kc(st, "wo", self.w_o, D)
            gpre, r_gpre = self.load_gain_bc(st, "gpreD", 1)
            gpost, r_gpost = self.load_gain_bc(st, "gpostD", 3)
            r_c = Res("constD")
            triI = sb("triI", [128, 128], BF16)
            triC = sb("triC", [128, 128], BF16)
            S.dma("pool", lambda h: h.dma_start(out=triI[:], in_=self.consts[:, 6, :]), writes=[r_c])
            S.dma("pool", lambda h: h.dma_start(out=triC[:], in_=self.consts[:, 7, :]), writes=[r_c])
            amask = sb("amask_sb", [128, 4, 128], F32)
            S.dma("sp", lambda h: h.dma_start(out=amask[:], in_=self.amask[:, :, :]), writes=[r_c])
            sel = sb("selD", [128, 4], F32)
            S.dma("sp", lambda h: h.dma_start(out=sel[:], in_=self.qsel[:, :]), writes=[r_c])
            pools = {
                "junk": Pool_(nc, st, "junkD", [128, D], BF16, 1),
                "stat": Pool_(nc, st, "statD", [128, 4], F32, 8),
                "hb": Pool_(nc, st, "hbD", [128, D], BF16, 2),
                "tp_banks": [7],
            }
            xl_pool = Pool_(nc, st, "xlD", [128, D], F32, 2)
            xq_pool = Pool_(nc, st, "xqD", [128, D], F32, 2)
            y_pool = Pool_(nc, st, "yD", [128, D], F32, 2)
            ss_pool = Pool_(nc, st, "ssD", [128, 2], F32, 4)
            hT_pool = Pool_(nc, st, "hTD", [128, 8, 128], BF16, 2)
            QT_pool = Pool_(nc, st, "QT", [128, H, 128], BF16, 2)
            kt_pool = Pool_(nc, st, "ktD", [128, H, 128], BF16, 7)
            v_pool = Pool_(nc, st, "vD", [128, H, 128], BF16, 7)
            e_pool = Pool_(nc, st, "eD", [128, 4, 128], F32, 12)
            sp_pool = Pool_(nc, st, "spD", [128, 4, 128], BF16, 12)
            eg_pool = Pool_(nc, st, "egD", [128, 4, 128], F32, 12)
            a_pool = Pool_(nc, st, "aD", [128, 4, 128], BF16, 12)
            osb_pool = Pool_(nc, st, "osbD", [128, H, 128], BF16, 2)
            oT_pool = Pool_(nc, st, "oTD", [128, H, 128], BF16, 2)
            PS, PR = self.psb, self.psr
            ZB, AB, OB = [0, 1, 6], [2, 3], [4, 5]

            def v4(ap):
                return ap.rearrange("p (a b) -> p a b", a=4)

            QTs = {}
            xqs = {}

            def gen_pro(m):
                xq, r_xq = xq_pool.get()
                xqs[m] = (xq, r_xq)
                for j in range(4):
                    xl, r_xl = xl_pool.get()
                    r0 = (4 * m + j) * 128
                    S.dma("sp", lambda h: h.dma_start(out=xl[:], in_=self.x1[r0:r0 + 128, :]), writes=[r_xl])
                    if j == 0:
                        S.op("dve", lambda h: h.tensor_scalar(out=xq[:], in0=xl[:], scalar1=sel[:, 0:1], scalar2=None, op0=ALU.mult), reads=[r_xl, r_c], writes=[r_xq])
                    else:
                        S.op("dve", lambda h: h.scalar_tensor_tensor(out=xq[:], in0=xl[:], scalar=sel[:, j:j + 1], in1=xq[:], op0=ALU.mult, op1=ALU.add), reads=[r_xl, r_c, r_xq], writes=[r_xq])
                    yield
                hT, r_hT = hT_pool.get()
                self.norm_to_hT(xq, r_xq, gpre, r_gpre, hT, r_hT, 0, pools)
                yield
                QT, r_QT = QT_pool.get()
                QTs[m] = (QT, r_QT)
                for g in range(2):
                    for hh in range(4):
                        hd = g * 4 + hh
                        for kc in range(8):
                            S.op("pe", lambda h: h.matmul(PS[7][:, hh * 128:(hh + 1) * 128], wq[:, kc, hd * 128:(hd + 1) * 128], hT[:, kc, :], start=(kc == 0), stop=(kc == 7)),
                                 reads=[r_wq, r_hT], writes=[PR[7]])
                    S.op("act", lambda h: h.activation(out=QT[:, g * 4:(g + 1) * 4, :], in_=v4(PS[7][:]), func=AF.Copy, scale=HD ** -0.5), reads=[PR[7]], writes=[r_QT])
                    yield

            def gen_epi(m, osb, r_osb):
                xq, r_xq = xqs[m]
                oT, r_oT = oT_pool.get()
                for half in range(2):
                    pt = PS[7].bitcast(BF16)
                    for j in range(4):
                        c = half * 4 + j
                        S.op("pe", lambda h: h.transpose(pt[:, j * 128:(j + 1) * 128], osb[:, c, :], self.ident_b[:]), reads=[r_osb, self.r_const], writes=[PR[7]])
                    S.op("act", lambda h: h.copy(oT[:, half * 4:half * 4 + 4, :], v4(pt[:, 0:512])), reads=[PR[7]], writes=[r_oT])
                    yield
                y, r_y = y_pool.get()
                ss, r_ss = ss_pool.get()
                for half in range(2):
                    pd = PS[7]
                    for c in range(8):
                        S.op("pe", lambda h: h.matmul(pd[:], oT[:, c, :], wo[:, c, half * 512:(half + 1) * 512], start=(c == 0), stop=(c == 7)), reads=[r_oT, r_wo], writes=[PR[7]])
                    jk, r_jk = pools["junk"].get()
                    S.op("act", lambda h: h.activation(out=jk[:, 0:512], in_=pd[:], func=AF.Square, accum_out=ss[:, half:half + 1]), reads=[PR[7]], writes=[r_jk, r_ss])
                    S.op("dve", lambda h: h.tensor_copy(y[:, half * 512:(half + 1) * 512], pd[:]), reads=[PR[7]], writes=[r_y])
                    yield
                self.post_norm_residual(y, r_y, ss, r_ss, gpost, r_gpost, xq, r_xq, pools)
                S.dma("sp", lambda h: h.dma_start(out=self.xm2[m * 128:(m + 1) * 128, :], in_=y[:]), reads=[r_y])

            def gen_side(epi, pro):
                if epi is not None:
                    yield from epi
                if pro is not None:
                    yield from pro

            for _ in gen_pro(0):
                pass
            pend_epi = None
            for m in range(NQB):
                QT, r_QT = QTs[m]
                for g in range(2):
                    S.op("dve", lambda h: h.memset(PS[OB[g]][:], 0.0), writes=[PR[OB[g]]])
                nkb = 4 * m + 4
                kv_tiles = {}

                def get_kv(kb):
                    if kb not in kv_tiles:
                        kt, r_kt = kt_pool.get()
                        vv, r_vv = v_pool.get()
                        S.dma("sp", lambda h: h.dma_start(out=kt[:], in_=self.KT[kb]), writes=[r_kt])
                        S.dma("sp", lambda h: h.dma_start(out=vv[:], in_=self.Vd[kb]), writes=[r_vv])
                        kv_tiles[kb] = (kt, r_kt, vv, r_vv)
                    return kv_tiles[kb]

                accseq = [nkb - 1, nkb - 1]

                def gen_unit(kidx, g, m=m, QT=QT, r_QT=r_QT, nkb=nkb):
                    kb = nkb - 1 - kidx
                    kt, r_kt, vv, r_vv = get_kv(kb)
                    zb = free_z.pop(0)
                    for hh in range(4):
                        hd = g * 4 + hh
                        S.op("pe", lambda h: h.matmul(PS[zb][:, hh * 128:(hh + 1) * 128], kt[:, hd, :], QT[:, hd, :], start=True, stop=True), reads=[r_kt, r_QT], writes=[PR[zb]])
                    yield
                    e_, r_e = e_pool.get()
                    S.op("act", lambda h: h.activation(out=e_[:], in_=v4(PS[zb][:]), func=AF.Exp), reads=[PR[zb]], writes=[r_e])
                    free_z.append(zb)
                    yield
                    if kb >= 4 * m:
                        mk = amask[:, kb - 4 * m, :].unsqueeze(1).to_broadcast([128, 4, 128])
                        S.op("dve", lambda h: h.tensor_tensor(out=e_[:], in0=e_[:], in1=mk, op=ALU.mult), reads=[r_e, r_c], writes=[r_e])
                        yield
                    sp, r_sp = sp_pool.get()
                    S.op("act", lambda h: h.activation(out=sp[:], in_=e_[:], func=AF.Ln, bias=self.one_col[:, 0:1], scale=1.0), reads=[r_e, self.r_const], writes=[r_sp])
                    yield
                    while accseq[g] != kb:
                        yield
                    S.op("pe", lambda h: h.matmul(PS[AB[g]][:], triI[:], sp[:].rearrange("p a b -> p (a b)"), start=(kidx == 0), stop=True, skip_group_check=True), reads=[r_sp, r_c], writes=[PR[AB[g]]])
                    yield
                    eg, r_eg = eg_pool.get()
                    S.op("act", lambda h: h.activation(out=eg[:], in_=v4(PS[AB[g]][:]), func=AF.Exp, scale=-1.0), reads=[PR[AB[g]]], writes=[r_eg])
                    S.op("pe", lambda h: h.matmul(PS[AB[g]][:], triC[:], sp[:].rearrange("p a b -> p (a b)"), start=False, stop=True, skip_group_check=True), reads=[r_sp, r_c], writes=[PR[AB[g]]])
                    accseq[g] = kb - 1
                    yield
                    a_, r_a = a_pool.get()
                    S.op("dve", lambda h: h.tensor_tensor(out=a_[:], in0=e_[:], in1=eg[:], op=ALU.mult), reads=[r_e, r_eg], writes=[r_a])
                    yield
                    for hh in range(4):
                        hd = g * 4 + hh
                        S.op("pe", lambda h: h.matmul(PS[OB[g]][:, hh * 128:(hh + 1) * 128], a_[:, hh, :], vv[:, hd, :], start=False, stop=(kidx == nkb - 1), skip_group_check=True), reads=[r_a, r_vv], writes=[PR[OB[g]]])

                free_z = list(ZB)
                pending = [(kidx, g) for kidx in range(nkb) for g in range(2)]
                active = []
                side = gen_side(pend_epi, gen_pro(m + 1) if m + 1 < NQB else None)
                side_done = False
                while pending or active or not side_done:
                    while pending and free_z and len(active) < 10:
                        active.append(gen_unit(*pending.pop(0)))
                        next(active[-1])
                    for gn in list(active):
                        try:
                            next(gn)
                        except StopIteration:
                            active.remove(gn)
                    if not side_done:
                        try:
                            next(side)
                        except StopIteration:
                            side_done = True
                osb, r_osb = osb_pool.get()
                S.op("act", lambda h: h.copy(osb[:, 0:4, :], v4(PS[OB[0]][:])), reads=[PR[OB[0]]], writes=[r_osb])
                S.op("dve", lambda h: h.tensor_copy(osb[:, 4:8, :], v4(PS[OB[1]][:])), reads=[PR[OB[1]]], writes=[r_osb])
                pend_epi = gen_epi(m, osb, r_osb)
            for _ in pend_epi:
                pass


def make_consts():
    c = np.zeros((128, 8, 128), np.float32)
    p = np.arange(128)[:, None]
    f = np.arange(128)[None, :]
    same = (p // 64) == (f // 64)
    c[:, 0, :] = np.eye(128, dtype=np.float32)
    c[:, 1, :] = (same & (p <= f)).astype(np.float32)
    c[:, 2, :] = same.astype(np.float32)
    c[:, 3, :] = np.where(same & (f >= p), 0.0, -1e5)
    c[:, 4, :] = np.where(same & (p > f), 0.0, 1e5)
    c[:, 5, :] = 1.0
    c[:, 6, :] = (p >= f).astype(np.float32)
    c[:, 7, :] = (p < f).astype(np.float32)
    return c


def make_amask(r):
    a = np.zeros((128, 4, 128), np.float32)
    s_ = np.arange(128)[:, None]
    t_ = np.arange(128)[None, :]
    for j in range(4):
        if j < r:
            a[:, j, :] = 1.0
        elif j == r:
            a[:, j, :] = (s_ < t_).astype(np.float32)
    return a


def make_qsel(r):
    q = np.zeros((128, 4), np.float32)
    q[:, r] = 1.0
    return q


def make_selall():
    sel = np.zeros((8, 1024), np.float32)
    for hh in range(8):
        sel[hh, hh * 128:(hh + 1) * 128] = 1
    return sel


_NC_CACHE = {}


def _get_nc(T, NQB):
    key = (T, NQB)
    if key not in _NC_CACHE:
        kb = KB(T, NQB, phases="ABCDE")
        kb.chain_dt = BF16
        _NC_CACHE[key] = kb.build()
    return _NC_CACHE[key]


def kernel(x, mix_pre_gain, mix_post_gain, mlp_pre_gain, mlp_post_gain, mlp_w_up, mlp_w_down,
           gdn_w_in, gdn_conv_w, gdn_a_log, gdn_dt_bias, gdn_out_gain, gdn_w_out,
           kv_gain, w_kv, sb_w_q, sb_w_o):
    f = lambda a: np.ascontiguousarray(np.asarray(a), dtype=np.float32)
    x = f(x)
    B, T, _ = x.shape
    n_cores = 8
    per_b = n_cores // B
    NQB = T // 128 // per_b
    gains = f(np.concatenate([f(mix_pre_gain), f(mix_post_gain), f(mlp_pre_gain), f(mlp_post_gain), f(kv_gain)[None]], 0))
    conv_w = f(gdn_conv_w)[0]
    common = {
        "gains": gains,
        "mlp_w_up": f(mlp_w_up),
        "mlp_w_down": f(mlp_w_down),
        "gdn_w_in": f(gdn_w_in)[0],
        "gdn_w_out": f(gdn_w_out)[0],
        "w_kv": f(w_kv),
        "sb_w_q": f(sb_w_q)[0],
        "sb_w_o": f(sb_w_o)[0],
        "consts": make_consts(),
        "convw_l": f(conv_w.reshape(4, 24, 128).transpose(2, 1, 0)),
        "small_bc": f(np.broadcast_to(np.stack([f(gdn_a_log)[0], f(gdn_dt_bias)[0]])[None], (128, 2, 8))),
        "ogain_bc": f(np.broadcast_to(f(gdn_out_gain)[0][None], (128, 128))),
        "selall": make_selall(),
    }
    in_maps = []
    for c in range(n_cores):
        b, r = c // per_b, c % per_b
        d = dict(common)
        d["x"] = x[b]
        d["qsel"] = make_qsel(r)
        d["amask"] = make_amask(r)
        in_maps.append(d)
    nc = _get_nc(T, NQB)
    res = run_bass_kernel_spmd(nc, in_maps, core_ids=list(range(n_cores)))
    out = np.zeros((B, T, D), np.float32)
    for c in range(n_cores):
        b, r = c // per_b, c % per_b
        o = res.results[c]["out"]
        for m in range(NQB):
            i = 4 * m + r
            out[b, i * 128:(i + 1) * 128, :] = o[m * 128:(m + 1) * 128, :]
    return out
```
